# Optimizing a Trainium2 kernel written in Bass

```python
import math
import jax, jax.numpy as jnp
from jax import lax
import numpy as np

D_MODEL = 1024
BATCH = 8
SEQ = 4096
DEPTH = 4

N_MIXERS = 2
D_PLE = 256
D_FF = 2816
RMS_EPS = 1e-6

SSD_EXPAND = 2
SSD_D_INNER = SSD_EXPAND * D_MODEL
SSD_HEAD_DIM = 64
SSD_HEADS = SSD_D_INNER // SSD_HEAD_DIM
SSD_GROUPS = 8
SSD_D_STATE = 128
SSD_CONV = 5
SSD_CHUNK = 128
SSD_BC = SSD_GROUPS * SSD_D_STATE
SSD_CONV_DIM = SSD_D_INNER + 4 * SSD_BC
SSD_IN_DIM = SSD_D_INNER + SSD_CONV_DIM + 2 * SSD_HEADS

GRID_W = 64
NA_HEAD_DIM = 64
NA_HEADS = D_MODEL // NA_HEAD_DIM
NA_WIN_ROWS = 8
NA_WIN_COLS = 16
NA_QBLOCK_COLS = 16
NA_KBLOCK_COLS = NA_QBLOCK_COLS + NA_WIN_COLS
NA_RPB_ROWS = 2 * NA_WIN_ROWS - 1
NA_RPB_COLS = 2 * NA_WIN_COLS - 1

N_SSD_LAYERS = (DEPTH + 1) // 2
N_NA_LAYERS = DEPTH // 2

kernel_name = "bidir_hybrid_ssd_natten_macaron"


def rmsnorm(x, g):
    xf = x.astype(jnp.float32)
    y = xf * lax.rsqrt(jnp.mean(xf * xf, axis=-1, keepdims=True) + RMS_EPS)
    return (y * g.astype(jnp.float32)).astype(x.dtype)


def swiglu(x, w_gu, w_down):
    g, u = jnp.split(x @ w_gu, 2, axis=-1)
    return (jax.nn.silu(g) * u) @ w_down


def centred_depthwise_conv(x, w, b):
    k, c = w.shape
    y = lax.conv_general_dilated(x, w[:, None, :].astype(x.dtype), window_strides=(1,),
                                 padding=((k // 2, k // 2),),
                                 dimension_numbers=('NWC', 'WIO', 'NWC'),
                                 feature_group_count=c)
    return y + b


def ssd_scan(x, dt, a, b_mat, c_mat):
    bsz, seqlen, nh, hp = x.shape
    ng, ns = b_mat.shape[-2:]
    hg = nh // ng
    q = SSD_CHUNK
    nc = seqlen // q
    xc = x.reshape(bsz, nc, q, ng, hg, hp)
    bc = b_mat.reshape(bsz, nc, q, ng, ns)
    cc = c_mat.reshape(bsz, nc, q, ng, ns)
    dtc = dt.astype(jnp.float32).reshape(bsz, nc, q, ng, hg)
    cum = jnp.cumsum(dtc * a.astype(jnp.float32).reshape(ng, hg), axis=2)
    xdt = xc * dtc[..., None].astype(x.dtype)
    lower = jnp.tril(jnp.ones((q, q), dtype=bool))[:, :, None, None]
    seg = cum[:, :, :, None] - cum[:, :, None, :]
    cb = jnp.einsum('bcqgn,bckgn->bcqkg', cc, bc)
    scores = (cb[..., None] * jnp.exp(jnp.where(lower, seg, -jnp.inf))).astype(x.dtype)
    y_diag = jnp.einsum('bcqkgj,bckgjp->bcqgjp', scores, xdt)
    decay_to_end = jnp.exp(cum[:, :, -1:] - cum)
    states = jnp.einsum('bckgn,bckgjp->bcgjpn', bc, xdt * decay_to_end[..., None].astype(x.dtype))
    chunk_decay = jnp.exp(cum[:, :, -1]).astype(x.dtype)

    def step(carry, inp):
        st, dec = inp
        return carry * dec[..., None, None] + st, carry

    _, prev = lax.scan(step, jnp.zeros_like(states[:, 0]),
                       (jnp.moveaxis(states, 1, 0), jnp.moveaxis(chunk_decay, 1, 0)))
    prev = jnp.moveaxis(prev, 0, 1)
    y_off = jnp.einsum('bcqgn,bcgjpn->bcqgjp', cc, prev) * jnp.exp(cum)[..., None].astype(x.dtype)
    return (y_diag + y_off).reshape(bsz, seqlen, nh, hp)


def ssd_mixer(h, w_in, conv_w, conv_b, dt_bias, a_log, d_skip, norm_g, w_out):
    bsz, seqlen, _ = h.shape
    proj = h @ w_in
    z, xbc, dt_raw = jnp.split(proj, [SSD_D_INNER, SSD_D_INNER + SSD_CONV_DIM], axis=-1)
    xbc = jax.nn.silu(centred_depthwise_conv(xbc, conv_w, conv_b))
    xs, b_f, c_f, b_b, c_b = jnp.split(
        xbc, [SSD_D_INNER, SSD_D_INNER + SSD_BC, SSD_D_INNER + 2 * SSD_BC, SSD_D_INNER + 3 * SSD_BC], axis=-1)
    xs = xs.reshape(bsz, seqlen, SSD_HEADS, SSD_HEAD_DIM)
    grp = lambda t: t.reshape(bsz, seqlen, SSD_GROUPS, SSD_D_STATE)
    flip = lambda t: jnp.flip(t, axis=1)
    dt = jax.nn.softplus(dt_raw.reshape(bsz, seqlen, 2, SSD_HEADS) + dt_bias)
    a = -jnp.exp(a_log.astype(jnp.float32))
    y_fwd = ssd_scan(xs, dt[:, :, 0], a[0], grp(b_f), grp(c_f))
    y_bwd = flip(ssd_scan(flip(xs), flip(dt[:, :, 1]), a[1], flip(grp(b_b)), flip(grp(c_b))))
    y = (y_fwd + y_bwd + xs * d_skip[:, None]).reshape(bsz, seqlen, SSD_D_INNER)
    y = rmsnorm(y * jax.nn.silu(z), norm_g)
    return y @ w_out


def neighbourhood_attention(h, w_qkv, q_norm, k_norm, rpb, w_out):
    bsz, seqlen, _ = h.shape
    rows = seqlen // GRID_W
    win_r = min(NA_WIN_ROWS, rows)
    n_cb = GRID_W // NA_QBLOCK_COLS
    qkv = (h @ w_qkv).reshape(bsz, rows, GRID_W, 3, NA_HEADS, NA_HEAD_DIM)
    q = rmsnorm(qkv[:, :, :, 0], q_norm) * (NA_HEAD_DIM ** -0.5)
    k = rmsnorm(qkv[:, :, :, 1], k_norm)
    v = qkv[:, :, :, 2]
    q_col = np.arange(GRID_W).reshape(n_cb, NA_QBLOCK_COLS)
    win_c0 = np.clip(q_col - NA_WIN_COLS // 2, 0, GRID_W - NA_WIN_COLS)
    kblk_c0 = np.clip(q_col[:, 0] - NA_WIN_COLS // 2, 0, GRID_W - NA_KBLOCK_COLS)
    k_col = kblk_c0[:, None] + np.arange(NA_KBLOCK_COLS)
    kc_b = k_col[:, None, :]
    col_valid = jnp.asarray((kc_b >= win_c0[..., None]) & (kc_b < win_c0[..., None] + NA_WIN_COLS))
    col_rel = jnp.asarray(np.clip(kc_b - q_col[..., None] + NA_WIN_COLS - 1, 0, NA_RPB_COLS - 1))

    def row_block(args):
        r, q_row = args
        r0 = jnp.clip(r - win_r // 2, 0, rows - win_r)
        k_blk = lax.dynamic_slice_in_dim(k, r0, win_r, axis=1)[:, :, k_col]
        v_blk = lax.dynamic_slice_in_dim(v, r0, win_r, axis=1)[:, :, k_col]
        q_blk = q_row.reshape(bsz, n_cb, NA_QBLOCK_COLS, NA_HEADS, NA_HEAD_DIM)
        s = jnp.einsum('bxqhd,brxkhd->bhxqrk', q_blk, k_blk).astype(jnp.float32)
        row_rel = r0 - r + jnp.arange(win_r) + NA_WIN_ROWS - 1
        bias = jnp.transpose(rpb[:, row_rel][:, :, col_rel], (0, 2, 3, 1, 4))
        bias = jnp.where(col_valid[:, :, None, :], bias.astype(jnp.float32), -jnp.inf)
        s = s + bias
        pr = jax.nn.softmax(s.reshape(s.shape[:4] + (-1,)), axis=-1).reshape(s.shape).astype(v.dtype)
        o = jnp.einsum('bhxqrk,brxkhd->bxqhd', pr, v_blk)
        return o.reshape(bsz, GRID_W, NA_HEADS * NA_HEAD_DIM)

    out = lax.map(row_block, (jnp.arange(rows), jnp.moveaxis(q, 1, 0)))
    out = jnp.moveaxis(out, 0, 1).reshape(bsz, seqlen, NA_HEADS * NA_HEAD_DIM)
    return out @ w_out


def setup_inputs(seed: int = 0) -> dict:
    key = jax.random.key(seed)
    keys = list(jax.random.split(key, 32))
    nxt = lambda: keys.pop()
    nrm = lambda shape, scale: jax.random.normal(nxt(), shape, jnp.float32) * scale
    gain = lambda shape: 1.0 + nrm(shape, 0.02)
    ns, nn_ = N_SSD_LAYERS, N_NA_LAYERS
    dt0 = jnp.exp(jax.random.uniform(nxt(), (ns, 2, SSD_HEADS), jnp.float32,
                                     minval=math.log(1e-3), maxval=math.log(1e-1)))
    dt_bias = dt0 + jnp.log(-jnp.expm1(-dt0))
    a_log = jnp.log(jax.random.uniform(nxt(), (ns, 2, SSD_HEADS), jnp.float32, minval=1.0, maxval=16.0))
    return {
        "x": nrm((BATCH, SEQ, D_MODEL), 1.0),
        "p": nrm((DEPTH, BATCH, SEQ, D_PLE), 1.0),
        "ffn1_norm": gain((DEPTH, D_MODEL)),
        "ffn1_w_gu": nrm((DEPTH, D_MODEL, 2 * D_FF), D_MODEL ** -0.5),
        "ffn1_w_down": nrm((DEPTH, D_FF, D_MODEL), D_FF ** -0.5),
        "mix_norm": gain((DEPTH, D_MODEL)),
        "ffn2_norm": gain((DEPTH, D_MODEL)),
        "ffn2_w_gu": nrm((DEPTH, D_MODEL, 2 * D_FF), D_MODEL ** -0.5),
        "ffn2_w_down": nrm((DEPTH, D_FF, D_MODEL), D_FF ** -0.5),
        "ple_norm": gain((DEPTH, D_MODEL)),
        "ple_w_gate": nrm((DEPTH, D_MODEL, D_MODEL), D_MODEL ** -0.5),
        "ple_w_proj": nrm((DEPTH, D_PLE, D_MODEL), D_PLE ** -0.5),
        "ple_post_norm": gain((DEPTH, D_MODEL)),
        "ssd_w_in": nrm((ns, D_MODEL, SSD_IN_DIM), D_MODEL ** -0.5),
        "ssd_conv_w": nrm((ns, SSD_CONV, SSD_CONV_DIM), SSD_CONV ** -0.5),
        "ssd_conv_b": nrm((ns, SSD_CONV_DIM), 0.02),
        "ssd_dt_bias": dt_bias,
        "ssd_a_log": a_log,
        "ssd_d": gain((ns, SSD_HEADS)),
        "ssd_norm": gain((ns, SSD_D_INNER)),
        "ssd_w_out": nrm((ns, SSD_D_INNER, D_MODEL), SSD_D_INNER ** -0.5),
        "na_w_qkv": nrm((nn_, D_MODEL, 3 * NA_HEADS * NA_HEAD_DIM), D_MODEL ** -0.5),
        "na_q_norm": gain((nn_, NA_HEAD_DIM)),
        "na_k_norm": gain((nn_, NA_HEAD_DIM)),
        "na_rpb": nrm((nn_, NA_HEADS, NA_RPB_ROWS, NA_RPB_COLS), 0.02),
        "na_w_out": nrm((nn_, NA_HEADS * NA_HEAD_DIM, D_MODEL), (NA_HEADS * NA_HEAD_DIM) ** -0.5),
    }


def reference(x, p, ffn1_norm, ffn1_w_gu, ffn1_w_down, mix_norm, ffn2_norm, ffn2_w_gu, ffn2_w_down,
              ple_norm, ple_w_gate, ple_w_proj, ple_post_norm,
              ssd_w_in, ssd_conv_w, ssd_conv_b, ssd_dt_bias, ssd_a_log, ssd_d, ssd_norm, ssd_w_out,
              na_w_qkv, na_q_norm, na_k_norm, na_rpb, na_w_out):
    h = x
    for i in range(DEPTH):
        h = h + 0.5 * swiglu(rmsnorm(h, ffn1_norm[i]), ffn1_w_gu[i], ffn1_w_down[i])
        hn = rmsnorm(h, mix_norm[i])
        j = i // N_MIXERS
        if i % N_MIXERS == 0:
            h = h + ssd_mixer(hn, ssd_w_in[j], ssd_conv_w[j], ssd_conv_b[j], ssd_dt_bias[j],
                              ssd_a_log[j], ssd_d[j], ssd_norm[j], ssd_w_out[j])
        else:
            h = h + neighbourhood_attention(hn, na_w_qkv[j], na_q_norm[j], na_k_norm[j],
                                            na_rpb[j], na_w_out[j])
        h = h + 0.5 * swiglu(rmsnorm(h, ffn2_norm[i]), ffn2_w_gu[i], ffn2_w_down[i])
        gate = jax.nn.sigmoid(rmsnorm(h, ple_norm[i]) @ ple_w_gate[i])
        h = h + gate * rmsnorm(p[i] @ ple_w_proj[i], ple_post_norm[i])
    return h
```

```python
import contextlib
import os
import numpy as np
import concourse.bass as bass
import concourse.mybir as mybir
from concourse.bass_utils import run_bass_kernel_spmd

F32 = mybir.dt.float32
BF16 = mybir.dt.bfloat16
AF = mybir.ActivationFunctionType
ALU = mybir.AluOpType
AX = mybir.AxisListType

D = 1024
SEQ = 4096
DEPTH = 4
DFF = 2816
NFF = DFF // 128
DPLE = 256
EPS = 1e-6
TH = 2048
NTT = TH // 512
KD = D // 128


class Buf:
    __slots__ = ("ap", "writers", "readers", "name")

    def __init__(self, ap, name=""):
        self.ap = ap
        self.writers = []
        self.readers = []
        self.name = name


class Op:
    __slots__ = ("eng", "fn", "seq", "deps", "ddeps", "dsem", "dcount", "target", "rank")

    def __init__(self, eng, fn):
        self.eng = eng
        self.fn = fn
        self.deps = {}
        self.ddeps = {}
        self.dsem = None
        self.dcount = 0
        self.target = False
        self.rank = 0


def I(name, *args, **kwargs):
    return lambda e: getattr(e, name)(*args, **kwargs)


COMPUTE = ("pe", "act", "dve", "pool")
ENGS = ("pe", "act", "dve", "pool", "sp")
SAME_ENG_WINDOW = int(os.environ.get("SEW", "2"))
SEM_SEG = 30000


class Sched:
    def __init__(self, nc, stack):
        self.nc = nc
        self.stack = stack
        self.ops = {e: [] for e in ENGS}
        self.dma_sems = []
        self.dma_counts = []
        self.n_ops = 0

    def new_dsem(self, name):
        if not hasattr(self, "_dsem_names"):
            self._dsem_names = {}
        if name in self._dsem_names:
            return self._dsem_names[name]
        self._dsem_names[name] = len(self.dma_sems)
        s = self.stack.enter_context(self.nc.semaphore(name))
        self.dma_sems.append(s)
        self.dma_counts.append(0)
        return len(self.dma_sems) - 1

    def _dep_on(self, op, ref):
        e, seq, dsem = ref
        if dsem is not None:
            op.ddeps[dsem] = self.dma_counts[dsem]
        else:
            if op.deps.get(e, -1) < seq:
                op.deps[e] = seq

    def emit(self, eng, fn, reads=(), writes=(), dsem=None):
        op = Op(eng, fn)
        op.seq = len(self.ops[eng])
        for b in reads:
            for r in b.writers:
                self._dep_on(op, r)
        for b in writes:
            for r in b.readers:
                self._dep_on(op, r)
            for r in b.writers:
                self._dep_on(op, r)
        if dsem is not None:
            self.dma_counts[dsem] += 1
            op.dsem = dsem
            op.dcount = self.dma_counts[dsem]
        ref = (eng, op.seq, dsem)
        wset = set(id(b) for b in writes)
        for b in writes:
            if b.readers:
                b.writers = [ref]
                b.readers = []
            else:
                b.writers.append(ref)
                if len(b.writers) > 64:
                    b.writers = self._prune(b.writers)
        for b in reads:
            if id(b) not in wset:
                b.readers.append(ref)
                if len(b.readers) > 64:
                    b.readers = self._prune(b.readers)
        self.ops[eng].append(op)
        self.n_ops += 1
        return op

    @staticmethod
    def _prune(refs):
        best = {}
        out = []
        for (e, seq, dsem) in refs:
            if dsem is not None:
                k = ("d", dsem)
                best[k] = (e, seq, dsem)
            else:
                k = ("c", e)
                if k not in best or best[k][1] < seq:
                    best[k] = (e, seq, dsem)
        return list(best.values())

    def barrier(self):
        last = {}
        for e in COMPUTE:
            for i in range(len(self.ops[e]) - 1, -1, -1):
                if self.ops[e][i].fn is not None and self.ops[e][i].dsem is None:
                    last[e] = i
                    break
        dcounts = list(self.dma_counts)
        self._pending_barrier = (last, dcounts)
        for e in ENGS:
            op = Op(e, None)
            op.seq = len(self.ops[e])
            for e2, s in last.items():
                if e2 != e:
                    op.deps[e2] = s
            for i, c in enumerate(dcounts):
                if c > 0:
                    op.ddeps[i] = c
            self.ops[e].append(op)

    def replay(self):
        nc = self.nc
        for e in ENGS:
            for op in self.ops[e]:
                for e2, s in op.deps.items():
                    if e2 == e:
                        if e == "pe" or e == "sp":
                            continue
                        if op.seq - s > SAME_ENG_WINDOW:
                            continue
                    self.ops[e2][s].target = True
        esems = {}
        for e in COMPUTE:
            n = 0
            for op in self.ops[e]:
                if op.target:
                    n += 1
                op.rank = n
            nseg = n // SEM_SEG + 1
            esems[e] = [self.stack.enter_context(nc.semaphore(f"es_{e}_{i}")) for i in range(nseg)]
        self.esems = esems
        engobj = {"pe": "tensor", "act": "scalar", "dve": "vector", "pool": "gpsimd", "sp": "sync"}
        sched = self

        def run(e, eng):
            waited = {}
            for op in sched.ops[e]:
                for e2, s in op.deps.items():
                    if e2 == e:
                        if e == "pe" or e == "sp":
                            continue
                        if op.seq - s > SAME_ENG_WINDOW:
                            continue
                    r = sched.ops[e2][s].rank
                    seg = (r - 1) // SEM_SEG
                    val = r - seg * SEM_SEG
                    key = (e2, seg)
                    if waited.get(key, 0) >= val:
                        continue
                    waited[key] = val
                    eng.wait_ge(esems[e2][seg], val)
                for di, c in op.ddeps.items():
                    key = ("d", di)
                    if waited.get(key, 0) >= c:
                        continue
                    waited[key] = c
                    eng.wait_ge(sched.dma_sems[di], 16 * c)
                if op.fn is None:
                    continue
                ins = op.fn(eng)
                if op.dsem is not None:
                    ins.then_inc(sched.dma_sems[op.dsem], 16)
                elif op.target:
                    seg = (op.rank - 1) // SEM_SEG
                    ins.then_inc(esems[e][seg], 1)

        with nc.Block() as block:
            @block.tensor
            def _(eng):
                run("pe", eng)

            @block.scalar
            def _(eng):
                run("act", eng)

            @block.vector
            def _(eng):
                run("dve", eng)

            @block.gpsimd
            def _(eng):
                run("pool", eng)

            @block.sync
            def _(eng):
                run("sp", eng)


SBUF_BASE = 16384
SBUF_TOP = 229376


class Arena:
    def __init__(self, top):
        self.top = top


class Ctx:
    def __init__(self, nc, stack):
        self.nc = nc
        self.stack = stack
        self.s = Sched(nc, stack)
        self.ps_t = stack.enter_context(nc.psum_tensor("ps_all", [128, 8 * 512], F32))
        self.psum = [Buf(self.ps_t[:, i * 512:(i + 1) * 512], f"ps{i}") for i in range(8)]
        self._n = 0

    def sb(self, stack, shape, dt, name=None):
        self._n += 1
        nbytes = int(np.prod(shape[1:])) * (2 if dt == BF16 else 4)
        nbytes = (nbytes + 63) // 64 * 64
        off = stack.top
        stack.top += nbytes
        assert stack.top <= SBUF_TOP, f"SBUF arena overflow {stack.top}"
        return self.nc.alloc_sbuf_tensor_at(f"{name or 't'}_{self._n}", list(shape), dt, offset=off)

    def dram(self, name, shape, dt, kind="Internal"):
        return self.nc.dram_tensor(name, list(shape), dt, kind=kind)


def rmsnorm_T(cx, st, hres_b, hres_t, gain_ap, xn_t, xn_b, rstd_t, rstd_b, sq_t, sq_b, ones_t, ones_b, ps_ids):
    s = cx.s
    for tt in range(NTT):
        ps = cx.psum[ps_ids[tt % len(ps_ids)]]
        tsl = slice(tt * 512, (tt + 1) * 512)
        for d in range(KD):
            sq = sq_b[(tt * KD + d) % len(sq_b)]
            sqt = sq_t[(tt * KD + d) % len(sq_b)]
            s.emit("act", I("activation", out=sqt[:, :], in_=hres_t[:, d, tsl], func=AF.Square),
                   reads=[hres_b[d][tt]], writes=[sq])
            s.emit("pe", I("matmul", ps.ap[:, :], lhsT=ones_t[:, :], rhs=sqt[:, :], start=(d == 0), stop=(d == KD - 1)),
                   reads=[sq, ones_b], writes=[ps])
        s.emit("act", I("activation", out=rstd_t[:, tsl], in_=ps.ap[:, :], func=AF.Sqrt, bias=cx.eps_t[:, 0:1], scale=1.0),
               reads=[ps], writes=[rstd_b[tt]])
        s.emit("dve", I("reciprocal", out=rstd_t[:, tsl], in_=rstd_t[:, tsl]),
               reads=[rstd_b[tt]], writes=[rstd_b[tt]])
        for d in range(KD):
            s.emit("dve", I("scalar_tensor_tensor", out=xn_t[:, d, tsl], in0=hres_t[:, d, tsl], scalar=gain_ap[:, d:d + 1], in1=rstd_t[:, tsl], op0=ALU.mult, op1=ALU.mult),
                   reads=[hres_b[d][tt], rstd_b[tt]], writes=[xn_b[d][tt]])


def build_chain_phase(cx, src_dram, hT_dram, half_list, sublayers, W):
    nc = cx.nc
    s = cx.s
    if True:
        st = Arena(cx.const_top)
        hres_t = cx.sb(st, [128, KD, TH], F32, "hres")
        xn_t = cx.sb(st, [128, KD, TH], BF16, "xn")
        rstd_t = cx.sb(st, [128, TH], F32, "rstd")
        NSQ = 2
        sq_t = [cx.sb(st, [128, 512], BF16, "sq") for _ in range(NSQ)]
        G = 2
        act_t = [cx.sb(st, [128, G, TH], BF16, "act") for _ in range(2)]
        wgu_t = [cx.sb(st, [128, KD, 2, G * 128], BF16, "wgu") for _ in range(2)]
        wd_t = [cx.sb(st, [128, G, D], BF16, "wd") for _ in range(2)]
        sg_t = [cx.sb(st, [128, 512], F32, "sg") for _ in range(2)]
        wgate_t = cx.sb(st, [128, KD, D], BF16, "wgate")
        wproj_t = cx.sb(st, [128, 2, D], BF16, "wproj")
        pT_t = cx.sb(st, [128, 2, TH], BF16, "pT")
        proj_t = cx.sb(st, [128, KD, 512], F32, "proj")
        gate_t = [cx.sb(st, [128, 512], F32, "gate") for _ in range(2)]
        tmp_t = [cx.sb(st, [128, 512], F32, "tmp") for _ in range(2)]

        hres_b = [[Buf(None, f"hres{d}_{tt}") for tt in range(NTT)] for d in range(KD)]
        xn_b = [[Buf(None) for tt in range(NTT)] for d in range(KD)]
        rstd_b = [Buf(None) for tt in range(NTT)]
        sq_b = [Buf(None) for _ in range(NSQ)]
        act_b = [[[Buf(None) for tt in range(NTT)] for g in range(G)] for _ in range(2)]
        wgu_b = [Buf(None) for _ in range(2)]
        wd_b = [Buf(None) for _ in range(2)]
        sg_b = [Buf(None) for _ in range(2)]
        wgate_b = Buf(None)
        wproj_b = Buf(None)
        pT_b = Buf(None)
        proj_b = [Buf(None) for d in range(KD)]
        gate_b = [Buf(None) for _ in range(2)]
        tmp_b = [Buf(None) for _ in range(2)]
        ones_t, ones_b = W["ones_t"], W["ones_b"]
        gains_t, gains_b = W["gains_t"], W["gains_b"]

        ds_h = [s.new_dsem(f"dh{d}") for d in range(2)]
        ds_wgu = [s.new_dsem(f"dwgu{i}") for i in range(2)]
        ds_wd = [s.new_dsem(f"dwd{i}") for i in range(2)]
        ds_misc = s.new_dsem("dmisc")
        ds_st = s.new_dsem("dst")
        hT_b = W["hT_b"]

        for half in half_list:
            t0 = half * TH
            for d in range(KD):
                s.emit("sp", I("dma_start", out=hres_t[:, d, :], in_=src_dram[d * 128:(d + 1) * 128, t0:t0 + TH]),
                       reads=[hT_b[(d, half)]], writes=hres_b[d], dsem=ds_h[d % 2])
            stored = [False]

            def store_hres():
                if stored[0]:
                    return
                stored[0] = True
                for d in range(KD):
                    s.emit("sp", I("dma_start", out=hT_dram[d * 128:(d + 1) * 128, t0:t0 + TH], in_=hres_t[:, d, :]),
                           reads=hres_b[d], writes=[hT_b[(d, half)]], dsem=ds_st)

            for si, sub in enumerate(sublayers):
                if sub[0] == "normout":
                    _, layer, dst = sub
                    if si == len(sublayers) - 1:
                        store_hres()
                    gi = W["gain_idx"][("mix_norm", layer)]
                    rmsnorm_T(cx, st, hres_b, hres_t, gains_t[:, gi, :], xn_t, xn_b, rstd_t, rstd_b, sq_t, sq_b, ones_t, ones_b, [6, 7])
                    for d in range(KD):
                        s.emit("sp", I("dma_start", out=dst[d * 128:(d + 1) * 128, t0:t0 + TH], in_=xn_t[:, d, :]),
                               reads=xn_b[d], writes=[W["scr_b"]], dsem=ds_st)
                elif sub[0] == "proj":
                    _, srcT, w_ap, k0 = sub
                    w_v = w_ap.rearrange("(kc p) c -> p kc c", p=128)
                    s.emit("pool", I("dma_start", out=wgate_t[:, :, :], in_=w_v[:, k0:k0 + KD, :]), writes=[wgate_b], dsem=ds_misc)
                    for d in range(KD):
                        s.emit("sp", I("dma_start", out=xn_t[:, d, :], in_=srcT[(k0 + d) * 128:(k0 + d + 1) * 128, t0:t0 + TH]),
                               reads=[W["scr_b"]], writes=xn_b[d], dsem=ds_h[d % 2])
                    for d in range(KD):
                        for tt in range(NTT):
                            tsl = slice(tt * 512, (tt + 1) * 512)
                            pa = cx.psum[4 + (d * NTT + tt) % 2]
                            for k in range(KD):
                                s.emit("pe", I("matmul", pa.ap[:, :], lhsT=wgate_t[:, k, d * 128:(d + 1) * 128], rhs=xn_t[:, k, tsl], start=(k == 0), stop=(k == KD - 1)),
                                       reads=[wgate_b, xn_b[k][tt]], writes=[pa])
                            s.emit("dve", I("tensor_tensor", out=hres_t[:, d, tsl], in0=hres_t[:, d, tsl], in1=pa.ap[:, :], op=ALU.add),
                                   reads=[pa, hres_b[d][tt]], writes=[hres_b[d][tt]])
                elif sub[0] == "proj_ssd":
                    _, srcT, w_ap, ng_t = sub
                    w_v = w_ap.rearrange("(kc p) c -> p kc c", p=128)
                    xn16 = xn_t[:, :, :].rearrange("p a (b t) -> p (a b) t", b=2)
                    src_v = srcT.rearrange("(kc p) t -> p kc t", p=128)
                    for qtr in range(2):
                        q0 = t0 + qtr * 1024
                        allx = [xn_b[kc // 2][(kc % 2) * 2 + t2] for kc in range(16) for t2 in range(2)]
                        for kh in range(2):
                            s.emit("sp", I("dma_start", out=xn16[:, kh * 8:(kh + 1) * 8, :], in_=src_v[:, kh * 8:(kh + 1) * 8, q0:q0 + 1024]),
                                   reads=[W["scr_b"]], writes=allx, dsem=ds_h[kh])
                        for t2 in range(2):
                            tt = qtr * 2 + t2
                            lsl = slice(t2 * 512, (t2 + 1) * 512)
                            tsl = slice(tt * 512, (tt + 1) * 512)
                            pss = cx.psum[6 + t2]
                            for kc in range(16):
                                xb = xn_b[kc // 2][(kc % 2) * 2 + t2]
                                sq = sq_b[kc % NSQ]
                                sqt = sq_t[kc % NSQ]
                                s.emit("act", I("activation", out=sqt[:, :], in_=xn16[:, kc, lsl], func=AF.Square), reads=[xb], writes=[sq])
                                s.emit("pe", I("matmul", pss.ap[:, :], lhsT=W["ones2k_t"][:, :], rhs=sqt[:, :], start=(kc == 0), stop=(kc == 15)), reads=[sq, ones_b], writes=[pss])
                            s.emit("act", I("activation", out=rstd_t[:, tsl], in_=pss.ap[:, :], func=AF.Sqrt, bias=cx.eps_t[:, 0:1], scale=1.0), reads=[pss], writes=[rstd_b[tt]])
                            s.emit("dve", I("reciprocal", out=rstd_t[:, tsl], in_=rstd_t[:, tsl]), reads=[rstd_b[tt]], writes=[rstd_b[tt]])
                            for kc in range(16):
                                xb = xn_b[kc // 2][(kc % 2) * 2 + t2]
                                s.emit("dve", I("scalar_tensor_tensor", out=xn16[:, kc, lsl], in0=xn16[:, kc, lsl], scalar=ng_t[:, kc:kc + 1], in1=rstd_t[:, tsl], op0=ALU.mult, op1=ALU.mult),
                                       reads=[xb, rstd_b[tt], W["ng_b"]], writes=[xb])
                        for kh in range(2):
                            s.emit("pool", I("dma_start", out=wgate_t[:, :, :], in_=w_v[:, kh * 8:(kh + 1) * 8, :]), writes=[wgate_b], dsem=ds_misc)
                            for d in range(KD):
                                for t2 in range(2):
                                    tt = qtr * 2 + t2
                                    lsl = slice(t2 * 512, (t2 + 1) * 512)
                                    tsl = slice(tt * 512, (tt + 1) * 512)
                                    pa = cx.psum[4 + (d * 2 + t2) % 2]
                                    for k in range(KD):
                                        kc = kh * 8 + k
                                        xb = xn_b[kc // 2][(kc % 2) * 2 + t2]
                                        s.emit("pe", I("matmul", pa.ap[:, :], lhsT=wgate_t[:, k, d * 128:(d + 1) * 128], rhs=xn16[:, kc, lsl], start=(k == 0), stop=(k == KD - 1)),
                                               reads=[wgate_b, xb], writes=[pa])
                                    s.emit("dve", I("tensor_tensor", out=hres_t[:, d, tsl], in0=hres_t[:, d, tsl], in1=pa.ap[:, :], op=ALU.add),
                                           reads=[pa, hres_b[d][tt]], writes=[hres_b[d][tt]])
                elif sub[0] == "ffn":
                    _, layer, which = sub
                    w_gu = W[f"ffn{which}_w_gu"][layer]
                    w_dn = W[f"ffn{which}_w_down"][layer]
                    gi = W["gain_idx"][(f"ffn{which}_norm", layer)]
                    rmsnorm_T(cx, st, hres_b, hres_t, gains_t[:, gi, :], xn_t, xn_b, rstd_t, rstd_b, sq_t, sq_b, ones_t, ones_b, [6, 7])
                    ngrp = NFF // G
                    w_gu_v = w_gu.rearrange("(kc p) c -> p kc c", p=128)
                    w_dn_v = w_dn.rearrange("(c p) d -> p c d", p=128)

                    def phaseA(gi_, units):
                        bi = gi_ % 2
                        c0 = gi_ * G * 128
                        s.emit("pool", I("dma_start", out=wgu_t[bi][:, :, 0, :], in_=w_gu_v[:, :, c0:c0 + G * 128]),
                               writes=[wgu_b[bi]], dsem=ds_wgu[bi])
                        s.emit("pool", I("dma_start", out=wgu_t[bi][:, :, 1, :], in_=w_gu_v[:, :, DFF + c0:DFF + c0 + G * 128]),
                               writes=[wgu_b[bi]], dsem=ds_wgu[bi])
                        s.emit("pool", I("dma_start", out=wd_t[bi][:, :, :], in_=w_dn_v[:, gi_ * G:(gi_ + 1) * G, :]),
                               writes=[wd_b[bi]], dsem=ds_wd[bi])
                        for g in range(G):
                            for tt in range(NTT):
                                units.append((gi_, g, tt))

                    def unitA(gi_, g, tt, bq=()):
                        bi = gi_ % 2
                        if True:
                            if True:
                                tsl = slice(tt * 512, (tt + 1) * 512)
                                pg = cx.psum[(g * NTT + tt) % 2]
                                pu = cx.psum[2 + (g * NTT + tt) % 2]
                                for k in range(KD):
                                    s.emit("pe", I("matmul", pg.ap[:, :], lhsT=wgu_t[bi][:, k, 0, g * 128:(g + 1) * 128], rhs=xn_t[:, k, tsl], start=(k == 0), stop=(k == KD - 1)),
                                           reads=[wgu_b[bi], xn_b[k][tt]], writes=[pg])
                                    if k % 4 == 3 and bq:
                                        unitB(*bq.pop(0))
                                for k in range(KD):
                                    s.emit("pe", I("matmul", pu.ap[:, :], lhsT=wgu_t[bi][:, k, 1, g * 128:(g + 1) * 128], rhs=xn_t[:, k, tsl], start=(k == 0), stop=(k == KD - 1)),
                                           reads=[wgu_b[bi], xn_b[k][tt]], writes=[pu])
                                    if k % 4 == 3 and bq:
                                        unitB(*bq.pop(0))
                                sgi = (g * NTT + tt) % 2
                                s.emit("act", I("activation", out=sg_t[sgi][:, :], in_=pg.ap[:, :], func=AF.Silu),
                                       reads=[pg], writes=[sg_b[sgi]])
                                s.emit("dve", I("tensor_tensor", out=act_t[bi][:, g, tsl], in0=sg_t[sgi][:, :], in1=pu.ap[:, :], op=ALU.mult),
                                       reads=[pu, sg_b[sgi]], writes=[act_b[bi][g][tt]])

                    def unitB(gi_, d, tt):
                        bi = gi_ % 2
                        if True:
                            if True:
                                tsl = slice(tt * 512, (tt + 1) * 512)
                                pa = cx.psum[4 + (d * NTT + tt) % 4]
                                for g in range(G):
                                    s.emit("pe", I("matmul", pa.ap[:, :], lhsT=wd_t[bi][:, g, d * 128:(d + 1) * 128], rhs=act_t[bi][:, g, tsl], start=(g == 0), stop=(g == G - 1)),
                                           reads=[wd_b[bi], act_b[bi][g][tt]], writes=[pa])
                                s.emit("dve", I("scalar_tensor_tensor", out=hres_t[:, d, tsl], in0=pa.ap[:, :], scalar=0.5, in1=hres_t[:, d, tsl], op0=ALU.mult, op1=ALU.add),
                                       reads=[pa, hres_b[d][tt]], writes=[hres_b[d][tt]])

                    ua = []
                    phaseA(0, ua)
                    for u in ua:
                        unitA(*u)
                    for gi_ in range(1, ngrp + 1):
                        ua = []
                        if gi_ < ngrp:
                            phaseA(gi_, ua)
                        ub = [(gi_ - 1, d, tt) for d in range(KD) for tt in range(NTT)]
                        for u in ua:
                            unitA(*u, bq=ub)
                        while ub:
                            unitB(*ub.pop(0))
                else:
                    _, layer = sub
                    gi = W["gain_idx"][("ple_norm", layer)]
                    gpi = W["gain_idx"][("ple_post_norm", layer)]
                    rmsnorm_T(cx, st, hres_b, hres_t, gains_t[:, gi, :], xn_t, xn_b, rstd_t, rstd_b, sq_t, sq_b, ones_t, ones_b, [6, 7])
                    wg_v = W["ple_w_gate"][layer].rearrange("(kc p) c -> p kc c", p=128)
                    wp_v = W["ple_w_proj"][layer].rearrange("(kc p) c -> p kc c", p=128)
                    pT_v = W["pT"][layer].rearrange("(kc p) t -> p kc t", p=128)
                    s.emit("pool", I("dma_start", out=wgate_t[:, :, :], in_=wg_v), writes=[wgate_b], dsem=ds_misc)
                    s.emit("pool", I("dma_start", out=wproj_t[:, :, :], in_=wp_v), writes=[wproj_b], dsem=ds_misc)
                    s.emit("pool", I("dma_start", out=pT_t[:, :, :], in_=pT_v[:, :, t0:t0 + TH]), writes=[pT_b], dsem=ds_misc)
                    for tt in range(NTT):
                        tsl = slice(tt * 512, (tt + 1) * 512)
                        pss = cx.psum[6 + tt % 2]
                        for d in range(KD):
                            pp = cx.psum[d % 2]
                            for k in range(2):
                                s.emit("pe", I("matmul", pp.ap[:, :], lhsT=wproj_t[:, k, d * 128:(d + 1) * 128], rhs=pT_t[:, k, tsl], start=(k == 0), stop=(k == 1)),
                                       reads=[wproj_b, pT_b], writes=[pp])
                            s.emit("act", I("activation", out=proj_t[:, d, :], in_=pp.ap[:, :], func=AF.Copy),
                                   reads=[pp], writes=[proj_b[d]])
                            sq = sq_b[d % NSQ]
                            sqt = sq_t[d % NSQ]
                            s.emit("act", I("activation", out=sqt[:, :], in_=proj_t[:, d, :], func=AF.Square),
                                   reads=[proj_b[d]], writes=[sq])
                            s.emit("pe", I("matmul", pss.ap[:, :], lhsT=ones_t[:, :], rhs=sqt[:, :], start=(d == 0), stop=(d == KD - 1)),
                                   reads=[sq, ones_b], writes=[pss])
                        s.emit("act", I("activation", out=rstd_t[:, tsl], in_=pss.ap[:, :], func=AF.Sqrt, bias=cx.eps_t[:, 0:1], scale=1.0),
                               reads=[pss], writes=[rstd_b[tt]])
                        s.emit("dve", I("reciprocal", out=rstd_t[:, tsl], in_=rstd_t[:, tsl]),
                               reads=[rstd_b[tt]], writes=[rstd_b[tt]])
                        for d in range(KD):
                            pgt = cx.psum[2 + d % 2]
                            for k in range(KD):
                                s.emit("pe", I("matmul", pgt.ap[:, :], lhsT=wgate_t[:, k, d * 128:(d + 1) * 128], rhs=xn_t[:, k, tsl], start=(k == 0), stop=(k == KD - 1)),
                                       reads=[wgate_b, xn_b[k][tt]], writes=[pgt])
                            gb = d % 2
                            s.emit("act", I("activation", out=gate_t[gb][:, :], in_=pgt.ap[:, :], func=AF.Sigmoid),
                                   reads=[pgt], writes=[gate_b[gb]])
                            s.emit("dve", I("scalar_tensor_tensor", out=tmp_t[gb][:, :], in0=proj_t[:, d, :], scalar=gains_t[:, gpi, d:d + 1], in1=rstd_t[:, tsl], op0=ALU.mult, op1=ALU.mult),
                                   reads=[proj_b[d], rstd_b[tt]], writes=[tmp_b[gb]])
                            s.emit("dve", I("tensor_tensor", out=tmp_t[gb][:, :], in0=tmp_t[gb][:, :], in1=gate_t[gb][:, :], op=ALU.mult),
                                   reads=[tmp_b[gb], gate_b[gb]], writes=[tmp_b[gb]])
                            s.emit("dve", I("tensor_tensor", out=hres_t[:, d, tsl], in0=hres_t[:, d, tsl], in1=tmp_t[gb][:, :], op=ALU.add),
                                   reads=[tmp_b[gb], hres_b[d][tt]], writes=[hres_b[d][tt]])
            store_hres()
        s.barrier()


GAIN_NAMES = ["ffn1_norm", "mix_norm", "ffn2_norm", "ple_norm", "ple_post_norm"]
WEIGHT_SHAPES = (("ffn1_w_gu", [D, 2 * DFF], DEPTH), ("ffn1_w_down", [DFF, D], DEPTH), ("ffn2_w_gu", [D, 2 * DFF], DEPTH),
                 ("ffn2_w_down", [DFF, D], DEPTH), ("ple_w_gate", [D, D], DEPTH), ("ple_w_proj", [DPLE, D], DEPTH),
                 ("na_w_qkv", [D, 3 * D], 2), ("na_w_out", [D, D], 2), ("na_bias_g", [16, 128, 14, 256], 2),
                 ("ssd_w_in", [D, 8256], 2), ("ssd_w_out", [2048, D], 2), ("ssd_cw", [128, 48, 5], 2), ("ssd_cb", [128, 48], 2),
                 ("ssd_dtb", [64, 1], 2), ("ssd_alog", [64, 1], 2), ("ssd_dexp", [128, 16], 2), ("ssd_ng", [128, 16], 2))


def build_nc(plan, test_in=(), test_out=()):
    nc = bass.Bass("TRN2", target_bir_lowering=False)
    with contextlib.ExitStack() as stack:
        cx = Ctx(nc, stack)
        s = cx.s
        W = {}

        def scr(name, shape, dt):
            kind = "ExternalInput" if name in test_in else ("ExternalOutput" if name in test_out else "Internal")
            return nc.dram_tensor(name, list(shape), dt, kind=kind).ap()

        xT = nc.dram_tensor("xT", [D, SEQ], F32, kind="ExternalInput").ap()
        outT = nc.dram_tensor("outT", [D, SEQ], F32, kind="ExternalOutput").ap()
        W["pT"] = [nc.dram_tensor(f"pT{i}", [DPLE, SEQ], F32, kind="ExternalInput").ap() for i in range(DEPTH)]
        gains = nc.dram_tensor("gains", [128, len(GAIN_NAMES) * DEPTH, KD], F32, kind="ExternalInput").ap()
        ident_d = nc.dram_tensor("ident", [128, 128], F32, kind="ExternalInput").ap()
        nag_d = nc.dram_tensor("na_gain", [128, 2, 2], F32, kind="ExternalInput").ap()
        for nm, shp, n in WEIGHT_SHAPES:
            W[nm] = [nc.dram_tensor(f"{nm}{i}", shp, F32, kind="ExternalInput").ap() for i in range(n)]
        W["gain_idx"] = {(nm, l): gi * DEPTH + l for gi, nm in enumerate(GAIN_NAMES) for l in range(DEPTH)}
        W["xnT"] = scr("xnT", [D, SEQ], BF16)
        W["mixT"] = scr("mixT", [2 * D, SEQ], BF16)
        ar = Arena(SBUF_BASE)
        ones_t = cx.sb(ar, [128, 128], BF16, "ones")
        NG = len(GAIN_NAMES) * DEPTH
        gains_t = cx.sb(ar, [128, NG, KD], F32, "gains")
        W["ones_t"], W["ones_b"] = ones_t, Buf(None)
        W["gains_t"], W["gains_b"] = gains_t, Buf(None)
        ds_c = s.new_dsem("dconst")
        cb = W["ones_b"]
        s.emit("dve", I("memset", ones_t[:, :], 1.0 / D), writes=[cb])
        cx.eps_t = cx.sb(ar, [128, 1], F32, "eps")
        s.emit("dve", I("memset", cx.eps_t[:, :], EPS), writes=[cb])
        W["eps64_t"] = cx.sb(ar, [128, 1], F32, "eps64")
        s.emit("dve", I("memset", W["eps64_t"][:, :], 64 * EPS), writes=[cb])
        W["ident_t"], W["ident_b"] = cx.sb(ar, [128, 128], BF16, "ident"), cb
        W["bd1_t"] = cx.sb(ar, [128, 128], BF16, "bd1")
        W["bd64_t"] = cx.sb(ar, [128, 128], BF16, "bd64")
        W["bd_b"] = cb
        for t, v in ((W["bd1_t"], 1.0), (W["bd64_t"], 1.0 / 64)):
            s.emit("dve", I("memset", t[:, :], 0.0), writes=[cb])
            s.emit("dve", I("memset", t[0:64, 0:64], v), writes=[cb])
            s.emit("dve", I("memset", t[64:128, 64:128], v), writes=[cb])
        W["nag_t"], W["nag_b"] = cx.sb(ar, [128, 2, 2], F32, "nag"), cb
        s.emit("sp", I("dma_start", out=gains_t[:, :, :], in_=gains), writes=[cb], dsem=ds_c)
        s.emit("sp", I("dma_start", out=W["nag_t"][:, :, :], in_=nag_d), writes=[cb], dsem=ds_c)
        ds_c2 = s.new_dsem("dconst2")
        s.emit("pool", I("dma_start", out=W["ident_t"][:, :], in_=ident_d), writes=[cb], dsem=ds_c2)
        identf_d = nc.dram_tensor("identf", [128, 128], F32, kind="ExternalInput").ap()
        ssdc_d = nc.dram_tensor("ssd_consts", [128, 1280], F32, kind="ExternalInput").ap()
        W["identf_t"] = cx.sb(ar, [128, 128], F32, "identf")
        W["ssdc_t"] = cx.sb(ar, [128, 1280], BF16, "ssdc")
        W["ones1_t"] = cx.sb(ar, [128, 128], BF16, "ones1")
        W["one_t"] = cx.sb(ar, [128, 1], F32, "one")
        s.emit("dve", I("memset", W["ones1_t"][:, :], 1.0), writes=[cb])
        s.emit("dve", I("memset", W["one_t"][:, :], 1.0), writes=[cb])
        s.emit("sp", I("dma_start", out=W["identf_t"][:, :], in_=identf_d), writes=[cb], dsem=ds_c)
        s.emit("pool", I("dma_start", out=W["ssdc_t"][:, :], in_=ssdc_d), writes=[cb], dsem=ds_c2)
        W["ones2k_t"] = cx.sb(ar, [128, 128], BF16, "ones2k")
        s.emit("dve", I("memset", W["ones2k_t"][:, :], 1.0 / 2048), writes=[cb])
        W["ng_t"] = [cx.sb(ar, [128, 16], F32, "ssdng") for _ in range(2)]
        W["ng_b"] = cb
        for l in range(2):
            s.emit("sp", I("dma_start", out=W["ng_t"][l][:, :], in_=W["ssd_ng"][l]), writes=[cb], dsem=ds_c)
        W["zsT"] = scr("zsT", [2048, SEQ], BF16)
        W["xcT"] = scr("xcT", [6144, SEQ], BF16)
        W["scr2_b"] = Buf(None)
        cx.const_top = ar.top
        hT_b = {(d, half): Buf(None) for d in range(KD) for half in range(2)}
        W["hT_b"] = hT_b
        W["scr_b"] = Buf(None)
        s.barrier()
        first = True
        for ph in plan:
            if ph[0] == "chain":
                _, halves, subs = ph
                subs2 = []
                for sub in subs:
                    if sub[0] == "normout":
                        subs2.append(("normout", sub[1], W["xnT"]))
                    elif sub[0] == "proj_na":
                        subs2.append(("proj", W["mixT"], W["na_w_out"][sub[1]], 0))
                    elif sub[0] == "proj_ssd":
                        subs2.append(("proj_ssd", W["mixT"], W["ssd_w_out"][sub[1]], W["ng_t"][sub[1]]))
                    else:
                        subs2.append(sub)
                build_chain_phase(cx, xT if first else outT, outT, halves, subs2, W)
                first = False
            elif ph[0] == "na":
                build_na_phase(cx, W["xnT"], W["mixT"], ph[1], W)
            elif ph[0] == "ssd":
                build_ssd_phase(cx, W["xnT"], W["mixT"], W["zsT"], W["xcT"], ph[1], W)
        s.barrier()
        s.replay()
    return nc


NA_NE = 14


def na_tables():
    idx = np.zeros((128, NA_NE, 256), dtype=np.int64)
    PAD = 15 * 31
    ents = [(4, 4 - 4 + 2 * j) for j in range(6)] + [(0, 2 * j) for j in range(4)] + [(60, 56 + 2 * j) for j in range(4)]
    for e, (rb, a0) in enumerate(ents):
        for jrow in range(2):
            a = a0 + jrow
            for qr in range(4):
                r = rb + qr
                r0 = min(max(r - 4, 0), 56)
                vrow = (r0 <= a <= r0 + 7)
                rr = a - r + 7
                for c in range(64):
                    wc0 = min(max(c - 8, 0), 48)
                    for kc in range(64):
                        ok = vrow and (wc0 <= kc < wc0 + 16)
                        cr = kc - c + 15
                        idx[jrow * 64 + kc, e, qr * 64 + c] = (rr * 31 + cr) if ok else PAD
    return idx


def build_na_phase(cx, xnT, oT, j, W):
    s = cx.s
    st = Arena(cx.const_top)
    ps_t = cx.ps_t
    ps7_bf = ps_t[:, 7 * 512:8 * 512].bitcast(BF16)
    xn_t = cx.sb(st, [128, KD, SEQ], BF16, "na_xn")
    w_t = [cx.sb(st, [128, KD, 3, 128], BF16, "na_w") for _ in range(2)]
    qT_t = [cx.sb(st, [128, SEQ], BF16, "na_qT") for _ in range(2)]
    kT_t = [cx.sb(st, [128, SEQ], BF16, "na_kT") for _ in range(2)]
    vx_t = [cx.sb(st, [128, 32, 2, 66], BF16, "na_vx") for _ in range(2)]
    bias_t = [cx.sb(st, [128, NA_NE, 256], BF16, "na_bias") for _ in range(2)]
    P_t = [cx.sb(st, [128, 6 * 256], BF16, "na_P") for _ in range(2)]
    otok_t = cx.sb(st, [128, 32, 128], BF16, "na_otok")
    oT_t = [cx.sb(st, [128, SEQ], BF16, "na_oT") for _ in range(2)]
    qsb_t = [cx.sb(st, [128, 512], F32, "na_qsb") for _ in range(2)]
    sq_t = [cx.sb(st, [128, 512], BF16, "na_sq") for _ in range(2)]
    rs_t = [cx.sb(st, [128, 512], F32, "na_rs") for _ in range(2)]
    rec_t = [cx.sb(st, [128, 2], F32, "na_rec") for _ in range(2)]

    xn_b = [[Buf(None) for _ in range(8)] for _ in range(KD)]
    w_b = [Buf(None) for _ in range(2)]
    qT_b = [[Buf(None) for _ in range(8)] for _ in range(2)]
    kT_b = [[Buf(None) for _ in range(8)] for _ in range(2)]
    vx_b = [[Buf(None) for _ in range(8)] for _ in range(2)]
    vx1_b = [Buf(None) for _ in range(2)]
    bias_b = [Buf(None) for _ in range(2)]
    P_b = [Buf(None) for _ in range(2)]
    otok_b = [Buf(None) for _ in range(8)]
    oT_b = [Buf(None) for _ in range(2)]
    qsb_b = [Buf(None) for _ in range(2)]
    sq_b = [Buf(None) for _ in range(2)]
    rs_b = [Buf(None) for _ in range(2)]
    rec_b = [Buf(None) for _ in range(2)]
    O_b = [cx.psum[6], cx.psum[7]]
    ident_t, ident_b = W["ident_t"], W["ident_b"]
    bd1_t, bd64_t, bd_b = W["bd1_t"], W["bd64_t"], W["bd_b"]
    nag_t, nag_b = W["nag_t"], W["nag_b"]
    eps64_t = W["eps64_t"]

    ds_x = s.new_dsem("na_dx")
    ds_w = [s.new_dsem(f"na_dw{i}") for i in range(2)]
    ds_b = [s.new_dsem(f"na_db{i}") for i in range(2)]
    ds_o = [s.new_dsem(f"na_do{i}") for i in range(2)]

    xn_v = xnT.rearrange("(kc p) t -> p kc t", p=128)
    for k in range(KD):
        s.emit("sp", I("dma_start", out=xn_t[:, k, :], in_=xnT[k * 128:(k + 1) * 128, :]),
               reads=[W["scr_b"]], writes=xn_b[k], dsem=ds_x)
    for bi in range(2):
        s.emit("pool", I("memset", vx_t[bi][:, :, :, 64:66], 1.0), writes=[vx1_b[bi]])
    w_v = W["na_w_qkv"][j].rearrange("(kc p) c -> p kc c", p=128)
    bias_g = W["na_bias_g"][j]

    for c in range(KD):
        bi = c % 2
        for wi in range(3):
            s.emit("pool", I("dma_start", out=w_t[bi][:, :, wi, :], in_=w_v[:, :, wi * D + c * 128: wi * D + (c + 1) * 128]),
                   writes=[w_b[bi]], dsem=ds_w[bi])
        units = [(tt, wi) for tt in range(8) for wi in range(2)]
        bank_rr = [0]

        def proj(u):
            tt, wi = units[u]
            pb = cx.psum[bank_rr[0] % 6]
            bank_rr[0] += 1
            tsl = slice(tt * 512, (tt + 1) * 512)
            for k in range(KD):
                s.emit("pe", I("matmul", pb.ap[:, :], lhsT=w_t[bi][:, k, wi, :], rhs=xn_t[:, k, tsl], start=(k == 0), stop=(k == KD - 1)),
                       reads=[w_b[bi], xn_b[k][tt]], writes=[pb])
            x = u % 2
            s.emit("act", I("activation", out=sq_t[x][:, :], in_=pb.ap[:, :], func=AF.Square), reads=[pb], writes=[sq_b[x]])
            s.emit("act", I("activation", out=qsb_t[x][:, :], in_=pb.ap[:, :], func=AF.Copy), reads=[pb], writes=[qsb_b[x]])

        def norm(u):
            tt, wi = units[u]
            x = u % 2
            tsl = slice(tt * 512, (tt + 1) * 512)
            p7 = cx.psum[7]
            bd = bd1_t if wi == 0 else bd64_t
            ept = eps64_t if wi == 0 else cx.eps_t
            s.emit("pe", I("matmul", p7.ap[:, :], lhsT=bd[:, :], rhs=sq_t[x][:, :], start=True, stop=True), reads=[bd_b, sq_b[x]], writes=[p7])
            s.emit("act", I("activation", out=rs_t[x][:, :], in_=p7.ap[:, :], func=AF.Sqrt, bias=ept[:, 0:1], scale=1.0), reads=[p7], writes=[rs_b[x]])
            s.emit("dve", I("reciprocal", out=rs_t[x][:, :], in_=rs_t[x][:, :]), reads=[rs_b[x]], writes=[rs_b[x]])
            dst_t = qT_t if wi == 0 else kT_t
            dst_b = qT_b if wi == 0 else kT_b
            s.emit("dve", I("scalar_tensor_tensor", out=dst_t[bi][:, tsl], in0=qsb_t[x][:, :], scalar=nag_t[:, j, wi:wi + 1], in1=rs_t[x][:, :], op0=ALU.mult, op1=ALU.mult),
                   reads=[qsb_b[x], rs_b[x], nag_b], writes=[dst_b[bi][tt]])

        proj(0)
        for u in range(1, len(units)):
            proj(u)
            norm(u - 1)
        norm(len(units) - 1)
        for tg in range(8):
            pb = cx.psum[bank_rr[0] % 6]
            bank_rr[0] += 1
            for t4 in range(4):
                tile = tg * 4 + t4
                for k in range(KD):
                    s.emit("pe", I("matmul", pb.ap[:, t4 * 128:(t4 + 1) * 128], lhsT=xn_t[:, k, tile * 128:(tile + 1) * 128], rhs=w_t[bi][:, k, 2, :], start=(k == 0), stop=(k == KD - 1)),
                           reads=[w_b[bi], xn_b[k][tile // 4]], writes=[pb])
            s.emit("act", I("activation", out=vx_t[bi][:, tg * 4:(tg + 1) * 4, :, 0:64], in_=pb.ap[:, :].rearrange("p (t h d) -> p t h d", t=4, h=2), func=AF.Copy),
                   reads=[pb], writes=[vx_b[bi][tg]])

        iters = []
        for hh in range(2):
            for b in range(16):
                iters.append((hh, b))

        def geom(b):
            rb = 4 * b
            if b == 0:
                return rb, [(6 + jj, 2 * jj) for jj in range(4)]
            if b == 15:
                return rb, [(10 + jj, 56 + 2 * jj) for jj in range(4)]
            return rb, [(jj, rb - 4 + 2 * jj) for jj in range(6)]

        def S_stage(it):
            hh, b = iters[it]
            g = it % 2
            h = 2 * c + hh
            hb = h % 2
            if b == 0:
                s.emit("pool", I("dma_start", out=bias_t[hb][:, :, :], in_=bias_g[h]), writes=[bias_b[hb]], dsem=ds_b[hb])
            rb, tiles = geom(b)
            q0 = rb * 64
            for jj, (ent, a0) in enumerate(tiles):
                bank = cx.psum[3 * g + jj // 2]
                reg = ps_t[:, 3 * g * 512 + jj * 256: 3 * g * 512 + (jj + 1) * 256]
                k0 = a0 * 64
                s.emit("pe", I("matmul", reg, lhsT=kT_t[bi][hh * 64:(hh + 1) * 64, k0:k0 + 128], rhs=qT_t[bi][hh * 64:(hh + 1) * 64, q0:q0 + 256], start=True, stop=False),
                       reads=[kT_b[bi][k0 // 512], kT_b[bi][(k0 + 127) // 512], qT_b[bi][q0 // 512]], writes=[bank])
                s.emit("pe", I("matmul", reg, lhsT=ident_t[:, :], rhs=bias_t[hb][:, ent, :], start=False, stop=True),
                       reads=[bias_b[hb], ident_b], writes=[bank])
            nt = len(tiles)
            banks = [cx.psum[3 * g + x] for x in range((nt + 1) // 2)]
            s.emit("act", I("activation", out=P_t[g][:, 0:nt * 256], in_=ps_t[:, 3 * g * 512: 3 * g * 512 + nt * 256], func=AF.Exp),
                   reads=banks, writes=[P_b[g]])

        def PV_stage(it):
            hh, b = iters[it]
            g = it % 2
            rb, tiles = geom(b)
            nt = len(tiles)
            obase = (6 + g) * 512
            for t in range(2):
                oreg = ps_t[:, obase + t * 66: obase + t * 66 + 65]
                for jj, (ent, a0) in enumerate(tiles):
                    vt = a0 // 2
                    s.emit("pe", I("matmul", oreg, lhsT=P_t[g][:, jj * 256 + t * 128: jj * 256 + (t + 1) * 128], rhs=vx_t[bi][:, vt, hh, 0:65], start=(jj == 0), stop=(jj == nt - 1)),
                           reads=[P_b[g], vx_b[bi][vt // 4], vx1_b[bi]], writes=[O_b[g]])
            s.emit("dve", I("reciprocal", out=rec_t[g][:, :], in_=ps_t[:, obase:obase + 132].rearrange("p (t d) -> p t d", t=2)[:, :, 64]),
                   reads=[O_b[g]], writes=[rec_b[g]])
            for t in range(2):
                tile = rb // 2 + t
                s.emit("dve", I("tensor_scalar", out=otok_t[:, tile, hh * 64:(hh + 1) * 64], in0=ps_t[:, obase + t * 66: obase + t * 66 + 64], scalar1=rec_t[g][:, t:t + 1], scalar2=None, op0=ALU.mult),
                       reads=[O_b[g], rec_b[g]], writes=[otok_b[tile // 4]])

        S_stage(0)
        for it in range(1, len(iters)):
            S_stage(it)
            PV_stage(it - 1)
        PV_stage(len(iters) - 1)

        for tg in range(8):
            p7 = cx.psum[7]
            for t4 in range(4):
                tile = tg * 4 + t4
                s.emit("pe", I("transpose", out=ps7_bf[:, t4 * 128:(t4 + 1) * 128], in_=otok_t[:, tile, :], identity=ident_t[:, :]),
                       reads=[otok_b[tg], ident_b], writes=[p7])
            s.emit("act", I("activation", out=oT_t[bi][:, tg * 512:(tg + 1) * 512], in_=ps7_bf[:, 0:512], func=AF.Copy),
                   reads=[p7], writes=[oT_b[bi]])
        s.emit("sp", I("dma_start", out=oT[c * 128:(c + 1) * 128, :], in_=oT_t[bi][:, :]),
               reads=[oT_b[bi]], writes=[W["scr_b"]], dsem=ds_o[bi])
    s.barrier()


_NA_IDX = None


def host_bias_g(rpb):
    global _NA_IDX
    if _NA_IDX is None:
        _NA_IDX = na_tables()
    flat = np.concatenate([rpb.reshape(16, -1).astype(np.float32), np.full((16, 1), -30000.0, np.float32)], axis=1)
    return np.ascontiguousarray(flat[:, _NA_IDX])


def host_na_gain(qs, ks):
    out = np.zeros((128, 2, 2), np.float32)
    for l in range(2):
        out[:, l, 0] = np.tile(qs[l], 2)
        out[:, l, 1] = np.tile(ks[l], 2)
    return out


def dummy_inputs():
    ins = {"xT": np.zeros((D, SEQ), np.float32), "gains": np.ones((128, 20, KD), np.float32),
           "ident": np.eye(128, dtype=np.float32), "na_gain": np.ones((128, 2, 2), np.float32),
           "identf": np.eye(128, dtype=np.float32), "ssd_consts": host_ssd_consts()}
    for i in range(DEPTH):
        ins[f"pT{i}"] = np.zeros((DPLE, SEQ), np.float32)
    for nm, shp, n in WEIGHT_SHAPES:
        for i in range(n):
            ins[f"{nm}{i}"] = np.zeros(shp, np.float32)
    return ins


DI = 2048
NQ = 6
SSD_IN = 8256


def build_ssd_phase(cx, xnT, uT, zsT, xcT, j, W):
    s = cx.s
    ps_t = cx.ps_t
    base = Arena(cx.const_top)
    sc_t = cx.sb(base, [128, 32, NQ, 64], F32, "ssd_sc")
    sc_b = Buf(None)
    identf_t = W["identf_t"]
    ident_t, ident_b = W["ident_t"], W["ident_b"]
    cb = ident_b
    w_in = W["ssd_w_in"][j]
    w_v = w_in.rearrange("(kc p) c -> p kc c", p=128)
    st = Arena(base.top)
    xn_t = cx.sb(st, [128, KD, SEQ], BF16, "sa_xn")
    xn_b = [[Buf(None) for _ in range(8)] for _ in range(KD)]
    w_t = [cx.sb(st, [128, KD, 128], BF16, "sa_w") for _ in range(2)]
    w_b = [Buf(None) for _ in range(2)]
    cw_t = cx.sb(st, [128, 48, 5], F32, "sa_cw")
    cbias_t = cx.sb(st, [128, 48], F32, "sa_cb")
    dtb_t = cx.sb(st, [64, 1], F32, "sa_dtb")
    a_t = cx.sb(st, [64, 1], F32, "sa_a")
    prm_b = Buf(None)
    ds_x = s.new_dsem("sa_dx")
    ds_w = [s.new_dsem(f"sa_dw{i}") for i in range(2)]
    ds_p = s.new_dsem("sa_dp")
    ds_o = [s.new_dsem(f"sa_do{i}") for i in range(2)]
    for k in range(KD):
        s.emit("sp", I("dma_start", out=xn_t[:, k, :], in_=xnT[k * 128:(k + 1) * 128, :]),
               reads=[W["scr_b"]], writes=xn_b[k], dsem=ds_x)
    s.emit("sp", I("dma_start", out=cw_t[:, :, :], in_=W["ssd_cw"][j]), writes=[prm_b], dsem=ds_p)
    s.emit("sp", I("dma_start", out=cbias_t[:, :], in_=W["ssd_cb"][j]), writes=[prm_b], dsem=ds_p)
    s.emit("sp", I("dma_start", out=dtb_t[:, :], in_=W["ssd_dtb"][j]), writes=[prm_b], dsem=ds_p)
    s.emit("sp", I("dma_start", out=a_t[:, :], in_=W["ssd_alog"][j]), writes=[prm_b], dsem=ds_p)
    s.emit("act", I("activation", out=a_t[:, :], in_=a_t[:, :], func=AF.Exp), reads=[prm_b], writes=[prm_b])
    s.emit("dve", I("tensor_scalar", out=a_t[:, :], in0=a_t[:, :], scalar1=-1.0, scalar2=None, op0=ALU.mult), reads=[prm_b], writes=[prm_b])

    s.emit("pool", I("dma_start", out=w_t[0][:, :, 0:64], in_=w_v[:, :, 8192:8256]), writes=[w_b[0]], dsem=ds_w[0])
    sa = Arena(st.top)
    HT = 1024
    names = ["dt", "dtar", "cum", "X", "tmp", "dte", "E", "cd", "dtd", "negX", "mask", "e1"]
    F = {n: cx.sb(sa, [64, HT], F32, "sf_" + n) for n in names}
    dtab_t = cx.sb(sa, [64, HT], BF16, "sf_dtab")
    fb = {n: Buf(None) for n in names + ["dtab"]}
    s.emit("pool", I("memset", F["mask"][:, :], 1.0), writes=[fb["mask"]])
    s.emit("pool", I("memset", F["mask"][:, :].rearrange("p (c q) -> p c q", q=128)[:, :, 0:1], 0.0), writes=[fb["mask"]])
    for half in range(SEQ // HT):
        for t4 in range(HT // 512):
            tt = half * (HT // 512) + t4
            pb = cx.psum[tt % 4]
            tsl = slice(tt * 512, (tt + 1) * 512)
            lsl = slice(t4 * 512, (t4 + 1) * 512)
            for k in range(KD):
                s.emit("pe", I("matmul", pb.ap[0:64, :], lhsT=w_t[0][:, k, 0:64], rhs=xn_t[:, k, tsl], start=(k == 0), stop=(k == KD - 1)),
                       reads=[w_b[0], xn_b[k][tt]], writes=[pb])
            s.emit("act", I("activation", out=F["e1"][:, lsl], in_=pb.ap[0:64, :], func=AF.Exp, bias=dtb_t[:, 0:1], scale=1.0),
                   reads=[pb, prm_b], writes=[fb["e1"]])
        s.emit("act", I("activation", out=F["dt"][:, :], in_=F["e1"][:, :], func=AF.Ln, bias=W["one_t"][0:64, 0:1], scale=1.0),
               reads=[fb["e1"]], writes=[fb["dt"]])
        s.emit("dve", I("tensor_scalar", out=dtab_t[:, :], in0=F["dt"][:, :], scalar1=a_t[:, 0:1], scalar2=None, op0=ALU.mult),
               reads=[fb["dt"], prm_b], writes=[fb["dtab"]])
        s.emit("dve", I("tensor_copy", out=F["dtar"][:, :], in_=dtab_t[:, :]), reads=[fb["dtab"]], writes=[fb["dtar"]])
        s.emit("dve", I("tensor_tensor_scan", out=F["cum"][:, :], data0=F["mask"][:, :], data1=F["dtar"][:, :], initial=0.0, op0=ALU.mult, op1=ALU.add),
               reads=[fb["mask"], fb["dtar"]], writes=[fb["cum"]])
        cumv = F["cum"][:, :].rearrange("p (c q) -> p c q", q=128)
        totbc = cumv[:, :, 127:128].to_broadcast([64, HT // 128, 128])
        v3 = lambda n, lo=0, hi=64: F[n][lo:hi, :].rearrange("p (c q) -> p c q", q=128)
        s.emit("dve", I("tensor_copy", out=F["X"][0:32, :], in_=F["cum"][0:32, :]), reads=[fb["cum"]], writes=[fb["X"]])
        totbc_b = cumv[32:64, :, 127:128].to_broadcast([32, HT // 128, 128])
        s.emit("dve", I("tensor_tensor", out=v3("tmp", 32, 64), in0=totbc_b, in1=v3("cum", 32, 64), op=ALU.subtract), reads=[fb["cum"]], writes=[fb["tmp"]])
        s.emit("dve", I("tensor_tensor", out=F["X"][32:64, :], in0=F["tmp"][32:64, :], in1=F["dtar"][32:64, :], op=ALU.add), reads=[fb["tmp"], fb["dtar"]], writes=[fb["X"]])
        s.emit("dve", I("tensor_scalar", out=F["negX"][:, :], in0=F["X"][:, :], scalar1=-1.0, scalar2=None, op0=ALU.mult), reads=[fb["X"]], writes=[fb["negX"]])
        s.emit("act", I("activation", out=F["E"][:, :], in_=F["X"][:, :], func=AF.Exp), reads=[fb["X"]], writes=[fb["E"]])
        s.emit("dve", I("tensor_tensor", out=v3("tmp"), in0=totbc, in1=v3("X"), op=ALU.subtract), reads=[fb["cum"], fb["X"]], writes=[fb["tmp"]])
        s.emit("act", I("activation", out=F["dte"][:, :], in_=F["tmp"][:, :], func=AF.Exp), reads=[fb["tmp"]], writes=[fb["dte"]])
        s.emit("act", I("activation", out=v3("cd"), in_=totbc, func=AF.Exp), reads=[fb["cum"]], writes=[fb["cd"]])
        s.emit("dve", I("tensor_tensor", out=F["dtd"][:, :], in0=F["dt"][:, :], in1=F["dte"][:, :], op=ALU.mult), reads=[fb["dt"], fb["dte"]], writes=[fb["dtd"]])
        qn = ["dt", "dtd", "dtar", "negX", "E", "cd"]
        for cl in range(HT // 128):
            c = half * (HT // 128) + cl
            pb = cx.psum[4 + cl % 2]
            for qi, n in enumerate(qn):
                s.emit("pe", I("transpose", out=pb.ap[:, qi * 64:(qi + 1) * 64], in_=F[n][:, cl * 128:(cl + 1) * 128], identity=identf_t[0:64, 0:64]),
                       reads=[fb[n], cb], writes=[pb])
            s.emit("act", I("activation", out=sc_t[:, c, :, :], in_=pb.ap[:, 0:NQ * 64].rearrange("p (q h) -> p q h", q=NQ), func=AF.Copy),
                   reads=[pb], writes=[sc_b])
    s.barrier()

    sa = Arena(st.top)
    xe_t = [cx.sb(sa, [128, SEQ + 4], F32, "sa_xe") for _ in range(2)]
    acc_t = [cx.sb(sa, [128, SEQ], F32, "sa_acc") for _ in range(2)]
    ob_t = [cx.sb(sa, [128, SEQ], BF16, "sa_ob") for _ in range(2)]
    xe_b = [[Buf(None) for _ in range(8)] for _ in range(2)]
    xepad_b = [Buf(None) for _ in range(2)]
    acc_b = [Buf(None) for _ in range(2)]
    ob_b = [Buf(None) for _ in range(2)]
    for i in range(2):
        s.emit("pool", I("memset", xe_t[i][:, 0:2], 0.0), writes=[xepad_b[i]])
        s.emit("pool", I("memset", xe_t[i][:, SEQ + 2:SEQ + 4], 0.0), writes=[xepad_b[i]])
    for m in range(64):
        bi = m % 2
        s.emit("pool", I("dma_start", out=w_t[bi][:, :, :], in_=w_v[:, :, m * 128:(m + 1) * 128]), writes=[w_b[bi]], dsem=ds_w[bi])
        for tt in range(8):
            pb = cx.psum[(m * 8 + tt) % 6]
            tsl = slice(tt * 512, (tt + 1) * 512)
            for k in range(KD):
                s.emit("pe", I("matmul", pb.ap[:, :], lhsT=w_t[bi][:, k, :], rhs=xn_t[:, k, tsl], start=(k == 0), stop=(k == KD - 1)),
                       reads=[w_b[bi], xn_b[k][tt]], writes=[pb])
            if m < 16:
                s.emit("act", I("activation", out=ob_t[bi][:, tsl], in_=pb.ap[:, :], func=AF.Silu), reads=[pb], writes=[ob_b[bi]])
            else:
                s.emit("act", I("activation", out=xe_t[bi][:, 2 + tt * 512: 2 + (tt + 1) * 512], in_=pb.ap[:, :], func=AF.Copy), reads=[pb], writes=[xe_b[bi][tt]])
        if m < 16:
            s.emit("sp", I("dma_start", out=zsT[m * 128:(m + 1) * 128, :], in_=ob_t[bi][:, :]), reads=[ob_b[bi]], writes=[W["scr2_b"]], dsem=ds_o[bi])
        else:
            cm = m - 16
            rd = xe_b[bi] + [xepad_b[bi], prm_b]
            s.emit("dve", I("tensor_scalar", out=acc_t[bi][:, :], in0=xe_t[bi][:, 0:SEQ], scalar1=cw_t[:, cm, 0:1], scalar2=cbias_t[:, cm:cm + 1], op0=ALU.mult, op1=ALU.add),
                   reads=rd, writes=[acc_b[bi]])
            for kk in (1, 2, 3, 4):
                s.emit("dve", I("scalar_tensor_tensor", out=acc_t[bi][:, :], in0=xe_t[bi][:, kk:kk + SEQ], scalar=cw_t[:, cm, kk:kk + 1], in1=acc_t[bi][:, :], op0=ALU.mult, op1=ALU.add),
                       reads=rd + [acc_b[bi]], writes=[acc_b[bi]])
            s.emit("act", I("activation", out=ob_t[bi][:, :], in_=acc_t[bi][:, :], func=AF.Silu), reads=[acc_b[bi]], writes=[ob_b[bi]])
            s.emit("sp", I("dma_start", out=xcT[cm * 128:(cm + 1) * 128, :], in_=ob_t[bi][:, :]), reads=[ob_b[bi]], writes=[W["scr2_b"]], dsem=ds_o[bi])
    s.barrier()
    build_ssd_scan(cx, base, sc_t, uT, zsT, xcT, j, W)


def build_ssd_scan(cx, base, sc_t, uT, zsT, xcT, j, W):
    s = cx.s
    ps_t = cx.ps_t
    st = Arena(base.top)
    ident_t, cb = W["ident_t"], W["ident_b"]
    identf_t = W["identf_t"]
    ones1_t = W["ones1_t"]
    U_t = [W["ssdc_t"][:, 0:128], W["ssdc_t"][:, 128:256]]
    M_t = [W["ssdc_t"][:, 256:768], W["ssdc_t"][:, 768:1280]]
    BT = [cx.sb(st, [128, SEQ], BF16, "sb_BT") for _ in range(2)]
    CT = [cx.sb(st, [128, SEQ], BF16, "sb_CT") for _ in range(2)]
    Btok = [cx.sb(st, [128, 32, 128], BF16, "sb_Btok") for _ in range(2)]
    xs_tok = cx.sb(st, [128, 32, 256], BF16, "sb_xstok")
    y_tok = cx.sb(st, [128, 32, 256], F32, "sb_ytok")
    S32 = [cx.sb(st, [128, 256], F32, "sb_S32") for _ in range(2)]
    Sbf = [cx.sb(st, [128, 256], BF16, "sb_Sbf") for _ in range(2)]
    rhs1 = [cx.sb(st, [128, 512], BF16, "sb_rhs1") for _ in range(2)]
    LT = [cx.sb(st, [128, 512], F32, "sb_LT") for _ in range(2)]
    STt = [cx.sb(st, [128, 512], BF16, "sb_ST") for _ in range(2)]
    xdt2 = [cx.sb(st, [128, 2, 4, 64], BF16, "sb_xdt2") for _ in range(2)]
    xdt = [t[:, 0, :, :].rearrange("p h e -> p (h e)") for t in xdt2]
    xdtd = [t[:, 1, :, :].rearrange("p h e -> p (h e)") for t in xdt2]
    yo = [cx.sb(st, [128, 256], F32, "sb_yo") for _ in range(2)]
    xq = [cx.sb(st, [128, 2, 512], BF16, "sb_xq") for _ in range(2)]
    zq = [cx.sb(st, [128, 2, 512], BF16, "sb_zq") for _ in range(2)]
    tq = [cx.sb(st, [128, 512], F32, "sb_tq") for _ in range(2)]
    uT_t = cx.sb(st, [128, 2, SEQ], BF16, "sb_uT")
    dexp_t = cx.sb(st, [128, 16], F32, "sb_dexp")

    BT_b = [Buf(None) for _ in range(2)]
    CT_b = [Buf(None) for _ in range(2)]
    Btok_b = [[Buf(None) for _ in range(8)] for _ in range(2)]
    xs_b = [Buf(None) for _ in range(8)]
    y_b = [Buf(None) for _ in range(32)]
    S32_b = [Buf(None) for _ in range(2)]
    Sbf_b = [Buf(None) for _ in range(2)]
    rhs1_b = [Buf(None) for _ in range(2)]
    LT_b = [Buf(None) for _ in range(2)]
    ST_b = [Buf(None) for _ in range(2)]
    xdt_b = [Buf(None) for _ in range(2)]
    xdtd_b = [Buf(None) for _ in range(2)]
    yo_b = [Buf(None) for _ in range(2)]
    xq_b = [Buf(None) for _ in range(2)]
    zq_b = [Buf(None) for _ in range(2)]
    tq_b = [Buf(None) for _ in range(2)]
    uT_b = [Buf(None) for _ in range(2)]
    dexp_b = Buf(None)
    sc_b = Buf(None)
    ds_g = [s.new_dsem(f"sb_dg{i}") for i in range(4)]
    ds_q = [s.new_dsem(f"sb_dq{i}") for i in range(2)]
    ds_u = s.new_dsem("sb_du")
    ds_d = s.new_dsem("sb_dd")
    s.emit("sp", I("dma_start", out=dexp_t[:, :], in_=W["ssd_dexp"][j]), writes=[dexp_b], dsem=ds_d)
    bf = lambda bank: ps_t[:, bank * 512:(bank + 1) * 512].bitcast(BF16)

    def load_bc(g):
        for d in range(2):
            r0 = 2048 + d * 2048 + g * 128
            s.emit("sp", I("dma_start", out=BT[d][:, :], in_=xcT[r0:r0 + 128, :]), reads=[W["scr2_b"]], writes=[BT_b[d]], dsem=ds_g[d * 2])
            s.emit("sp", I("dma_start", out=CT[d][:, :], in_=xcT[r0 + 1024:r0 + 1024 + 128, :]), reads=[W["scr2_b"]], writes=[CT_b[d]], dsem=ds_g[d * 2 + 1])

    load_bc(0)
    for g in range(8):
        for d in range(2):
            for c4 in range(8):
                bank = 6 + c4 % 2
                for t in range(4):
                    c = c4 * 4 + t
                    s.emit("pe", I("transpose", out=bf(bank)[:, t * 128:(t + 1) * 128], in_=BT[d][:, c * 128:(c + 1) * 128], identity=ident_t[:, :]),
                           reads=[BT_b[d], cb], writes=[cx.psum[bank]])
                s.emit("act", I("activation", out=Btok[d][:, c4 * 4:(c4 + 1) * 4, :], in_=bf(bank)[:, 0:512].rearrange("p (t n) -> p t n", t=4), func=AF.Copy),
                       reads=[cx.psum[bank]], writes=[Btok_b[d][c4]])
        for quad in range(8):
            qb = quad % 2
            for cc in range(2):
                r0 = g * 256 + cc * 128
                s.emit("sp", I("dma_start", out=xq[qb][:, cc, :], in_=xcT[r0:r0 + 128, quad * 512:(quad + 1) * 512]), reads=[W["scr2_b"]], writes=[xq_b[qb]], dsem=ds_q[qb])
            bank = 6 + quad % 2
            for t in range(4):
                for cc in range(2):
                    s.emit("pe", I("transpose", out=bf(bank)[:, (t * 2 + cc) * 128:(t * 2 + cc + 1) * 128], in_=xq[qb][:, cc, t * 128:(t + 1) * 128], identity=ident_t[:, :]),
                           reads=[xq_b[qb], cb], writes=[cx.psum[bank]])
            s.emit("act", I("activation", out=xs_tok[:, quad * 4:(quad + 1) * 4, :], in_=bf(bank)[:, 0:1024].rearrange("p (t n) -> p t n", t=4), func=AF.Copy),
                   reads=[cx.psum[bank]], writes=[xs_b[quad]])
        for d in range(2):
            s.emit("dve", I("memset", S32[d][:, :], 0.0), writes=[S32_b[d]])
            s.emit("dve", I("memset", Sbf[d][:, :], 0.0), writes=[Sbf_b[d]])

        steps = [(i, d) for i in range(32) for d in range(2)]

        def pre(n):
            i, d = steps[n]
            c = i if d == 0 else 31 - i
            par = n % 2
            csl = slice(c * 128, (c + 1) * 128)
            h0 = d * 32 + g * 4
            v4 = lambda t: t[:, :].rearrange("p (h q) -> p h q", h=4)
            s.emit("pool", I("tensor_tensor", out=v4(rhs1[par]), in0=U_t[d].unsqueeze(1).to_broadcast([128, 4, 128]),
                             in1=sc_t[:, c, 2, h0:h0 + 4].unsqueeze(2).to_broadcast([128, 4, 128]), op=ALU.mult),
                   reads=[sc_b, cb], writes=[rhs1_b[par]])
            seg = cx.psum[par]
            s.emit("pe", I("matmul", seg.ap[:, :], lhsT=ones1_t[:, :], rhs=rhs1[par][:, :], start=True, stop=False), reads=[rhs1_b[par], cb], writes=[seg])
            s.emit("pe", I("matmul", seg.ap[:, :], lhsT=ident_t[:, :], rhs=M_t[d], start=False, stop=True), reads=[cb], writes=[seg])
            for h in range(4):
                s.emit("act", I("activation", out=LT[par][:, h * 128:(h + 1) * 128], in_=seg.ap[:, h * 128:(h + 1) * 128], func=AF.Exp, bias=sc_t[:, c, 3, h0 + h:h0 + h + 1], scale=1.0),
                       reads=[seg, sc_b], writes=[LT_b[par]])
            cbk = cx.psum[2 + par]
            s.emit("pe", I("matmul", cbk.ap[:, 0:128], lhsT=BT[d][:, csl], rhs=CT[d][:, csl], start=True, stop=True), reads=[BT_b[d], CT_b[d]], writes=[cbk])
            s.emit("dve", I("tensor_tensor", out=v4(STt[par]), in0=v4(LT[par]), in1=cbk.ap[:, 0:128].unsqueeze(1).to_broadcast([128, 4, 128]), op=ALU.mult),
                   reads=[LT_b[par], cbk], writes=[ST_b[par]])
            xv = xs_tok[:, c, :].rearrange("p (h e) -> p h e", h=4)
            v64 = lambda t: t[:, :].rearrange("p (h e) -> p h e", h=4)
            s.emit("pool", I("tensor_tensor", out=xdt2[par][:, :, :, :], in0=xv.unsqueeze(1).to_broadcast([128, 2, 4, 64]),
                             in1=sc_t[:, c, 0:2, h0:h0 + 4].unsqueeze(3).to_broadcast([128, 2, 4, 64]), op=ALU.mult),
                   reads=[xs_b[c // 4], sc_b], writes=[xdt_b[par], xdtd_b[par]])

        def post(n):
            i, d = steps[n]
            c = i if d == 0 else 31 - i
            par = n % 2
            csl = slice(c * 128, (c + 1) * 128)
            h0 = d * 32 + g * 4
            first = i < 16
            yb = cx.psum[4 + par]
            sb_ = cx.psum[6 + par]
            for h in range(4):
                s.emit("pe", I("matmul", yb.ap[:, h * 64:(h + 1) * 64], lhsT=STt[par][:, h * 128:(h + 1) * 128], rhs=xdt[par][:, h * 64:(h + 1) * 64], start=True, stop=True),
                       reads=[ST_b[par], xdt_b[par]], writes=[yb])
            s.emit("pe", I("matmul", yb.ap[:, 256:512], lhsT=CT[d][:, csl], rhs=Sbf[d][:, :], start=True, stop=True), reads=[CT_b[d], Sbf_b[d]], writes=[yb])
            s.emit("pe", I("matmul", sb_.ap[:, 0:256], lhsT=Btok[d][:, c, :], rhs=xdtd[par][:, :], start=True, stop=True), reads=[Btok_b[d][c // 4], xdtd_b[par]], writes=[sb_])
            v64 = lambda t: t.rearrange("p (h e) -> p h e", h=4)
            s.emit("dve", I("tensor_tensor", out=v64(yo[par][:, :]), in0=v64(yb.ap[:, 256:512]), in1=sc_t[:, c, 4, h0:h0 + 4].unsqueeze(2).to_broadcast([128, 4, 64]), op=ALU.mult),
                   reads=[yb, sc_b], writes=[yo_b[par]])
            if first:
                s.emit("dve", I("tensor_tensor", out=y_tok[:, c, :], in0=yo[par][:, :], in1=yb.ap[:, 0:256], op=ALU.add), reads=[yo_b[par], yb], writes=[y_b[c]])
            else:
                s.emit("dve", I("tensor_tensor", out=yo[par][:, :], in0=yo[par][:, :], in1=yb.ap[:, 0:256], op=ALU.add), reads=[yo_b[par], yb], writes=[yo_b[par]])
                s.emit("dve", I("tensor_tensor", out=y_tok[:, c, :], in0=y_tok[:, c, :], in1=yo[par][:, :], op=ALU.add), reads=[yo_b[par], y_b[c]], writes=[y_b[c]])
            for h in range(4):
                hs = slice(h * 64, (h + 1) * 64)
                s.emit("dve", I("scalar_tensor_tensor", out=S32[d][:, hs], in0=S32[d][:, hs], scalar=sc_t[:, c, 5, h0 + h:h0 + h + 1], in1=sb_.ap[:, hs], op0=ALU.mult, op1=ALU.add),
                       reads=[S32_b[d], sb_, sc_b], writes=[S32_b[d]])
            pending_sbf.append(d)

        pending_sbf = []

        def flush_sbf():
            while pending_sbf:
                d_ = pending_sbf.pop(0)
                s.emit("act", I("activation", out=Sbf[d_][:, :], in_=S32[d_][:, :], func=AF.Copy), reads=[S32_b[d_]], writes=[Sbf_b[d_]])

        pre(0)
        for n in range(1, len(steps)):
            pre(n)
            flush_sbf()
            post(n - 1)
        flush_sbf()
        post(len(steps) - 1)
        pending_sbf.clear()
        if g + 1 < 8:
            load_bc(g + 1)

        xfin = [xs_tok[:, 0:16, :].rearrange("p a b -> p (a b)"), xs_tok[:, 16:32, :].rearrange("p a b -> p (a b)")]
        zfin = [Btok[0][:, :, :].rearrange("p a b -> p (a b)"), Btok[1][:, :, :].rearrange("p a b -> p (a b)")]
        xfin_b = [xs_b[0:4], xs_b[4:8]]
        zfin_b = [Btok_b[0], Btok_b[1]]
        for cc in range(2):
            r0 = g * 256 + cc * 128
            s.emit("sp", I("dma_start", out=xfin[cc], in_=xcT[r0:r0 + 128, :]), reads=[W["scr2_b"]], writes=xfin_b[cc], dsem=ds_q[0])
            s.emit("sp", I("dma_start", out=zfin[cc], in_=zsT[r0:r0 + 128, :]), reads=[W["scr2_b"]], writes=zfin_b[cc], dsem=ds_q[1])
        for quad in range(8):
            qb = quad % 2
            for cc in range(2):
                bank = cx.psum[(quad * 2 + cc) % 4]
                for t in range(4):
                    c = quad * 4 + t
                    s.emit("pe", I("transpose", out=bank.ap[:, t * 128:(t + 1) * 128], in_=y_tok[:, c, cc * 128:(cc + 1) * 128], identity=identf_t[:, :]),
                           reads=[y_b[c], cb], writes=[bank])
                tb = (quad * 2 + cc) % 2
                qsl = slice(quad * 512, (quad + 1) * 512)
                s.emit("dve", I("scalar_tensor_tensor", out=tq[tb][:, :], in0=xfin[cc][:, qsl], scalar=dexp_t[:, g * 2 + cc:g * 2 + cc + 1], in1=bank.ap[:, :], op0=ALU.mult, op1=ALU.add),
                       reads=xfin_b[cc] + [bank, dexp_b], writes=[tq_b[tb]])
                s.emit("pool", I("tensor_tensor", out=uT_t[:, cc, qsl], in0=tq[tb][:, :], in1=zfin[cc][:, qsl], op=ALU.mult),
                       reads=[tq_b[tb]] + zfin_b[cc], writes=[uT_b[cc]])
        for cc in range(2):
            r0 = g * 256 + cc * 128
            s.emit("sp", I("dma_start", out=uT[r0:r0 + 128, :], in_=uT_t[:, cc, :]), reads=[uT_b[cc]], writes=[W["scr_b"]], dsem=ds_u)
    s.barrier()


def host_ssd_consts():
    j = np.arange(128)[:, None]
    q = np.arange(128)[None, :]
    Uf = (j <= q).astype(np.float32)
    Ub = (j >= q).astype(np.float32)
    Mf = np.where(q >= j, 0.0, -30000.0).astype(np.float32)
    Mb = np.where(j >= q, 0.0, -30000.0).astype(np.float32)
    return np.ascontiguousarray(np.concatenate([Uf, Ub, np.tile(Mf, (1, 4)), np.tile(Mb, (1, 4))], axis=1))


def host_ssd_params(conv_w, conv_b, dt_bias, a_log, d_skip, norm_g):
    out = {}
    out["ssd_cw"] = np.ascontiguousarray(conv_w.reshape(5, 48, 128).transpose(2, 1, 0)).astype(np.float32)
    out["ssd_cb"] = np.ascontiguousarray(conv_b.reshape(48, 128).T).astype(np.float32)
    out["ssd_dtb"] = np.ascontiguousarray(dt_bias.reshape(64, 1)).astype(np.float32)
    out["ssd_alog"] = np.ascontiguousarray(a_log.reshape(64, 1)).astype(np.float32)
    out["ssd_dexp"] = np.ascontiguousarray(np.repeat(d_skip, 64).reshape(16, 128).T).astype(np.float32)
    out["ssd_ng"] = np.ascontiguousarray(norm_g.reshape(16, 128).T).astype(np.float32)
    return out


def full_plan():
    plan = [("chain", [0, 1], [("ffn", 0, 1), ("normout", 0)])]
    for i in range(DEPTH):
        jm = i // 2
        if i % 2 == 0:
            plan.append(("ssd", jm))
            head = [("proj_ssd", jm)]
        else:
            plan.append(("na", jm))
            head = [("proj_na", jm)]
        subs = head + [("ffn", i, 2), ("ple", i)]
        if i + 1 < DEPTH:
            subs += [("ffn", i + 1, 1), ("normout", i + 1)]
        plan.append(("chain", [0, 1], subs))
    return plan


_NC_CACHE = {}


def kernel(x, p, ffn1_norm, ffn1_w_gu, ffn1_w_down, mix_norm, ffn2_norm, ffn2_w_gu, ffn2_w_down,
           ple_norm, ple_w_gate, ple_w_proj, ple_post_norm,
           ssd_w_in, ssd_conv_w, ssd_conv_b, ssd_dt_bias, ssd_a_log, ssd_d, ssd_norm, ssd_w_out,
           na_w_qkv, na_q_norm, na_k_norm, na_rpb, na_w_out):
    f32 = lambda a: np.ascontiguousarray(np.asarray(a, dtype=np.float32))
    x = np.asarray(x, dtype=np.float32)
    p = np.asarray(p, dtype=np.float32)
    B = x.shape[0]
    shared = {}
    gl = {"ffn1_norm": ffn1_norm, "mix_norm": mix_norm, "ffn2_norm": ffn2_norm, "ple_norm": ple_norm, "ple_post_norm": ple_post_norm}
    gains = np.stack([np.asarray(gl[nm], np.float32)[l] for nm in GAIN_NAMES for l in range(DEPTH)])
    shared["gains"] = np.ascontiguousarray(gains.reshape(len(GAIN_NAMES) * DEPTH, KD, 128).transpose(2, 0, 1))
    shared["ident"] = np.eye(128, dtype=np.float32)
    shared["identf"] = np.eye(128, dtype=np.float32)
    shared["ssd_consts"] = host_ssd_consts()
    shared["na_gain"] = host_na_gain(np.asarray(na_q_norm, np.float32), np.asarray(na_k_norm, np.float32))
    wl = {"ffn1_w_gu": ffn1_w_gu, "ffn1_w_down": ffn1_w_down, "ffn2_w_gu": ffn2_w_gu, "ffn2_w_down": ffn2_w_down,
          "ple_w_gate": ple_w_gate, "ple_w_proj": ple_w_proj, "na_w_qkv": na_w_qkv, "na_w_out": na_w_out,
          "ssd_w_in": ssd_w_in, "ssd_w_out": ssd_w_out}
    for nm, arr in wl.items():
        arr = np.asarray(arr, np.float32)
        for i in range(arr.shape[0]):
            shared[f"{nm}{i}"] = f32(arr[i])
    rpb = np.asarray(na_rpb, np.float32)
    for l in range(2):
        shared[f"na_bias_g{l}"] = host_bias_g(rpb[l])
        sp = host_ssd_params(np.asarray(ssd_conv_w, np.float32)[l], np.asarray(ssd_conv_b, np.float32)[l],
                             np.asarray(ssd_dt_bias, np.float32)[l], np.asarray(ssd_a_log, np.float32)[l],
                             np.asarray(ssd_d, np.float32)[l], np.asarray(ssd_norm, np.float32)[l])
        for k, v in sp.items():
            shared[f"{k}{l}"] = v
    in_maps = []
    for b in range(B):
        m = dict(shared)
        m["xT"] = np.ascontiguousarray(x[b].T)
        for i in range(DEPTH):
            m[f"pT{i}"] = np.ascontiguousarray(p[i, b].T)
        in_maps.append(m)
    if "nc" not in _NC_CACHE:
        _NC_CACHE["nc"] = build_nc(full_plan())
    nc = _NC_CACHE["nc"]
    res = run_bass_kernel_spmd(nc, in_maps, core_ids=list(range(B)))
    out = np.stack([np.ascontiguousarray(np.asarray(r["outT"], dtype=np.float32).T) for r in res.results])
    return out
```

```python
import contextlib
import os
import numpy as np
import concourse.bass as bass
import concourse.mybir as mybir
from concourse.bass_utils import run_bass_kernel_spmd

F32 = mybir.dt.float32
BF16 = mybir.dt.bfloat16
AF = mybir.ActivationFunctionType
ALU = mybir.AluOpType
AX = mybir.AxisListType

D = 1024
SEQ = 4096
DEPTH = 4
DFF = 2816
NFF = DFF // 128
DPLE = 256
EPS = 1e-6
TH = 2048
NTT = TH // 512
KD = D // 128


class Buf:
    __slots__ = ("ap", "writers", "readers", "name")

    def __init__(self, ap, name=""):
        self.ap = ap
        self.writers = []
        self.readers = []
        self.name = name


class Op:
    __slots__ = ("eng", "fn", "seq", "deps", "ddeps", "dsem", "dcount", "target", "rank")

    def __init__(self, eng, fn):
        self.eng = eng
        self.fn = fn
        self.deps = {}
        self.ddeps = {}
        self.dsem = None
        self.dcount = 0
        self.target = False
        self.rank = 0


def I(name, *args, **kwargs):
    return lambda e: getattr(e, name)(*args, **kwargs)


COMPUTE = ("pe", "act", "dve", "pool")
ENGS = ("pe", "act", "dve", "pool", "sp")
SAME_ENG_WINDOW = int(os.environ.get("SEW", "2"))
SEM_SEG = 30000


class Sched:
    def __init__(self, nc, stack):
        self.nc = nc
        self.stack = stack
        self.ops = {e: [] for e in ENGS}
        self.dma_sems = []
        self.dma_counts = []
        self.n_ops = 0

    def new_dsem(self, name):
        if not hasattr(self, "_dsem_names"):
            self._dsem_names = {}
        if name in self._dsem_names:
            return self._dsem_names[name]
        self._dsem_names[name] = len(self.dma_sems)
        s = self.stack.enter_context(self.nc.semaphore(name))
        self.dma_sems.append(s)
        self.dma_counts.append(0)
        return len(self.dma_sems) - 1

    def _dep_on(self, op, ref):
        e, seq, dsem = ref
        if dsem is not None:
            op.ddeps[dsem] = self.dma_counts[dsem]
        else:
            if op.deps.get(e, -1) < seq:
                op.deps[e] = seq

    def emit(self, eng, fn, reads=(), writes=(), dsem=None):
        op = Op(eng, fn)
        op.seq = len(self.ops[eng])
        for b in reads:
            for r in b.writers:
                self._dep_on(op, r)
        for b in writes:
            for r in b.readers:
                self._dep_on(op, r)
            for r in b.writers:
                self._dep_on(op, r)
        if dsem is not None:
            self.dma_counts[dsem] += 1
            op.dsem = dsem
            op.dcount = self.dma_counts[dsem]
        ref = (eng, op.seq, dsem)
        wset = set(id(b) for b in writes)
        for b in writes:
            if b.readers:
                b.writers = [ref]
                b.readers = []
            else:
                b.writers.append(ref)
                if len(b.writers) > 64:
                    b.writers = self._prune(b.writers)
        for b in reads:
            if id(b) not in wset:
                b.readers.append(ref)
                if len(b.readers) > 64:
                    b.readers = self._prune(b.readers)
        self.ops[eng].append(op)
        self.n_ops += 1
        return op

    @staticmethod
    def _prune(refs):
        best = {}
        out = []
        for (e, seq, dsem) in refs:
            if dsem is not None:
                k = ("d", dsem)
                best[k] = (e, seq, dsem)
            else:
                k = ("c", e)
                if k not in best or best[k][1] < seq:
                    best[k] = (e, seq, dsem)
        return list(best.values())

    def barrier(self):
        last = {}
        for e in COMPUTE:
            for i in range(len(self.ops[e]) - 1, -1, -1):
                if self.ops[e][i].fn is not None and self.ops[e][i].dsem is None:
                    last[e] = i
                    break
        dcounts = list(self.dma_counts)
        self._pending_barrier = (last, dcounts)
        for e in ENGS:
            op = Op(e, None)
            op.seq = len(self.ops[e])
            for e2, s in last.items():
                if e2 != e:
                    op.deps[e2] = s
            for i, c in enumerate(dcounts):
                if c > 0:
                    op.ddeps[i] = c
            self.ops[e].append(op)

    def replay(self):
        nc = self.nc
        for e in ENGS:
            for op in self.ops[e]:
                for e2, s in op.deps.items():
                    if e2 == e:
                        if e == "pe" or e == "sp":
                            continue
                        if op.seq - s > SAME_ENG_WINDOW:
                            continue
                    self.ops[e2][s].target = True
        esems = {}
        for e in COMPUTE:
            n = 0
            for op in self.ops[e]:
                if op.target:
                    n += 1
                op.rank = n
            nseg = n // SEM_SEG + 1
            esems[e] = [self.stack.enter_context(nc.semaphore(f"es_{e}_{i}")) for i in range(nseg)]
        self.esems = esems
        engobj = {"pe": "tensor", "act": "scalar", "dve": "vector", "pool": "gpsimd", "sp": "sync"}
        sched = self

        def run(e, eng):
            waited = {}
            for op in sched.ops[e]:
                for e2, s in op.deps.items():
                    if e2 == e:
                        if e == "pe" or e == "sp":
                            continue
                        if op.seq - s > SAME_ENG_WINDOW:
                            continue
                    r = sched.ops[e2][s].rank
                    seg = (r - 1) // SEM_SEG
                    val = r - seg * SEM_SEG
                    key = (e2, seg)
                    if waited.get(key, 0) >= val:
                        continue
                    waited[key] = val
                    eng.wait_ge(esems[e2][seg], val)
                for di, c in op.ddeps.items():
                    key = ("d", di)
                    if waited.get(key, 0) >= c:
                        continue
                    waited[key] = c
                    eng.wait_ge(sched.dma_sems[di], 16 * c)
                if op.fn is None:
                    continue
                ins = op.fn(eng)
                if op.dsem is not None:
                    ins.then_inc(sched.dma_sems[op.dsem], 16)
                elif op.target:
                    seg = (op.rank - 1) // SEM_SEG
                    ins.then_inc(esems[e][seg], 1)

        with nc.Block() as block:
            @block.tensor
            def _(eng):
                run("pe", eng)

            @block.scalar
            def _(eng):
                run("act", eng)

            @block.vector
            def _(eng):
                run("dve", eng)

            @block.gpsimd
            def _(eng):
                run("pool", eng)

            @block.sync
            def _(eng):
                run("sp", eng)


SBUF_BASE = 16384
SBUF_TOP = 229376


class Arena:
    def __init__(self, top):
        self.top = top


class Ctx:
    def __init__(self, nc, stack):
        self.nc = nc
        self.stack = stack
        self.s = Sched(nc, stack)
        self.ps_t = stack.enter_context(nc.psum_tensor("ps_all", [128, 8 * 512], F32))
        self.psum = [Buf(self.ps_t[:, i * 512:(i + 1) * 512], f"ps{i}") for i in range(8)]
        self._n = 0

    def sb(self, stack, shape, dt, name=None):
        self._n += 1
        nbytes = int(np.prod(shape[1:])) * (2 if dt == BF16 else 4)
        nbytes = (nbytes + 63) // 64 * 64
        off = stack.top
        stack.top += nbytes
        assert stack.top <= SBUF_TOP, f"SBUF arena overflow {stack.top}"
        return self.nc.alloc_sbuf_tensor_at(f"{name or 't'}_{self._n}", list(shape), dt, offset=off)

    def dram(self, name, shape, dt, kind="Internal"):
        return self.nc.dram_tensor(name, list(shape), dt, kind=kind)


def rmsnorm_T(cx, st, hres_b, hres_t, gain_ap, xn_t, xn_b, rstd_t, rstd_b, sq_t, sq_b, ones_t, ones_b, ps_ids):
    s = cx.s
    for tt in range(NTT):
        ps = cx.psum[ps_ids[tt % len(ps_ids)]]
        tsl = slice(tt * 512, (tt + 1) * 512)
        for d in range(KD):
            sq = sq_b[(tt * KD + d) % len(sq_b)]
            sqt = sq_t[(tt * KD + d) % len(sq_b)]
            s.emit("act", I("activation", out=sqt[:, :], in_=hres_t[:, d, tsl], func=AF.Square),
                   reads=[hres_b[d][tt]], writes=[sq])
            s.emit("pe", I("matmul", ps.ap[:, :], lhsT=ones_t[:, :], rhs=sqt[:, :], start=(d == 0), stop=(d == KD - 1)),
                   reads=[sq, ones_b], writes=[ps])
        s.emit("act", I("activation", out=rstd_t[:, tsl], in_=ps.ap[:, :], func=AF.Ln, bias=cx.eps_t[:, 0:1], scale=1.0),
               reads=[ps], writes=[rstd_b[tt]])
        s.emit("act", I("activation", out=rstd_t[:, tsl], in_=rstd_t[:, tsl], func=AF.Exp, scale=-0.5),
               reads=[rstd_b[tt]], writes=[rstd_b[tt]])
        for d in range(KD):
            s.emit("dve", I("scalar_tensor_tensor", out=xn_t[:, d, tsl], in0=hres_t[:, d, tsl], scalar=gain_ap[:, d:d + 1], in1=rstd_t[:, tsl], op0=ALU.mult, op1=ALU.mult),
                   reads=[hres_b[d][tt], rstd_b[tt]], writes=[xn_b[d][tt]])


def build_chain_phase(cx, src_dram, hT_dram, half_list, sublayers, W):
    nc = cx.nc
    s = cx.s
    if True:
        st = Arena(cx.const_top)
        hres_t = cx.sb(st, [128, KD, TH], F32, "hres")
        xn_t = cx.sb(st, [128, KD, TH], BF16, "xn")
        rstd_t = cx.sb(st, [128, TH], F32, "rstd")
        NSQ = 2
        sq_t = [cx.sb(st, [128, 512], BF16, "sq") for _ in range(NSQ)]
        G = 2
        act_t = [cx.sb(st, [128, G, TH], BF16, "act") for _ in range(2)]
        wgu_t = [cx.sb(st, [128, KD, 2, G * 128], BF16, "wgu") for _ in range(2)]
        wd_t = [cx.sb(st, [128, G, D], BF16, "wd") for _ in range(2)]
        sg_t = [cx.sb(st, [128, 512], F32, "sg") for _ in range(2)]
        wgate_t = cx.sb(st, [128, KD, D], BF16, "wgate")
        wproj_t = cx.sb(st, [128, 2, D], BF16, "wproj")
        pT_t = cx.sb(st, [128, 2, TH], BF16, "pT")
        proj_t = cx.sb(st, [128, KD, 512], F32, "proj")
        gate_t = [cx.sb(st, [128, 512], F32, "gate") for _ in range(2)]
        tmp_t = [cx.sb(st, [128, 512], F32, "tmp") for _ in range(2)]

        hres_b = [[Buf(None, f"hres{d}_{tt}") for tt in range(NTT)] for d in range(KD)]
        xn_b = [[Buf(None) for tt in range(NTT)] for d in range(KD)]
        rstd_b = [Buf(None) for tt in range(NTT)]
        sq_b = [Buf(None) for _ in range(NSQ)]
        act_b = [[[Buf(None) for tt in range(NTT)] for g in range(G)] for _ in range(2)]
        wgu_b = [Buf(None) for _ in range(2)]
        wd_b = [Buf(None) for _ in range(2)]
        sg_b = [Buf(None) for _ in range(2)]
        wgate_b = Buf(None)
        wproj_b = Buf(None)
        pT_b = Buf(None)
        proj_b = [Buf(None) for d in range(KD)]
        gate_b = [Buf(None) for _ in range(2)]
        tmp_b = [Buf(None) for _ in range(2)]
        ones_t, ones_b = W["ones_t"], W["ones_b"]
        gains_t, gains_b = W["gains_t"], W["gains_b"]

        ds_h = [s.new_dsem(f"dh{d}") for d in range(2)]
        ds_wgu = [s.new_dsem(f"dwgu{i}") for i in range(2)]
        ds_wd = [s.new_dsem(f"dwd{i}") for i in range(2)]
        ds_misc = s.new_dsem("dmisc")
        ds_st = s.new_dsem("dst")
        hT_b = W["hT_b"]

        for half in half_list:
            t0 = half * TH
            for d in range(KD):
                s.emit("sp", I("dma_start", out=hres_t[:, d, :], in_=src_dram[d * 128:(d + 1) * 128, t0:t0 + TH]),
                       reads=[hT_b[(d, half)]], writes=hres_b[d], dsem=ds_h[d % 2])
            stored = [False]

            def store_hres():
                if stored[0]:
                    return
                stored[0] = True
                for d in range(KD):
                    s.emit("sp", I("dma_start", out=hT_dram[d * 128:(d + 1) * 128, t0:t0 + TH], in_=hres_t[:, d, :]),
                           reads=hres_b[d], writes=[hT_b[(d, half)]], dsem=ds_st)

            for si, sub in enumerate(sublayers):
                if sub[0] == "normout":
                    _, layer, dst = sub
                    if si == len(sublayers) - 1:
                        store_hres()
                    gi = W["gain_idx"][("mix_norm", layer)]
                    rmsnorm_T(cx, st, hres_b, hres_t, gains_t[:, gi, :], xn_t, xn_b, rstd_t, rstd_b, sq_t, sq_b, ones_t, ones_b, [6, 7])
                    for d in range(KD):
                        s.emit("sp", I("dma_start", out=dst[d * 128:(d + 1) * 128, t0:t0 + TH], in_=xn_t[:, d, :]),
                               reads=xn_b[d], writes=[W["scr_b"]], dsem=ds_st)
                elif sub[0] == "proj":
                    _, srcT, w_ap, k0 = sub
                    w_v = w_ap.rearrange("(kc p) c -> p kc c", p=128)
                    s.emit("pool", I("dma_start", out=wgate_t[:, :, :], in_=w_v[:, k0:k0 + KD, :]), writes=[wgate_b], dsem=ds_misc)
                    for d in range(KD):
                        s.emit("sp", I("dma_start", out=xn_t[:, d, :], in_=srcT[(k0 + d) * 128:(k0 + d + 1) * 128, t0:t0 + TH]),
                               reads=[W["scr_b"]], writes=xn_b[d], dsem=ds_h[d % 2])
                    for d in range(KD):
                        for tt in range(NTT):
                            tsl = slice(tt * 512, (tt + 1) * 512)
                            pa = cx.psum[4 + (d * NTT + tt) % 2]
                            for k in range(KD):
                                s.emit("pe", I("matmul", pa.ap[:, :], lhsT=wgate_t[:, k, d * 128:(d + 1) * 128], rhs=xn_t[:, k, tsl], start=(k == 0), stop=(k == KD - 1)),
                                       reads=[wgate_b, xn_b[k][tt]], writes=[pa])
                            s.emit("dve", I("tensor_tensor", out=hres_t[:, d, tsl], in0=hres_t[:, d, tsl], in1=pa.ap[:, :], op=ALU.add),
                                   reads=[pa, hres_b[d][tt]], writes=[hres_b[d][tt]])
                elif sub[0] == "proj_ssd":
                    _, srcT, w_ap, ng_t = sub
                    w_v = w_ap.rearrange("(kc p) c -> p kc c", p=128)
                    xn16 = xn_t[:, :, :].rearrange("p a (b t) -> p (a b) t", b=2)
                    src_v = srcT.rearrange("(kc p) t -> p kc t", p=128)
                    for qtr in range(2):
                        q0 = t0 + qtr * 1024
                        allx = [xn_b[kc // 2][(kc % 2) * 2 + t2] for kc in range(16) for t2 in range(2)]
                        for kh in range(2):
                            s.emit("sp", I("dma_start", out=xn16[:, kh * 8:(kh + 1) * 8, :], in_=src_v[:, kh * 8:(kh + 1) * 8, q0:q0 + 1024]),
                                   reads=[W["scr_b"]], writes=allx, dsem=ds_h[kh])
                        for t2 in range(2):
                            tt = qtr * 2 + t2
                            lsl = slice(t2 * 512, (t2 + 1) * 512)
                            tsl = slice(tt * 512, (tt + 1) * 512)
                            pss = cx.psum[6 + t2]
                            for kc in range(16):
                                xb = xn_b[kc // 2][(kc % 2) * 2 + t2]
                                sq = sq_b[kc % NSQ]
                                sqt = sq_t[kc % NSQ]
                                s.emit("act", I("activation", out=sqt[:, :], in_=xn16[:, kc, lsl], func=AF.Square), reads=[xb], writes=[sq])
                                s.emit("pe", I("matmul", pss.ap[:, :], lhsT=W["ones2k_t"][:, :], rhs=sqt[:, :], start=(kc == 0), stop=(kc == 15)), reads=[sq, ones_b], writes=[pss])
                            s.emit("act", I("activation", out=rstd_t[:, tsl], in_=pss.ap[:, :], func=AF.Ln, bias=cx.eps_t[:, 0:1], scale=1.0), reads=[pss], writes=[rstd_b[tt]])
                            s.emit("act", I("activation", out=rstd_t[:, tsl], in_=rstd_t[:, tsl], func=AF.Exp, scale=-0.5), reads=[rstd_b[tt]], writes=[rstd_b[tt]])
                            for kc in range(16):
                                xb = xn_b[kc // 2][(kc % 2) * 2 + t2]
                                s.emit("dve", I("scalar_tensor_tensor", out=xn16[:, kc, lsl], in0=xn16[:, kc, lsl], scalar=ng_t[:, kc:kc + 1], in1=rstd_t[:, tsl], op0=ALU.mult, op1=ALU.mult),
                                       reads=[xb, rstd_b[tt], W["ng_b"]], writes=[xb])
                        for kh in range(2):
                            s.emit("pool", I("dma_start", out=wgate_t[:, :, :], in_=w_v[:, kh * 8:(kh + 1) * 8, :]), writes=[wgate_b], dsem=ds_misc)
                            for d in range(KD):
                                for t2 in range(2):
                                    tt = qtr * 2 + t2
                                    lsl = slice(t2 * 512, (t2 + 1) * 512)
                                    tsl = slice(tt * 512, (tt + 1) * 512)
                                    pa = cx.psum[4 + (d * 2 + t2) % 2]
                                    for k in range(KD):
                                        kc = kh * 8 + k
                                        xb = xn_b[kc // 2][(kc % 2) * 2 + t2]
                                        s.emit("pe", I("matmul", pa.ap[:, :], lhsT=wgate_t[:, k, d * 128:(d + 1) * 128], rhs=xn16[:, kc, lsl], start=(k == 0), stop=(k == KD - 1)),
                                               reads=[wgate_b, xb], writes=[pa])
                                    s.emit("dve", I("tensor_tensor", out=hres_t[:, d, tsl], in0=hres_t[:, d, tsl], in1=pa.ap[:, :], op=ALU.add),
                                           reads=[pa, hres_b[d][tt]], writes=[hres_b[d][tt]])
                elif sub[0] == "ffn":
                    _, layer, which = sub
                    w_gu = W[f"ffn{which}_w_gu"][layer]
                    w_dn = W[f"ffn{which}_w_down"][layer]
                    gi = W["gain_idx"][(f"ffn{which}_norm", layer)]
                    rmsnorm_T(cx, st, hres_b, hres_t, gains_t[:, gi, :], xn_t, xn_b, rstd_t, rstd_b, sq_t, sq_b, ones_t, ones_b, [6, 7])
                    ngrp = NFF // G
                    w_gu_v = w_gu.rearrange("(kc p) c -> p kc c", p=128)
                    w_dn_v = w_dn.rearrange("(c p) d -> p c d", p=128)

                    def phaseA(gi_, units):
                        bi = gi_ % 2
                        c0 = gi_ * G * 128
                        s.emit("pool", I("dma_start", out=wgu_t[bi][:, :, 0, :], in_=w_gu_v[:, :, c0:c0 + G * 128]),
                               writes=[wgu_b[bi]], dsem=ds_wgu[bi])
                        s.emit("pool", I("dma_start", out=wgu_t[bi][:, :, 1, :], in_=w_gu_v[:, :, DFF + c0:DFF + c0 + G * 128]),
                               writes=[wgu_b[bi]], dsem=ds_wgu[bi])
                        s.emit("pool", I("dma_start", out=wd_t[bi][:, :, :], in_=w_dn_v[:, gi_ * G:(gi_ + 1) * G, :]),
                               writes=[wd_b[bi]], dsem=ds_wd[bi])
                        for g in range(G):
                            for tt in range(NTT):
                                units.append((gi_, g, tt))

                    def unitA(gi_, g, tt, bq=()):
                        bi = gi_ % 2
                        if True:
                            if True:
                                tsl = slice(tt * 512, (tt + 1) * 512)
                                pg = cx.psum[(g * NTT + tt) % 2]
                                pu = cx.psum[2 + (g * NTT + tt) % 2]
                                for k in range(KD):
                                    s.emit("pe", I("matmul", pg.ap[:, :], lhsT=wgu_t[bi][:, k, 0, g * 128:(g + 1) * 128], rhs=xn_t[:, k, tsl], start=(k == 0), stop=(k == KD - 1)),
                                           reads=[wgu_b[bi], xn_b[k][tt]], writes=[pg])
                                    if k % 4 == 3 and bq:
                                        unitB(*bq.pop(0))
                                for k in range(KD):
                                    s.emit("pe", I("matmul", pu.ap[:, :], lhsT=wgu_t[bi][:, k, 1, g * 128:(g + 1) * 128], rhs=xn_t[:, k, tsl], start=(k == 0), stop=(k == KD - 1)),
                                           reads=[wgu_b[bi], xn_b[k][tt]], writes=[pu])
                                    if k % 4 == 3 and bq:
                                        unitB(*bq.pop(0))
                                sgi = (g * NTT + tt) % 2
                                s.emit("act", I("activation", out=sg_t[sgi][:, :], in_=pg.ap[:, :], func=AF.Silu),
                                       reads=[pg], writes=[sg_b[sgi]])
                                s.emit("dve", I("tensor_tensor", out=act_t[bi][:, g, tsl], in0=sg_t[sgi][:, :], in1=pu.ap[:, :], op=ALU.mult),
                                       reads=[pu, sg_b[sgi]], writes=[act_b[bi][g][tt]])

                    def unitB(gi_, d, tt):
                        bi = gi_ % 2
                        if True:
                            if True:
                                tsl = slice(tt * 512, (tt + 1) * 512)
                                pa = cx.psum[4 + (d * NTT + tt) % 4]
                                for g in range(G):
                                    s.emit("pe", I("matmul", pa.ap[:, :], lhsT=wd_t[bi][:, g, d * 128:(d + 1) * 128], rhs=act_t[bi][:, g, tsl], start=(g == 0), stop=(g == G - 1)),
                                           reads=[wd_b[bi], act_b[bi][g][tt]], writes=[pa])
                                s.emit("dve", I("scalar_tensor_tensor", out=hres_t[:, d, tsl], in0=pa.ap[:, :], scalar=0.5, in1=hres_t[:, d, tsl], op0=ALU.mult, op1=ALU.add),
                                       reads=[pa, hres_b[d][tt]], writes=[hres_b[d][tt]])

                    ua = []
                    phaseA(0, ua)
                    for u in ua:
                        unitA(*u)
                    for gi_ in range(1, ngrp + 1):
                        ua = []
                        if gi_ < ngrp:
                            phaseA(gi_, ua)
                        ub = [(gi_ - 1, d, tt) for d in range(KD) for tt in range(NTT)]
                        for u in ua:
                            unitA(*u, bq=ub)
                        while ub:
                            unitB(*ub.pop(0))
                else:
                    _, layer = sub
                    gi = W["gain_idx"][("ple_norm", layer)]
                    gpi = W["gain_idx"][("ple_post_norm", layer)]
                    rmsnorm_T(cx, st, hres_b, hres_t, gains_t[:, gi, :], xn_t, xn_b, rstd_t, rstd_b, sq_t, sq_b, ones_t, ones_b, [6, 7])
                    wg_v = W["ple_w_gate"][layer].rearrange("(kc p) c -> p kc c", p=128)
                    wp_v = W["ple_w_proj"][layer].rearrange("(kc p) c -> p kc c", p=128)
                    pT_v = W["pT"][layer].rearrange("(kc p) t -> p kc t", p=128)
                    s.emit("pool", I("dma_start", out=wgate_t[:, :, :], in_=wg_v), writes=[wgate_b], dsem=ds_misc)
                    s.emit("pool", I("dma_start", out=wproj_t[:, :, :], in_=wp_v), writes=[wproj_b], dsem=ds_misc)
                    s.emit("pool", I("dma_start", out=pT_t[:, :, :], in_=pT_v[:, :, t0:t0 + TH]), writes=[pT_b], dsem=ds_misc)
                    for tt in range(NTT):
                        tsl = slice(tt * 512, (tt + 1) * 512)
                        pss = cx.psum[6 + tt % 2]
                        for d in range(KD):
                            pp = cx.psum[d % 2]
                            for k in range(2):
                                s.emit("pe", I("matmul", pp.ap[:, :], lhsT=wproj_t[:, k, d * 128:(d + 1) * 128], rhs=pT_t[:, k, tsl], start=(k == 0), stop=(k == 1)),
                                       reads=[wproj_b, pT_b], writes=[pp])
                            s.emit("act", I("activation", out=proj_t[:, d, :], in_=pp.ap[:, :], func=AF.Copy),
                                   reads=[pp], writes=[proj_b[d]])
                            sq = sq_b[d % NSQ]
                            sqt = sq_t[d % NSQ]
                            s.emit("act", I("activation", out=sqt[:, :], in_=proj_t[:, d, :], func=AF.Square),
                                   reads=[proj_b[d]], writes=[sq])
                            s.emit("pe", I("matmul", pss.ap[:, :], lhsT=ones_t[:, :], rhs=sqt[:, :], start=(d == 0), stop=(d == KD - 1)),
                                   reads=[sq, ones_b], writes=[pss])
                        s.emit("act", I("activation", out=rstd_t[:, tsl], in_=pss.ap[:, :], func=AF.Ln, bias=cx.eps_t[:, 0:1], scale=1.0),
                               reads=[pss], writes=[rstd_b[tt]])
                        s.emit("act", I("activation", out=rstd_t[:, tsl], in_=rstd_t[:, tsl], func=AF.Exp, scale=-0.5),
                               reads=[rstd_b[tt]], writes=[rstd_b[tt]])
                        for d in range(KD):
                            pgt = cx.psum[2 + d % 2]
                            for k in range(KD):
                                s.emit("pe", I("matmul", pgt.ap[:, :], lhsT=wgate_t[:, k, d * 128:(d + 1) * 128], rhs=xn_t[:, k, tsl], start=(k == 0), stop=(k == KD - 1)),
                                       reads=[wgate_b, xn_b[k][tt]], writes=[pgt])
                            gb = d % 2
                            s.emit("act", I("activation", out=gate_t[gb][:, :], in_=pgt.ap[:, :], func=AF.Sigmoid),
                                   reads=[pgt], writes=[gate_b[gb]])
                            s.emit("dve", I("scalar_tensor_tensor", out=tmp_t[gb][:, :], in0=proj_t[:, d, :], scalar=gains_t[:, gpi, d:d + 1], in1=rstd_t[:, tsl], op0=ALU.mult, op1=ALU.mult),
                                   reads=[proj_b[d], rstd_b[tt]], writes=[tmp_b[gb]])
                            s.emit("dve", I("tensor_tensor", out=tmp_t[gb][:, :], in0=tmp_t[gb][:, :], in1=gate_t[gb][:, :], op=ALU.mult),
                                   reads=[tmp_b[gb], gate_b[gb]], writes=[tmp_b[gb]])
                            s.emit("dve", I("tensor_tensor", out=hres_t[:, d, tsl], in0=hres_t[:, d, tsl], in1=tmp_t[gb][:, :], op=ALU.add),
                                   reads=[tmp_b[gb], hres_b[d][tt]], writes=[hres_b[d][tt]])
            store_hres()
        s.barrier()


GAIN_NAMES = ["ffn1_norm", "mix_norm", "ffn2_norm", "ple_norm", "ple_post_norm"]
WEIGHT_SHAPES = (("ffn1_w_gu", [D, 2 * DFF], DEPTH), ("ffn1_w_down", [DFF, D], DEPTH), ("ffn2_w_gu", [D, 2 * DFF], DEPTH),
                 ("ffn2_w_down", [DFF, D], DEPTH), ("ple_w_gate", [D, D], DEPTH), ("ple_w_proj", [DPLE, D], DEPTH),
                 ("na_w_qkv", [D, 3 * D], 2), ("na_w_out", [D, D], 2), ("na_bias_g", [16, 128, 14, 256], 2),
                 ("ssd_w_in", [D, 8256], 2), ("ssd_w_out", [2048, D], 2), ("ssd_cw", [128, 48, 5], 2), ("ssd_cb", [128, 48], 2),
                 ("ssd_dtb", [64, 1], 2), ("ssd_alog", [64, 1], 2), ("ssd_dexp", [128, 16], 2), ("ssd_ng", [128, 16], 2))


def build_nc(plan, test_in=(), test_out=()):
    nc = bass.Bass("TRN2", target_bir_lowering=False)
    with contextlib.ExitStack() as stack:
        cx = Ctx(nc, stack)
        s = cx.s
        W = {}

        def scr(name, shape, dt):
            kind = "ExternalInput" if name in test_in else ("ExternalOutput" if name in test_out else "Internal")
            return nc.dram_tensor(name, list(shape), dt, kind=kind).ap()

        xT = nc.dram_tensor("xT", [D, SEQ], F32, kind="ExternalInput").ap()
        outT = nc.dram_tensor("outT", [D, SEQ], F32, kind="ExternalOutput").ap()
        W["pT"] = [nc.dram_tensor(f"pT{i}", [DPLE, SEQ], F32, kind="ExternalInput").ap() for i in range(DEPTH)]
        gains = nc.dram_tensor("gains", [128, len(GAIN_NAMES) * DEPTH, KD], F32, kind="ExternalInput").ap()
        ident_d = nc.dram_tensor("ident", [128, 128], F32, kind="ExternalInput").ap()
        nag_d = nc.dram_tensor("na_gain", [128, 2, 2], F32, kind="ExternalInput").ap()
        for nm, shp, n in WEIGHT_SHAPES:
            W[nm] = [nc.dram_tensor(f"{nm}{i}", shp, F32, kind="ExternalInput").ap() for i in range(n)]
        W["gain_idx"] = {(nm, l): gi * DEPTH + l for gi, nm in enumerate(GAIN_NAMES) for l in range(DEPTH)}
        W["xnT"] = scr("xnT", [D, SEQ], BF16)
        W["mixT"] = scr("mixT", [2 * D, SEQ], BF16)
        ar = Arena(SBUF_BASE)
        ones_t = cx.sb(ar, [128, 128], BF16, "ones")
        NG = len(GAIN_NAMES) * DEPTH
        gains_t = cx.sb(ar, [128, NG, KD], F32, "gains")
        W["ones_t"], W["ones_b"] = ones_t, Buf(None)
        W["gains_t"], W["gains_b"] = gains_t, Buf(None)
        ds_c = s.new_dsem("dconst")
        cb = W["ones_b"]
        s.emit("dve", I("memset", ones_t[:, :], 1.0 / D), writes=[cb])
        cx.eps_t = cx.sb(ar, [128, 1], F32, "eps")
        s.emit("dve", I("memset", cx.eps_t[:, :], EPS), writes=[cb])
        W["eps64_t"] = cx.sb(ar, [128, 1], F32, "eps64")
        s.emit("dve", I("memset", W["eps64_t"][:, :], 64 * EPS), writes=[cb])
        W["ident_t"], W["ident_b"] = cx.sb(ar, [128, 128], BF16, "ident"), cb
        W["bd1_t"] = cx.sb(ar, [128, 128], BF16, "bd1")
        W["bd64_t"] = cx.sb(ar, [128, 128], BF16, "bd64")
        W["bd_b"] = cb
        for t, v in ((W["bd1_t"], 1.0), (W["bd64_t"], 1.0 / 64)):
            s.emit("dve", I("memset", t[:, :], 0.0), writes=[cb])
            s.emit("dve", I("memset", t[0:64, 0:64], v), writes=[cb])
            s.emit("dve", I("memset", t[64:128, 64:128], v), writes=[cb])
        W["nag_t"], W["nag_b"] = cx.sb(ar, [128, 2, 2], F32, "nag"), cb
        s.emit("sp", I("dma_start", out=gains_t[:, :, :], in_=gains), writes=[cb], dsem=ds_c)
        s.emit("sp", I("dma_start", out=W["nag_t"][:, :, :], in_=nag_d), writes=[cb], dsem=ds_c)
        ds_c2 = s.new_dsem("dconst2")
        s.emit("pool", I("dma_start", out=W["ident_t"][:, :], in_=ident_d), writes=[cb], dsem=ds_c2)
        identf_d = nc.dram_tensor("identf", [128, 128], F32, kind="ExternalInput").ap()
        ssdc_d = nc.dram_tensor("ssd_consts", [128, 1280], F32, kind="ExternalInput").ap()
        W["identf_t"] = cx.sb(ar, [128, 128], F32, "identf")
        W["ssdc_t"] = cx.sb(ar, [128, 1280], BF16, "ssdc")
        W["ones1_t"] = cx.sb(ar, [128, 128], BF16, "ones1")
        W["one_t"] = cx.sb(ar, [128, 1], F32, "one")
        s.emit("dve", I("memset", W["ones1_t"][:, :], 1.0), writes=[cb])
        s.emit("dve", I("memset", W["one_t"][:, :], 1.0), writes=[cb])
        s.emit("sp", I("dma_start", out=W["identf_t"][:, :], in_=identf_d), writes=[cb], dsem=ds_c)
        s.emit("pool", I("dma_start", out=W["ssdc_t"][:, :], in_=ssdc_d), writes=[cb], dsem=ds_c2)
        W["ones2k_t"] = cx.sb(ar, [128, 128], BF16, "ones2k")
        s.emit("dve", I("memset", W["ones2k_t"][:, :], 1.0 / 2048), writes=[cb])
        W["ng_t"] = [cx.sb(ar, [128, 16], F32, "ssdng") for _ in range(2)]
        W["ng_b"] = cb
        for l in range(2):
            s.emit("sp", I("dma_start", out=W["ng_t"][l][:, :], in_=W["ssd_ng"][l]), writes=[cb], dsem=ds_c)
        W["zsT"] = scr("zsT", [2048, SEQ], BF16)
        W["xcT"] = scr("xcT", [6144, SEQ], BF16)
        W["scr2_b"] = Buf(None)
        cx.const_top = ar.top
        hT_b = {(d, half): Buf(None) for d in range(KD) for half in range(2)}
        W["hT_b"] = hT_b
        W["scr_b"] = Buf(None)
        s.barrier()
        first = True
        for ph in plan:
            if ph[0] == "chain":
                _, halves, subs = ph
                subs2 = []
                for sub in subs:
                    if sub[0] == "normout":
                        subs2.append(("normout", sub[1], W["xnT"]))
                    elif sub[0] == "proj_na":
                        subs2.append(("proj", W["mixT"], W["na_w_out"][sub[1]], 0))
                    elif sub[0] == "proj_ssd":
                        subs2.append(("proj_ssd", W["mixT"], W["ssd_w_out"][sub[1]], W["ng_t"][sub[1]]))
                    else:
                        subs2.append(sub)
                build_chain_phase(cx, xT if first else outT, outT, halves, subs2, W)
                first = False
            elif ph[0] == "na":
                build_na_phase(cx, W["xnT"], W["mixT"], ph[1], W)
            elif ph[0] == "ssd":
                build_ssd_phase(cx, W["xnT"], W["mixT"], W["zsT"], W["xcT"], ph[1], W)
        s.barrier()
        s.replay()
    return nc


NA_NE = 14


def na_tables():
    idx = np.zeros((128, NA_NE, 256), dtype=np.int64)
    PAD = 15 * 31
    ents = [(4, 4 - 4 + 2 * j) for j in range(6)] + [(0, 2 * j) for j in range(4)] + [(60, 56 + 2 * j) for j in range(4)]
    for e, (rb, a0) in enumerate(ents):
        for jrow in range(2):
            a = a0 + jrow
            for qr in range(4):
                r = rb + qr
                r0 = min(max(r - 4, 0), 56)
                vrow = (r0 <= a <= r0 + 7)
                rr = a - r + 7
                for c in range(64):
                    wc0 = min(max(c - 8, 0), 48)
                    for kc in range(64):
                        ok = vrow and (wc0 <= kc < wc0 + 16)
                        cr = kc - c + 15
                        idx[jrow * 64 + kc, e, qr * 64 + c] = (rr * 31 + cr) if ok else PAD
    return idx


def build_na_phase(cx, xnT, oT, j, W):
    s = cx.s
    st = Arena(cx.const_top)
    ps_t = cx.ps_t
    ps7_bf = ps_t[:, 7 * 512:8 * 512].bitcast(BF16)
    xn_t = cx.sb(st, [128, KD, SEQ], BF16, "na_xn")
    w_t = [cx.sb(st, [128, KD, 3, 128], BF16, "na_w") for _ in range(2)]
    qT_t = [cx.sb(st, [128, SEQ], BF16, "na_qT") for _ in range(2)]
    kT_t = [cx.sb(st, [128, SEQ], BF16, "na_kT") for _ in range(2)]
    vx_t = [cx.sb(st, [128, 32, 2, 66], BF16, "na_vx") for _ in range(2)]
    bias_t = [cx.sb(st, [128, NA_NE, 256], BF16, "na_bias") for _ in range(2)]
    P_t = [cx.sb(st, [128, 6 * 256], BF16, "na_P") for _ in range(2)]
    otok_t = cx.sb(st, [128, 32, 128], BF16, "na_otok")
    oT_t = [cx.sb(st, [128, SEQ], BF16, "na_oT") for _ in range(2)]
    qsb_t = [cx.sb(st, [128, 512], F32, "na_qsb") for _ in range(2)]
    sq_t = [cx.sb(st, [128, 512], BF16, "na_sq") for _ in range(2)]
    rs_t = [cx.sb(st, [128, 512], F32, "na_rs") for _ in range(2)]
    rec_t = [cx.sb(st, [128, 2], F32, "na_rec") for _ in range(2)]

    xn_b = [[Buf(None) for _ in range(8)] for _ in range(KD)]
    w_b = [Buf(None) for _ in range(2)]
    qT_b = [[Buf(None) for _ in range(8)] for _ in range(2)]
    kT_b = [[Buf(None) for _ in range(8)] for _ in range(2)]
    vx_b = [[Buf(None) for _ in range(8)] for _ in range(2)]
    vx1_b = [Buf(None) for _ in range(2)]
    bias_b = [Buf(None) for _ in range(2)]
    P_b = [Buf(None) for _ in range(2)]
    otok_b = [Buf(None) for _ in range(8)]
    oT_b = [Buf(None) for _ in range(2)]
    qsb_b = [Buf(None) for _ in range(2)]
    sq_b = [Buf(None) for _ in range(2)]
    rs_b = [Buf(None) for _ in range(2)]
    rec_b = [Buf(None) for _ in range(2)]
    O_b = [cx.psum[6], cx.psum[7]]
    ident_t, ident_b = W["ident_t"], W["ident_b"]
    bd1_t, bd64_t, bd_b = W["bd1_t"], W["bd64_t"], W["bd_b"]
    nag_t, nag_b = W["nag_t"], W["nag_b"]
    eps64_t = W["eps64_t"]

    ds_x = s.new_dsem("na_dx")
    ds_w = [s.new_dsem(f"na_dw{i}") for i in range(2)]
    ds_b = [s.new_dsem(f"na_db{i}") for i in range(2)]
    ds_o = [s.new_dsem(f"na_do{i}") for i in range(2)]

    xn_v = xnT.rearrange("(kc p) t -> p kc t", p=128)
    for k in range(KD):
        s.emit("sp", I("dma_start", out=xn_t[:, k, :], in_=xnT[k * 128:(k + 1) * 128, :]),
               reads=[W["scr_b"]], writes=xn_b[k], dsem=ds_x)
    for bi in range(2):
        s.emit("pool", I("memset", vx_t[bi][:, :, :, 64:66], 1.0), writes=[vx1_b[bi]])
    w_v = W["na_w_qkv"][j].rearrange("(kc p) c -> p kc c", p=128)
    bias_g = W["na_bias_g"][j]

    for c in range(KD):
        bi = c % 2
        for wi in range(3):
            s.emit("pool", I("dma_start", out=w_t[bi][:, :, wi, :], in_=w_v[:, :, wi * D + c * 128: wi * D + (c + 1) * 128]),
                   writes=[w_b[bi]], dsem=ds_w[bi])
        units = [(tt, wi) for tt in range(8) for wi in range(2)]
        bank_rr = [0]

        def proj(u):
            tt, wi = units[u]
            pb = cx.psum[bank_rr[0] % 6]
            bank_rr[0] += 1
            tsl = slice(tt * 512, (tt + 1) * 512)
            for k in range(KD):
                s.emit("pe", I("matmul", pb.ap[:, :], lhsT=w_t[bi][:, k, wi, :], rhs=xn_t[:, k, tsl], start=(k == 0), stop=(k == KD - 1)),
                       reads=[w_b[bi], xn_b[k][tt]], writes=[pb])
            x = u % 2
            s.emit("act", I("activation", out=sq_t[x][:, :], in_=pb.ap[:, :], func=AF.Square), reads=[pb], writes=[sq_b[x]])
            s.emit("act", I("activation", out=qsb_t[x][:, :], in_=pb.ap[:, :], func=AF.Copy), reads=[pb], writes=[qsb_b[x]])

        def norm(u):
            tt, wi = units[u]
            x = u % 2
            tsl = slice(tt * 512, (tt + 1) * 512)
            p7 = cx.psum[7]
            bd = bd1_t if wi == 0 else bd64_t
            ept = eps64_t if wi == 0 else cx.eps_t
            s.emit("pe", I("matmul", p7.ap[:, :], lhsT=bd[:, :], rhs=sq_t[x][:, :], start=True, stop=True), reads=[bd_b, sq_b[x]], writes=[p7])
            s.emit("act", I("activation", out=rs_t[x][:, :], in_=p7.ap[:, :], func=AF.Ln, bias=ept[:, 0:1], scale=1.0), reads=[p7], writes=[rs_b[x]])
            s.emit("act", I("activation", out=rs_t[x][:, :], in_=rs_t[x][:, :], func=AF.Exp, scale=-0.5), reads=[rs_b[x]], writes=[rs_b[x]])
            dst_t = qT_t if wi == 0 else kT_t
            dst_b = qT_b if wi == 0 else kT_b
            s.emit("dve", I("scalar_tensor_tensor", out=dst_t[bi][:, tsl], in0=qsb_t[x][:, :], scalar=nag_t[:, j, wi:wi + 1], in1=rs_t[x][:, :], op0=ALU.mult, op1=ALU.mult),
                   reads=[qsb_b[x], rs_b[x], nag_b], writes=[dst_b[bi][tt]])

        proj(0)
        for u in range(1, len(units)):
            proj(u)
            norm(u - 1)
        norm(len(units) - 1)
        for tg in range(8):
            pb = cx.psum[bank_rr[0] % 6]
            bank_rr[0] += 1
            for t4 in range(4):
                tile = tg * 4 + t4
                for k in range(KD):
                    s.emit("pe", I("matmul", pb.ap[:, t4 * 128:(t4 + 1) * 128], lhsT=xn_t[:, k, tile * 128:(tile + 1) * 128], rhs=w_t[bi][:, k, 2, :], start=(k == 0), stop=(k == KD - 1)),
                           reads=[w_b[bi], xn_b[k][tile // 4]], writes=[pb])
            s.emit("act", I("activation", out=vx_t[bi][:, tg * 4:(tg + 1) * 4, :, 0:64], in_=pb.ap[:, :].rearrange("p (t h d) -> p t h d", t=4, h=2), func=AF.Copy),
                   reads=[pb], writes=[vx_b[bi][tg]])

        iters = []
        for hh in range(2):
            for b in range(16):
                iters.append((hh, b))

        def geom(b):
            rb = 4 * b
            if b == 0:
                return rb, [(6 + jj, 2 * jj) for jj in range(4)]
            if b == 15:
                return rb, [(10 + jj, 56 + 2 * jj) for jj in range(4)]
            return rb, [(jj, rb - 4 + 2 * jj) for jj in range(6)]

        def S_stage(it):
            hh, b = iters[it]
            g = it % 2
            h = 2 * c + hh
            hb = h % 2
            if b == 0:
                s.emit("pool", I("dma_start", out=bias_t[hb][:, :, :], in_=bias_g[h]), writes=[bias_b[hb]], dsem=ds_b[hb])
            rb, tiles = geom(b)
            q0 = rb * 64
            for par2 in range(2):
                sel = [(jj, te) for jj, te in enumerate(tiles) if jj % 2 == par2]
                for jj, (ent, a0) in sel:
                    bank = cx.psum[3 * g + jj // 2]
                    reg = ps_t[:, 3 * g * 512 + jj * 256: 3 * g * 512 + (jj + 1) * 256]
                    k0 = a0 * 64
                    s.emit("pe", I("matmul", reg, lhsT=kT_t[bi][hh * 64:(hh + 1) * 64, k0:k0 + 128], rhs=qT_t[bi][hh * 64:(hh + 1) * 64, q0:q0 + 256], start=True, stop=False),
                           reads=[kT_b[bi][k0 // 512], kT_b[bi][(k0 + 127) // 512], qT_b[bi][q0 // 512]], writes=[bank])
                for jj, (ent, a0) in sel:
                    bank = cx.psum[3 * g + jj // 2]
                    reg = ps_t[:, 3 * g * 512 + jj * 256: 3 * g * 512 + (jj + 1) * 256]
                    s.emit("pe", I("matmul", reg, lhsT=ident_t[:, :], rhs=bias_t[hb][:, ent, :], start=False, stop=True),
                           reads=[bias_b[hb], ident_b], writes=[bank])
            nt = len(tiles)
            banks = [cx.psum[3 * g + x] for x in range((nt + 1) // 2)]
            s.emit("act", I("activation", out=P_t[g][:, 0:nt * 256], in_=ps_t[:, 3 * g * 512: 3 * g * 512 + nt * 256], func=AF.Exp),
                   reads=banks, writes=[P_b[g]])

        def PV_stage(it):
            hh, b = iters[it]
            g = it % 2
            rb, tiles = geom(b)
            nt = len(tiles)
            obase = (6 + g) * 512
            for t in range(2):
                oreg = ps_t[:, obase + t * 66: obase + t * 66 + 65]
                for jj, (ent, a0) in enumerate(tiles):
                    vt = a0 // 2
                    s.emit("pe", I("matmul", oreg, lhsT=P_t[g][:, jj * 256 + t * 128: jj * 256 + (t + 1) * 128], rhs=vx_t[bi][:, vt, hh, 0:65], start=(jj == 0), stop=(jj == nt - 1)),
                           reads=[P_b[g], vx_b[bi][vt // 4], vx1_b[bi]], writes=[O_b[g]])
            s.emit("dve", I("reciprocal", out=rec_t[g][:, :], in_=ps_t[:, obase:obase + 132].rearrange("p (t d) -> p t d", t=2)[:, :, 64]),
                   reads=[O_b[g]], writes=[rec_b[g]])
            for t in range(2):
                tile = rb // 2 + t
                s.emit("dve", I("tensor_scalar", out=otok_t[:, tile, hh * 64:(hh + 1) * 64], in0=ps_t[:, obase + t * 66: obase + t * 66 + 64], scalar1=rec_t[g][:, t:t + 1], scalar2=None, op0=ALU.mult),
                       reads=[O_b[g], rec_b[g]], writes=[otok_b[tile // 4]])

        S_stage(0)
        for it in range(1, len(iters)):
            S_stage(it)
            PV_stage(it - 1)
        PV_stage(len(iters) - 1)

        for tg in range(8):
            p7 = cx.psum[7]
            for t4 in range(4):
                tile = tg * 4 + t4
                s.emit("pe", I("transpose", out=ps7_bf[:, t4 * 128:(t4 + 1) * 128], in_=otok_t[:, tile, :], identity=ident_t[:, :]),
                       reads=[otok_b[tg], ident_b], writes=[p7])
            s.emit("act", I("activation", out=oT_t[bi][:, tg * 512:(tg + 1) * 512], in_=ps7_bf[:, 0:512], func=AF.Copy),
                   reads=[p7], writes=[oT_b[bi]])
        s.emit("sp", I("dma_start", out=oT[c * 128:(c + 1) * 128, :], in_=oT_t[bi][:, :]),
               reads=[oT_b[bi]], writes=[W["scr_b"]], dsem=ds_o[bi])
    s.barrier()


_NA_IDX = None


def host_bias_g(rpb):
    global _NA_IDX
    if _NA_IDX is None:
        _NA_IDX = na_tables()
    flat = np.concatenate([rpb.reshape(16, -1).astype(np.float32), np.full((16, 1), -30000.0, np.float32)], axis=1)
    return np.ascontiguousarray(flat[:, _NA_IDX])


def host_na_gain(qs, ks):
    out = np.zeros((128, 2, 2), np.float32)
    for l in range(2):
        out[:, l, 0] = np.tile(qs[l], 2)
        out[:, l, 1] = np.tile(ks[l], 2)
    return out


def dummy_inputs():
    ins = {"xT": np.zeros((D, SEQ), np.float32), "gains": np.ones((128, 20, KD), np.float32),
           "ident": np.eye(128, dtype=np.float32), "na_gain": np.ones((128, 2, 2), np.float32),
           "identf": np.eye(128, dtype=np.float32), "ssd_consts": host_ssd_consts()}
    for i in range(DEPTH):
        ins[f"pT{i}"] = np.zeros((DPLE, SEQ), np.float32)
    for nm, shp, n in WEIGHT_SHAPES:
        for i in range(n):
            ins[f"{nm}{i}"] = np.zeros(shp, np.float32)
    return ins


DI = 2048
NQ = 6
SSD_IN = 8256


def build_ssd_phase(cx, xnT, uT, zsT, xcT, j, W):
    s = cx.s
    ps_t = cx.ps_t
    base = Arena(cx.const_top)
    sc_t = cx.sb(base, [128, 32, NQ, 64], F32, "ssd_sc")
    sc_b = Buf(None)
    identf_t = W["identf_t"]
    ident_t, ident_b = W["ident_t"], W["ident_b"]
    cb = ident_b
    w_in = W["ssd_w_in"][j]
    w_v = w_in.rearrange("(kc p) c -> p kc c", p=128)
    st = Arena(base.top)
    xn_t = cx.sb(st, [128, KD, SEQ], BF16, "sa_xn")
    xn_b = [[Buf(None) for _ in range(8)] for _ in range(KD)]
    w_t = [cx.sb(st, [128, KD, 128], BF16, "sa_w") for _ in range(2)]
    w_b = [Buf(None) for _ in range(2)]
    cw_t = cx.sb(st, [128, 48, 5], F32, "sa_cw")
    cbias_t = cx.sb(st, [128, 48], F32, "sa_cb")
    dtb_t = cx.sb(st, [64, 1], F32, "sa_dtb")
    a_t = cx.sb(st, [64, 1], F32, "sa_a")
    prm_b = Buf(None)
    ds_x = s.new_dsem("sa_dx")
    ds_w = [s.new_dsem(f"sa_dw{i}") for i in range(2)]
    ds_p = s.new_dsem("sa_dp")
    ds_o = [s.new_dsem(f"sa_do{i}") for i in range(2)]
    for k in range(KD):
        s.emit("sp", I("dma_start", out=xn_t[:, k, :], in_=xnT[k * 128:(k + 1) * 128, :]),
               reads=[W["scr_b"]], writes=xn_b[k], dsem=ds_x)
    s.emit("sp", I("dma_start", out=cw_t[:, :, :], in_=W["ssd_cw"][j]), writes=[prm_b], dsem=ds_p)
    s.emit("sp", I("dma_start", out=cbias_t[:, :], in_=W["ssd_cb"][j]), writes=[prm_b], dsem=ds_p)
    s.emit("sp", I("dma_start", out=dtb_t[:, :], in_=W["ssd_dtb"][j]), writes=[prm_b], dsem=ds_p)
    s.emit("sp", I("dma_start", out=a_t[:, :], in_=W["ssd_alog"][j]), writes=[prm_b], dsem=ds_p)
    s.emit("act", I("activation", out=a_t[:, :], in_=a_t[:, :], func=AF.Exp), reads=[prm_b], writes=[prm_b])
    s.emit("dve", I("tensor_scalar", out=a_t[:, :], in0=a_t[:, :], scalar1=-1.0, scalar2=None, op0=ALU.mult), reads=[prm_b], writes=[prm_b])

    s.emit("pool", I("dma_start", out=w_t[0][:, :, 0:64], in_=w_v[:, :, 8192:8256]), writes=[w_b[0]], dsem=ds_w[0])
    sa = Arena(st.top)
    HT = 1024
    names = ["dt", "dtar", "cum", "X", "tmp", "dte", "E", "cd", "dtd", "negX", "mask", "e1"]
    F = {n: cx.sb(sa, [64, HT], F32, "sf_" + n) for n in names}
    dtab_t = cx.sb(sa, [64, HT], BF16, "sf_dtab")
    fb = {n: Buf(None) for n in names + ["dtab"]}
    s.emit("pool", I("memset", F["mask"][:, :], 1.0), writes=[fb["mask"]])
    s.emit("pool", I("memset", F["mask"][:, :].rearrange("p (c q) -> p c q", q=128)[:, :, 0:1], 0.0), writes=[fb["mask"]])
    for half in range(SEQ // HT):
        for t4 in range(HT // 512):
            tt = half * (HT // 512) + t4
            pb = cx.psum[tt % 4]
            tsl = slice(tt * 512, (tt + 1) * 512)
            lsl = slice(t4 * 512, (t4 + 1) * 512)
            for k in range(KD):
                s.emit("pe", I("matmul", pb.ap[0:64, :], lhsT=w_t[0][:, k, 0:64], rhs=xn_t[:, k, tsl], start=(k == 0), stop=(k == KD - 1)),
                       reads=[w_b[0], xn_b[k][tt]], writes=[pb])
            s.emit("act", I("activation", out=F["e1"][:, lsl], in_=pb.ap[0:64, :], func=AF.Exp, bias=dtb_t[:, 0:1], scale=1.0),
                   reads=[pb, prm_b], writes=[fb["e1"]])
        s.emit("act", I("activation", out=F["dt"][:, :], in_=F["e1"][:, :], func=AF.Ln, bias=W["one_t"][0:64, 0:1], scale=1.0),
               reads=[fb["e1"]], writes=[fb["dt"]])
        s.emit("dve", I("tensor_scalar", out=dtab_t[:, :], in0=F["dt"][:, :], scalar1=a_t[:, 0:1], scalar2=None, op0=ALU.mult),
               reads=[fb["dt"], prm_b], writes=[fb["dtab"]])
        s.emit("dve", I("tensor_copy", out=F["dtar"][:, :], in_=dtab_t[:, :]), reads=[fb["dtab"]], writes=[fb["dtar"]])
        s.emit("dve", I("tensor_tensor_scan", out=F["cum"][:, :], data0=F["mask"][:, :], data1=F["dtar"][:, :], initial=0.0, op0=ALU.mult, op1=ALU.add),
               reads=[fb["mask"], fb["dtar"]], writes=[fb["cum"]])
        cumv = F["cum"][:, :].rearrange("p (c q) -> p c q", q=128)
        totbc = cumv[:, :, 127:128].to_broadcast([64, HT // 128, 128])
        v3 = lambda n, lo=0, hi=64: F[n][lo:hi, :].rearrange("p (c q) -> p c q", q=128)
        s.emit("dve", I("tensor_copy", out=F["X"][0:32, :], in_=F["cum"][0:32, :]), reads=[fb["cum"]], writes=[fb["X"]])
        totbc_b = cumv[32:64, :, 127:128].to_broadcast([32, HT // 128, 128])
        s.emit("dve", I("tensor_tensor", out=v3("tmp", 32, 64), in0=totbc_b, in1=v3("cum", 32, 64), op=ALU.subtract), reads=[fb["cum"]], writes=[fb["tmp"]])
        s.emit("dve", I("tensor_tensor", out=F["X"][32:64, :], in0=F["tmp"][32:64, :], in1=F["dtar"][32:64, :], op=ALU.add), reads=[fb["tmp"], fb["dtar"]], writes=[fb["X"]])
        s.emit("dve", I("tensor_scalar", out=F["negX"][:, :], in0=F["X"][:, :], scalar1=-1.0, scalar2=None, op0=ALU.mult), reads=[fb["X"]], writes=[fb["negX"]])
        s.emit("act", I("activation", out=F["E"][:, :], in_=F["X"][:, :], func=AF.Exp), reads=[fb["X"]], writes=[fb["E"]])
        s.emit("dve", I("tensor_tensor", out=v3("tmp"), in0=totbc, in1=v3("X"), op=ALU.subtract), reads=[fb["cum"], fb["X"]], writes=[fb["tmp"]])
        s.emit("act", I("activation", out=F["dte"][:, :], in_=F["tmp"][:, :], func=AF.Exp), reads=[fb["tmp"]], writes=[fb["dte"]])
        s.emit("act", I("activation", out=v3("cd"), in_=totbc, func=AF.Exp), reads=[fb["cum"]], writes=[fb["cd"]])
        s.emit("dve", I("tensor_tensor", out=F["dtd"][:, :], in0=F["dt"][:, :], in1=F["dte"][:, :], op=ALU.mult), reads=[fb["dt"], fb["dte"]], writes=[fb["dtd"]])
        qn = ["dt", "dtd", "dtar", "negX", "E", "cd"]
        for cl in range(HT // 128):
            c = half * (HT // 128) + cl
            pb = cx.psum[4 + cl % 2]
            for qi, n in enumerate(qn):
                s.emit("pe", I("transpose", out=pb.ap[:, qi * 64:(qi + 1) * 64], in_=F[n][:, cl * 128:(cl + 1) * 128], identity=identf_t[0:64, 0:64]),
                       reads=[fb[n], cb], writes=[pb])
            s.emit("act", I("activation", out=sc_t[:, c, :, :], in_=pb.ap[:, 0:NQ * 64].rearrange("p (q h) -> p q h", q=NQ), func=AF.Copy),
                   reads=[pb], writes=[sc_b])
    s.barrier()

    sa = Arena(st.top)
    xe_t = [cx.sb(sa, [128, SEQ + 4], F32, "sa_xe") for _ in range(2)]
    acc_t = [cx.sb(sa, [128, SEQ], F32, "sa_acc") for _ in range(2)]
    ob_t = [cx.sb(sa, [128, SEQ], BF16, "sa_ob") for _ in range(2)]
    xe_b = [[Buf(None) for _ in range(8)] for _ in range(2)]
    xepad_b = [Buf(None) for _ in range(2)]
    acc_b = [Buf(None) for _ in range(2)]
    ob_b = [Buf(None) for _ in range(2)]
    for i in range(2):
        s.emit("pool", I("memset", xe_t[i][:, 0:2], 0.0), writes=[xepad_b[i]])
        s.emit("pool", I("memset", xe_t[i][:, SEQ + 2:SEQ + 4], 0.0), writes=[xepad_b[i]])
    for m in range(64):
        bi = m % 2
        s.emit("pool", I("dma_start", out=w_t[bi][:, :, :], in_=w_v[:, :, m * 128:(m + 1) * 128]), writes=[w_b[bi]], dsem=ds_w[bi])
        for tt in range(8):
            pb = cx.psum[(m * 8 + tt) % 6]
            tsl = slice(tt * 512, (tt + 1) * 512)
            for k in range(KD):
                s.emit("pe", I("matmul", pb.ap[:, :], lhsT=w_t[bi][:, k, :], rhs=xn_t[:, k, tsl], start=(k == 0), stop=(k == KD - 1)),
                       reads=[w_b[bi], xn_b[k][tt]], writes=[pb])
            if m < 16:
                s.emit("act", I("activation", out=ob_t[bi][:, tsl], in_=pb.ap[:, :], func=AF.Silu), reads=[pb], writes=[ob_b[bi]])
            else:
                s.emit("act", I("activation", out=xe_t[bi][:, 2 + tt * 512: 2 + (tt + 1) * 512], in_=pb.ap[:, :], func=AF.Copy), reads=[pb], writes=[xe_b[bi][tt]])
        if m < 16:
            s.emit("sp", I("dma_start", out=zsT[m * 128:(m + 1) * 128, :], in_=ob_t[bi][:, :]), reads=[ob_b[bi]], writes=[W["scr2_b"]], dsem=ds_o[bi])
        else:
            cm = m - 16
            rd = xe_b[bi] + [xepad_b[bi], prm_b]
            s.emit("dve", I("tensor_scalar", out=acc_t[bi][:, :], in0=xe_t[bi][:, 0:SEQ], scalar1=cw_t[:, cm, 0:1], scalar2=cbias_t[:, cm:cm + 1], op0=ALU.mult, op1=ALU.add),
                   reads=rd, writes=[acc_b[bi]])
            for kk in (1, 2, 3, 4):
                s.emit("dve", I("scalar_tensor_tensor", out=acc_t[bi][:, :], in0=xe_t[bi][:, kk:kk + SEQ], scalar=cw_t[:, cm, kk:kk + 1], in1=acc_t[bi][:, :], op0=ALU.mult, op1=ALU.add),
                       reads=rd + [acc_b[bi]], writes=[acc_b[bi]])
            s.emit("act", I("activation", out=ob_t[bi][:, :], in_=acc_t[bi][:, :], func=AF.Silu), reads=[acc_b[bi]], writes=[ob_b[bi]])
            s.emit("sp", I("dma_start", out=xcT[cm * 128:(cm + 1) * 128, :], in_=ob_t[bi][:, :]), reads=[ob_b[bi]], writes=[W["scr2_b"]], dsem=ds_o[bi])
    s.barrier()
    build_ssd_scan(cx, base, sc_t, uT, zsT, xcT, j, W)


def build_ssd_scan(cx, base, sc_t, uT, zsT, xcT, j, W):
    s = cx.s
    ps_t = cx.ps_t
    st = Arena(base.top)
    ident_t, cb = W["ident_t"], W["ident_b"]
    identf_t = W["identf_t"]
    ones1_t = W["ones1_t"]
    U_t = [W["ssdc_t"][:, 0:128], W["ssdc_t"][:, 128:256]]
    M_t = [W["ssdc_t"][:, 256:768], W["ssdc_t"][:, 768:1280]]
    BT = [cx.sb(st, [128, SEQ], BF16, "sb_BT") for _ in range(2)]
    CT = [cx.sb(st, [128, SEQ], BF16, "sb_CT") for _ in range(2)]
    Btok = [cx.sb(st, [128, 32, 128], BF16, "sb_Btok") for _ in range(2)]
    xs_tok = cx.sb(st, [128, 32, 256], BF16, "sb_xstok")
    y_tok = cx.sb(st, [128, 32, 256], F32, "sb_ytok")
    S32 = [cx.sb(st, [128, 256], F32, "sb_S32") for _ in range(2)]
    Sbf = [cx.sb(st, [128, 256], BF16, "sb_Sbf") for _ in range(2)]
    rhs1 = [cx.sb(st, [128, 512], BF16, "sb_rhs1") for _ in range(2)]
    LT = [cx.sb(st, [128, 512], F32, "sb_LT") for _ in range(2)]
    STt = [cx.sb(st, [128, 512], BF16, "sb_ST") for _ in range(2)]
    xdt2 = [cx.sb(st, [128, 2, 4, 64], BF16, "sb_xdt2") for _ in range(2)]
    xdt = [t[:, 0, :, :].rearrange("p h e -> p (h e)") for t in xdt2]
    xdtd = [t[:, 1, :, :].rearrange("p h e -> p (h e)") for t in xdt2]
    yo = [cx.sb(st, [128, 256], F32, "sb_yo") for _ in range(2)]
    xq = [cx.sb(st, [128, 2, 512], BF16, "sb_xq") for _ in range(2)]
    zq = [cx.sb(st, [128, 2, 512], BF16, "sb_zq") for _ in range(2)]
    tq = [cx.sb(st, [128, 512], F32, "sb_tq") for _ in range(2)]
    uT_t = cx.sb(st, [128, 2, SEQ], BF16, "sb_uT")
    dexp_t = cx.sb(st, [128, 16], F32, "sb_dexp")

    BT_b = [Buf(None) for _ in range(2)]
    CT_b = [Buf(None) for _ in range(2)]
    Btok_b = [[Buf(None) for _ in range(8)] for _ in range(2)]
    xs_b = [Buf(None) for _ in range(8)]
    y_b = [Buf(None) for _ in range(32)]
    S32_b = [Buf(None) for _ in range(2)]
    Sbf_b = [Buf(None) for _ in range(2)]
    rhs1_b = [Buf(None) for _ in range(2)]
    LT_b = [Buf(None) for _ in range(2)]
    ST_b = [Buf(None) for _ in range(2)]
    xdt_b = [Buf(None) for _ in range(2)]
    xdtd_b = [Buf(None) for _ in range(2)]
    yo_b = [Buf(None) for _ in range(2)]
    xq_b = [Buf(None) for _ in range(2)]
    zq_b = [Buf(None) for _ in range(2)]
    tq_b = [Buf(None) for _ in range(2)]
    uT_b = [Buf(None) for _ in range(2)]
    dexp_b = Buf(None)
    sc_b = Buf(None)
    ds_g = [s.new_dsem(f"sb_dg{i}") for i in range(4)]
    ds_q = [s.new_dsem(f"sb_dq{i}") for i in range(2)]
    ds_u = s.new_dsem("sb_du")
    ds_d = s.new_dsem("sb_dd")
    s.emit("sp", I("dma_start", out=dexp_t[:, :], in_=W["ssd_dexp"][j]), writes=[dexp_b], dsem=ds_d)
    bf = lambda bank: ps_t[:, bank * 512:(bank + 1) * 512].bitcast(BF16)

    def load_bc(g):
        for d in range(2):
            r0 = 2048 + d * 2048 + g * 128
            s.emit("sp", I("dma_start", out=BT[d][:, :], in_=xcT[r0:r0 + 128, :]), reads=[W["scr2_b"]], writes=[BT_b[d]], dsem=ds_g[d * 2])
            s.emit("sp", I("dma_start", out=CT[d][:, :], in_=xcT[r0 + 1024:r0 + 1024 + 128, :]), reads=[W["scr2_b"]], writes=[CT_b[d]], dsem=ds_g[d * 2 + 1])

    load_bc(0)
    for g in range(8):
        for d in range(2):
            for c4 in range(8):
                bank = 6 + c4 % 2
                for t in range(4):
                    c = c4 * 4 + t
                    s.emit("pe", I("transpose", out=bf(bank)[:, t * 128:(t + 1) * 128], in_=BT[d][:, c * 128:(c + 1) * 128], identity=ident_t[:, :]),
                           reads=[BT_b[d], cb], writes=[cx.psum[bank]])
                s.emit("act", I("activation", out=Btok[d][:, c4 * 4:(c4 + 1) * 4, :], in_=bf(bank)[:, 0:512].rearrange("p (t n) -> p t n", t=4), func=AF.Copy),
                       reads=[cx.psum[bank]], writes=[Btok_b[d][c4]])
        for quad in range(8):
            qb = quad % 2
            for cc in range(2):
                r0 = g * 256 + cc * 128
                s.emit("sp", I("dma_start", out=xq[qb][:, cc, :], in_=xcT[r0:r0 + 128, quad * 512:(quad + 1) * 512]), reads=[W["scr2_b"]], writes=[xq_b[qb]], dsem=ds_q[qb])
            bank = 6 + quad % 2
            for t in range(4):
                for cc in range(2):
                    s.emit("pe", I("transpose", out=bf(bank)[:, (t * 2 + cc) * 128:(t * 2 + cc + 1) * 128], in_=xq[qb][:, cc, t * 128:(t + 1) * 128], identity=ident_t[:, :]),
                           reads=[xq_b[qb], cb], writes=[cx.psum[bank]])
            s.emit("act", I("activation", out=xs_tok[:, quad * 4:(quad + 1) * 4, :], in_=bf(bank)[:, 0:1024].rearrange("p (t n) -> p t n", t=4), func=AF.Copy),
                   reads=[cx.psum[bank]], writes=[xs_b[quad]])
        for d in range(2):
            s.emit("dve", I("memset", S32[d][:, :], 0.0), writes=[S32_b[d]])
            s.emit("dve", I("memset", Sbf[d][:, :], 0.0), writes=[Sbf_b[d]])

        steps = [(i, d) for i in range(32) for d in range(2)]

        def pre(n):
            i, d = steps[n]
            c = i if d == 0 else 31 - i
            par = n % 2
            csl = slice(c * 128, (c + 1) * 128)
            h0 = d * 32 + g * 4
            v4 = lambda t: t[:, :].rearrange("p (h q) -> p h q", h=4)
            s.emit("pool", I("tensor_tensor", out=v4(rhs1[par]), in0=U_t[d].unsqueeze(1).to_broadcast([128, 4, 128]),
                             in1=sc_t[:, c, 2, h0:h0 + 4].unsqueeze(2).to_broadcast([128, 4, 128]), op=ALU.mult),
                   reads=[sc_b, cb], writes=[rhs1_b[par]])
            seg = cx.psum[par]
            s.emit("pe", I("matmul", seg.ap[:, :], lhsT=ones1_t[:, :], rhs=rhs1[par][:, :], start=True, stop=False), reads=[rhs1_b[par], cb], writes=[seg])
            s.emit("pe", I("matmul", seg.ap[:, :], lhsT=ident_t[:, :], rhs=M_t[d], start=False, stop=True), reads=[cb], writes=[seg])
            for h in range(4):
                s.emit("act", I("activation", out=LT[par][:, h * 128:(h + 1) * 128], in_=seg.ap[:, h * 128:(h + 1) * 128], func=AF.Exp, bias=sc_t[:, c, 3, h0 + h:h0 + h + 1], scale=1.0),
                       reads=[seg, sc_b], writes=[LT_b[par]])
            cbk = cx.psum[2 + par]
            s.emit("pe", I("matmul", cbk.ap[:, 0:128], lhsT=BT[d][:, csl], rhs=CT[d][:, csl], start=True, stop=True), reads=[BT_b[d], CT_b[d]], writes=[cbk])
            s.emit("dve", I("tensor_tensor", out=v4(STt[par]), in0=v4(LT[par]), in1=cbk.ap[:, 0:128].unsqueeze(1).to_broadcast([128, 4, 128]), op=ALU.mult),
                   reads=[LT_b[par], cbk], writes=[ST_b[par]])
            xv = xs_tok[:, c, :].rearrange("p (h e) -> p h e", h=4)
            v64 = lambda t: t[:, :].rearrange("p (h e) -> p h e", h=4)
            s.emit("pool", I("tensor_tensor", out=xdt2[par][:, :, :, :], in0=xv.unsqueeze(1).to_broadcast([128, 2, 4, 64]),
                             in1=sc_t[:, c, 0:2, h0:h0 + 4].unsqueeze(3).to_broadcast([128, 2, 4, 64]), op=ALU.mult),
                   reads=[xs_b[c // 4], sc_b], writes=[xdt_b[par], xdtd_b[par]])

        def post(n):
            i, d = steps[n]
            c = i if d == 0 else 31 - i
            par = n % 2
            csl = slice(c * 128, (c + 1) * 128)
            h0 = d * 32 + g * 4
            first = i < 16
            yb = cx.psum[4 + par]
            sb_ = cx.psum[6 + par]
            for h in range(4):
                s.emit("pe", I("matmul", yb.ap[:, h * 64:(h + 1) * 64], lhsT=STt[par][:, h * 128:(h + 1) * 128], rhs=xdt[par][:, h * 64:(h + 1) * 64], start=True, stop=True),
                       reads=[ST_b[par], xdt_b[par]], writes=[yb])
            s.emit("pe", I("matmul", yb.ap[:, 256:512], lhsT=CT[d][:, csl], rhs=Sbf[d][:, :], start=True, stop=True), reads=[CT_b[d], Sbf_b[d]], writes=[yb])
            s.emit("pe", I("matmul", sb_.ap[:, 0:256], lhsT=Btok[d][:, c, :], rhs=xdtd[par][:, :], start=True, stop=True), reads=[Btok_b[d][c // 4], xdtd_b[par]], writes=[sb_])
            v64 = lambda t: t.rearrange("p (h e) -> p h e", h=4)
            s.emit("dve", I("tensor_tensor", out=v64(yo[par][:, :]), in0=v64(yb.ap[:, 256:512]), in1=sc_t[:, c, 4, h0:h0 + 4].unsqueeze(2).to_broadcast([128, 4, 64]), op=ALU.mult),
                   reads=[yb, sc_b], writes=[yo_b[par]])
            if first:
                s.emit("dve", I("tensor_tensor", out=y_tok[:, c, :], in0=yo[par][:, :], in1=yb.ap[:, 0:256], op=ALU.add), reads=[yo_b[par], yb], writes=[y_b[c]])
            else:
                s.emit("dve", I("tensor_tensor", out=yo[par][:, :], in0=yo[par][:, :], in1=yb.ap[:, 0:256], op=ALU.add), reads=[yo_b[par], yb], writes=[yo_b[par]])
                s.emit("dve", I("tensor_tensor", out=y_tok[:, c, :], in0=y_tok[:, c, :], in1=yo[par][:, :], op=ALU.add), reads=[yo_b[par], y_b[c]], writes=[y_b[c]])
            for h in range(4):
                hs = slice(h * 64, (h + 1) * 64)
                s.emit("dve", I("scalar_tensor_tensor", out=S32[d][:, hs], in0=S32[d][:, hs], scalar=sc_t[:, c, 5, h0 + h:h0 + h + 1], in1=sb_.ap[:, hs], op0=ALU.mult, op1=ALU.add),
                       reads=[S32_b[d], sb_, sc_b], writes=[S32_b[d]])
            pending_sbf.append(d)

        pending_sbf = []

        def flush_sbf():
            while pending_sbf:
                d_ = pending_sbf.pop(0)
                s.emit("act", I("activation", out=Sbf[d_][:, :], in_=S32[d_][:, :], func=AF.Copy), reads=[S32_b[d_]], writes=[Sbf_b[d_]])

        pre(0)
        for n in range(1, len(steps)):
            pre(n)
            flush_sbf()
            post(n - 1)
        flush_sbf()
        post(len(steps) - 1)
        pending_sbf.clear()
        if g + 1 < 8:
            load_bc(g + 1)

        xfin = [xs_tok[:, 0:16, :].rearrange("p a b -> p (a b)"), xs_tok[:, 16:32, :].rearrange("p a b -> p (a b)")]
        zfin = [Btok[0][:, :, :].rearrange("p a b -> p (a b)"), Btok[1][:, :, :].rearrange("p a b -> p (a b)")]
        xfin_b = [xs_b[0:4], xs_b[4:8]]
        zfin_b = [Btok_b[0], Btok_b[1]]
        for cc in range(2):
            r0 = g * 256 + cc * 128
            s.emit("sp", I("dma_start", out=xfin[cc], in_=xcT[r0:r0 + 128, :]), reads=[W["scr2_b"]], writes=xfin_b[cc], dsem=ds_q[0])
            s.emit("sp", I("dma_start", out=zfin[cc], in_=zsT[r0:r0 + 128, :]), reads=[W["scr2_b"]], writes=zfin_b[cc], dsem=ds_q[1])
        for quad in range(8):
            qb = quad % 2
            for cc in range(2):
                bank = cx.psum[(quad * 2 + cc) % 4]
                for t in range(4):
                    c = quad * 4 + t
                    s.emit("pe", I("transpose", out=bank.ap[:, t * 128:(t + 1) * 128], in_=y_tok[:, c, cc * 128:(cc + 1) * 128], identity=identf_t[:, :]),
                           reads=[y_b[c], cb], writes=[bank])
                tb = (quad * 2 + cc) % 2
                qsl = slice(quad * 512, (quad + 1) * 512)
                s.emit("dve", I("scalar_tensor_tensor", out=tq[tb][:, :], in0=xfin[cc][:, qsl], scalar=dexp_t[:, g * 2 + cc:g * 2 + cc + 1], in1=bank.ap[:, :], op0=ALU.mult, op1=ALU.add),
                       reads=xfin_b[cc] + [bank, dexp_b], writes=[tq_b[tb]])
                s.emit("pool", I("tensor_tensor", out=uT_t[:, cc, qsl], in0=tq[tb][:, :], in1=zfin[cc][:, qsl], op=ALU.mult),
                       reads=[tq_b[tb]] + zfin_b[cc], writes=[uT_b[cc]])
        for cc in range(2):
            r0 = g * 256 + cc * 128
            s.emit("sp", I("dma_start", out=uT[r0:r0 + 128, :], in_=uT_t[:, cc, :]), reads=[uT_b[cc]], writes=[W["scr_b"]], dsem=ds_u)
    s.barrier()


def host_ssd_consts():
    j = np.arange(128)[:, None]
    q = np.arange(128)[None, :]
    Uf = (j <= q).astype(np.float32)
    Ub = (j >= q).astype(np.float32)
    Mf = np.where(q >= j, 0.0, -30000.0).astype(np.float32)
    Mb = np.where(j >= q, 0.0, -30000.0).astype(np.float32)
    return np.ascontiguousarray(np.concatenate([Uf, Ub, np.tile(Mf, (1, 4)), np.tile(Mb, (1, 4))], axis=1))


def host_ssd_params(conv_w, conv_b, dt_bias, a_log, d_skip, norm_g):
    out = {}
    out["ssd_cw"] = np.ascontiguousarray(conv_w.reshape(5, 48, 128).transpose(2, 1, 0)).astype(np.float32)
    out["ssd_cb"] = np.ascontiguousarray(conv_b.reshape(48, 128).T).astype(np.float32)
    out["ssd_dtb"] = np.ascontiguousarray(dt_bias.reshape(64, 1)).astype(np.float32)
    out["ssd_alog"] = np.ascontiguousarray(a_log.reshape(64, 1)).astype(np.float32)
    out["ssd_dexp"] = np.ascontiguousarray(np.repeat(d_skip, 64).reshape(16, 128).T).astype(np.float32)
    out["ssd_ng"] = np.ascontiguousarray(norm_g.reshape(16, 128).T).astype(np.float32)
    return out


def full_plan():
    plan = [("chain", [0, 1], [("ffn", 0, 1), ("normout", 0)])]
    for i in range(DEPTH):
        jm = i // 2
        if i % 2 == 0:
            plan.append(("ssd", jm))
            head = [("proj_ssd", jm)]
        else:
            plan.append(("na", jm))
            head = [("proj_na", jm)]
        subs = head + [("ffn", i, 2), ("ple", i)]
        if i + 1 < DEPTH:
            subs += [("ffn", i + 1, 1), ("normout", i + 1)]
        plan.append(("chain", [0, 1], subs))
    return plan


_NC_CACHE = {}


def kernel(x, p, ffn1_norm, ffn1_w_gu, ffn1_w_down, mix_norm, ffn2_norm, ffn2_w_gu, ffn2_w_down,
           ple_norm, ple_w_gate, ple_w_proj, ple_post_norm,
           ssd_w_in, ssd_conv_w, ssd_conv_b, ssd_dt_bias, ssd_a_log, ssd_d, ssd_norm, ssd_w_out,
           na_w_qkv, na_q_norm, na_k_norm, na_rpb, na_w_out):
    f32 = lambda a: np.ascontiguousarray(np.asarray(a, dtype=np.float32))
    x = np.asarray(x, dtype=np.float32)
    p = np.asarray(p, dtype=np.float32)
    B = x.shape[0]
    shared = {}
    gl = {"ffn1_norm": ffn1_norm, "mix_norm": mix_norm, "ffn2_norm": ffn2_norm, "ple_norm": ple_norm, "ple_post_norm": ple_post_norm}
    gains = np.stack([np.asarray(gl[nm], np.float32)[l] for nm in GAIN_NAMES for l in range(DEPTH)])
    shared["gains"] = np.ascontiguousarray(gains.reshape(len(GAIN_NAMES) * DEPTH, KD, 128).transpose(2, 0, 1))
    shared["ident"] = np.eye(128, dtype=np.float32)
    shared["identf"] = np.eye(128, dtype=np.float32)
    shared["ssd_consts"] = host_ssd_consts()
    shared["na_gain"] = host_na_gain(np.asarray(na_q_norm, np.float32), np.asarray(na_k_norm, np.float32))
    wl = {"ffn1_w_gu": ffn1_w_gu, "ffn1_w_down": ffn1_w_down, "ffn2_w_gu": ffn2_w_gu, "ffn2_w_down": ffn2_w_down,
          "ple_w_gate": ple_w_gate, "ple_w_proj": ple_w_proj, "na_w_qkv": na_w_qkv, "na_w_out": na_w_out,
          "ssd_w_in": ssd_w_in, "ssd_w_out": ssd_w_out}
    for nm, arr in wl.items():
        arr = np.asarray(arr, np.float32)
        for i in range(arr.shape[0]):
            shared[f"{nm}{i}"] = f32(arr[i])
    rpb = np.asarray(na_rpb, np.float32)
    for l in range(2):
        shared[f"na_bias_g{l}"] = host_bias_g(rpb[l])
        sp = host_ssd_params(np.asarray(ssd_conv_w, np.float32)[l], np.asarray(ssd_conv_b, np.float32)[l],
                             np.asarray(ssd_dt_bias, np.float32)[l], np.asarray(ssd_a_log, np.float32)[l],
                             np.asarray(ssd_d, np.float32)[l], np.asarray(ssd_norm, np.float32)[l])
        for k, v in sp.items():
            shared[f"{k}{l}"] = v
    in_maps = []
    for b in range(B):
        m = dict(shared)
        m["xT"] = np.ascontiguousarray(x[b].T)
        for i in range(DEPTH):
            m[f"pT{i}"] = np.ascontiguousarray(p[i, b].T)
        in_maps.append(m)
    if "nc" not in _NC_CACHE:
        _NC_CACHE["nc"] = build_nc(full_plan())
    nc = _NC_CACHE["nc"]
    res = run_bass_kernel_spmd(nc, in_maps, core_ids=list(range(B)))
    out = np.stack([np.ascontiguousarray(np.asarray(r["outT"], dtype=np.float32).T) for r in res.results])
    return out
```

```python
import contextlib
import os
import numpy as np
import concourse.bass as bass
import concourse.mybir as mybir
from concourse.bass_utils import run_bass_kernel_spmd

F32 = mybir.dt.float32
BF16 = mybir.dt.bfloat16
AF = mybir.ActivationFunctionType
ALU = mybir.AluOpType
AX = mybir.AxisListType

D = 1024
SEQ = 4096
DEPTH = 4
DFF = 2816
NFF = DFF // 128
DPLE = 256
EPS = 1e-6
TH = 2048
NTT = TH // 512
KD = D // 128


class Buf:
    __slots__ = ("ap", "writers", "readers", "name")

    def __init__(self, ap, name=""):
        self.ap = ap
        self.writers = []
        self.readers = []
        self.name = name


class Op:
    __slots__ = ("eng", "fn", "seq", "deps", "ddeps", "dsem", "dcount", "target", "rank")

    def __init__(self, eng, fn):
        self.eng = eng
        self.fn = fn
        self.deps = {}
        self.ddeps = {}
        self.dsem = None
        self.dcount = 0
        self.target = False
        self.rank = 0


def I(name, *args, **kwargs):
    return lambda e: getattr(e, name)(*args, **kwargs)


COMPUTE = ("pe", "act", "dve", "pool")
ENGS = ("pe", "act", "dve", "pool", "sp")
SAME_ENG_WINDOW = int(os.environ.get("SEW", "2"))
SEM_SEG = 30000


class Sched:
    def __init__(self, nc, stack):
        self.nc = nc
        self.stack = stack
        self.ops = {e: [] for e in ENGS}
        self.dma_sems = []
        self.dma_counts = []
        self.n_ops = 0

    def new_dsem(self, name):
        if not hasattr(self, "_dsem_names"):
            self._dsem_names = {}
        if name in self._dsem_names:
            return self._dsem_names[name]
        self._dsem_names[name] = len(self.dma_sems)
        s = self.stack.enter_context(self.nc.semaphore(name))
        self.dma_sems.append(s)
        self.dma_counts.append(0)
        return len(self.dma_sems) - 1

    def _dep_on(self, op, ref):
        e, seq, dsem = ref
        if dsem is not None:
            op.ddeps[dsem] = self.dma_counts[dsem]
        else:
            if op.deps.get(e, -1) < seq:
                op.deps[e] = seq

    def emit(self, eng, fn, reads=(), writes=(), dsem=None):
        op = Op(eng, fn)
        op.seq = len(self.ops[eng])
        for b in reads:
            for r in b.writers:
                self._dep_on(op, r)
        for b in writes:
            for r in b.readers:
                self._dep_on(op, r)
            for r in b.writers:
                self._dep_on(op, r)
        if dsem is not None:
            self.dma_counts[dsem] += 1
            op.dsem = dsem
            op.dcount = self.dma_counts[dsem]
        ref = (eng, op.seq, dsem)
        wset = set(id(b) for b in writes)
        for b in writes:
            if b.readers:
                b.writers = [ref]
                b.readers = []
            else:
                b.writers.append(ref)
                if len(b.writers) > 64:
                    b.writers = self._prune(b.writers)
        for b in reads:
            if id(b) not in wset:
                b.readers.append(ref)
                if len(b.readers) > 64:
                    b.readers = self._prune(b.readers)
        self.ops[eng].append(op)
        self.n_ops += 1
        return op

    @staticmethod
    def _prune(refs):
        best = {}
        out = []
        for (e, seq, dsem) in refs:
            if dsem is not None:
                k = ("d", dsem)
                best[k] = (e, seq, dsem)
            else:
                k = ("c", e)
                if k not in best or best[k][1] < seq:
                    best[k] = (e, seq, dsem)
        return list(best.values())

    def barrier(self):
        last = {}
        for e in COMPUTE:
            for i in range(len(self.ops[e]) - 1, -1, -1):
                if self.ops[e][i].fn is not None and self.ops[e][i].dsem is None:
                    last[e] = i
                    break
        dcounts = list(self.dma_counts)
        self._pending_barrier = (last, dcounts)
        for e in ENGS:
            op = Op(e, None)
            op.seq = len(self.ops[e])
            for e2, s in last.items():
                if e2 != e:
                    op.deps[e2] = s
            for i, c in enumerate(dcounts):
                if c > 0:
                    op.ddeps[i] = c
            self.ops[e].append(op)

    def replay(self):
        nc = self.nc
        for e in ENGS:
            for op in self.ops[e]:
                for e2, s in op.deps.items():
                    if e2 == e:
                        if e == "pe" or e == "sp":
                            continue
                        if op.seq - s > SAME_ENG_WINDOW:
                            continue
                    self.ops[e2][s].target = True
        esems = {}
        for e in COMPUTE:
            n = 0
            for op in self.ops[e]:
                if op.target:
                    n += 1
                op.rank = n
            nseg = n // SEM_SEG + 1
            esems[e] = [self.stack.enter_context(nc.semaphore(f"es_{e}_{i}")) for i in range(nseg)]
        self.esems = esems
        engobj = {"pe": "tensor", "act": "scalar", "dve": "vector", "pool": "gpsimd", "sp": "sync"}
        sched = self

        def run(e, eng):
            waited = {}
            for op in sched.ops[e]:
                for e2, s in op.deps.items():
                    if e2 == e:
                        if e == "pe" or e == "sp":
                            continue
                        if op.seq - s > SAME_ENG_WINDOW:
                            continue
                    r = sched.ops[e2][s].rank
                    seg = (r - 1) // SEM_SEG
                    val = r - seg * SEM_SEG
                    key = (e2, seg)
                    if waited.get(key, 0) >= val:
                        continue
                    waited[key] = val
                    eng.wait_ge(esems[e2][seg], val)
                for di, c in op.ddeps.items():
                    key = ("d", di)
                    if waited.get(key, 0) >= c:
                        continue
                    waited[key] = c
                    eng.wait_ge(sched.dma_sems[di], 16 * c)
                if op.fn is None:
                    continue
                ins = op.fn(eng)
                if op.dsem is not None:
                    ins.then_inc(sched.dma_sems[op.dsem], 16)
                elif op.target:
                    seg = (op.rank - 1) // SEM_SEG
                    ins.then_inc(esems[e][seg], 1)

        with nc.Block() as block:
            @block.tensor
            def _(eng):
                run("pe", eng)

            @block.scalar
            def _(eng):
                run("act", eng)

            @block.vector
            def _(eng):
                run("dve", eng)

            @block.gpsimd
            def _(eng):
                run("pool", eng)

            @block.sync
            def _(eng):
                run("sp", eng)


SBUF_BASE = 16384
SBUF_TOP = 228864


class Arena:
    def __init__(self, top):
        self.top = top


class Ctx:
    def __init__(self, nc, stack):
        self.nc = nc
        self.stack = stack
        self.s = Sched(nc, stack)
        self.ps_t = stack.enter_context(nc.psum_tensor("ps_all", [128, 8 * 512], F32))
        self.psum = [Buf(self.ps_t[:, i * 512:(i + 1) * 512], f"ps{i}") for i in range(8)]
        self._n = 0

    def sb(self, stack, shape, dt, name=None):
        self._n += 1
        nbytes = int(np.prod(shape[1:])) * (2 if dt == BF16 else 4)
        nbytes = (nbytes + 63) // 64 * 64
        off = stack.top
        stack.top += nbytes
        assert stack.top <= SBUF_TOP, f"SBUF arena overflow {stack.top}"
        return self.nc.alloc_sbuf_tensor_at(f"{name or 't'}_{self._n}", list(shape), dt, offset=off)

    def dram(self, name, shape, dt, kind="Internal"):
        return self.nc.dram_tensor(name, list(shape), dt, kind=kind)


def rmsnorm_T(cx, st, hres_b, hres_t, gain_ap, xn_t, xn_b, rstd_t, rstd_b, sq_t, sq_b, ones_t, ones_b, ps_ids):
    s = cx.s
    for tt in range(NTT):
        ps = cx.psum[ps_ids[tt % len(ps_ids)]]
        tsl = slice(tt * 512, (tt + 1) * 512)
        for d in range(KD):
            sq = sq_b[(tt * KD + d) % len(sq_b)]
            sqt = sq_t[(tt * KD + d) % len(sq_b)]
            s.emit("act", I("activation", out=sqt[:, :], in_=hres_t[:, d, tsl], func=AF.Square),
                   reads=[hres_b[d][tt]], writes=[sq])
            s.emit("pe", I("matmul", ps.ap[:, :], lhsT=ones_t[:, :], rhs=sqt[:, :], start=(d == 0), stop=(d == KD - 1)),
                   reads=[sq, ones_b], writes=[ps])
        s.emit("act", I("activation", out=rstd_t[:, tsl], in_=ps.ap[:, :], func=AF.Ln, bias=cx.eps_t[:, 0:1], scale=1.0),
               reads=[ps], writes=[rstd_b[tt]])
        s.emit("act", I("activation", out=rstd_t[:, tsl], in_=rstd_t[:, tsl], func=AF.Exp, scale=-0.5),
               reads=[rstd_b[tt]], writes=[rstd_b[tt]])
        for d in range(KD):
            s.emit("dve", I("scalar_tensor_tensor", out=xn_t[:, d, tsl], in0=hres_t[:, d, tsl], scalar=gain_ap[:, d:d + 1], in1=rstd_t[:, tsl], op0=ALU.mult, op1=ALU.mult),
                   reads=[hres_b[d][tt], rstd_b[tt]], writes=[xn_b[d][tt]])


def build_chain_phase(cx, src_dram, hT_dram, half_list, sublayers, W):
    nc = cx.nc
    s = cx.s
    if True:
        st = Arena(cx.const_top)
        hres_t = cx.sb(st, [128, KD, TH], F32, "hres")
        xn_t = cx.sb(st, [128, KD, TH], BF16, "xn")
        rstd_t = cx.sb(st, [128, TH], F32, "rstd")
        NSQ = 2
        sq_t = [cx.sb(st, [128, 512], BF16, "sq") for _ in range(NSQ)]
        G = 2
        act_t = [cx.sb(st, [128, G, TH], BF16, "act") for _ in range(2)]
        wgu_t = [cx.sb(st, [128, KD, 2, G * 128], BF16, "wgu") for _ in range(2)]
        wd_t = [cx.sb(st, [128, G, D], BF16, "wd") for _ in range(2)]
        sg_t = [cx.sb(st, [128, 512], F32, "sg") for _ in range(2)]
        wgate_t = cx.sb(st, [128, KD, D], BF16, "wgate")
        wproj_t = cx.sb(st, [128, 2, D], BF16, "wproj")
        pT_t = cx.sb(st, [128, 2, TH], BF16, "pT")
        proj_t = cx.sb(st, [128, KD, 512], F32, "proj")
        gate_t = [cx.sb(st, [128, 512], F32, "gate") for _ in range(2)]
        tmp_t = [cx.sb(st, [128, 512], F32, "tmp")] * 2

        hres_b = [[Buf(None, f"hres{d}_{tt}") for tt in range(NTT)] for d in range(KD)]
        xn_b = [[Buf(None) for tt in range(NTT)] for d in range(KD)]
        rstd_b = [Buf(None) for tt in range(NTT)]
        sq_b = [Buf(None) for _ in range(NSQ)]
        act_b = [[[Buf(None) for tt in range(NTT)] for g in range(G)] for _ in range(2)]
        wgu_b = [Buf(None) for _ in range(2)]
        wd_b = [Buf(None) for _ in range(2)]
        sg_b = [Buf(None) for _ in range(2)]
        wgate_b = Buf(None)
        wproj_b = Buf(None)
        pT_b = Buf(None)
        proj_b = [Buf(None) for d in range(KD)]
        gate_b = [Buf(None) for _ in range(2)]
        tmp_b = [Buf(None)] * 2
        ones_t, ones_b = W["ones_t"], W["ones_b"]
        gains_t, gains_b = W["gains_t"], W["gains_b"]

        ds_h = [s.new_dsem(f"dh{d}") for d in range(2)]
        ds_wgu = [s.new_dsem(f"dwgu{i}") for i in range(2)]
        ds_wd = [s.new_dsem(f"dwd{i}") for i in range(2)]
        ds_misc = s.new_dsem("dmisc")
        ds_st = s.new_dsem("dst")
        hT_b = W["hT_b"]

        for half in half_list:
            t0 = half * TH
            for d in range(KD):
                s.emit("sp", I("dma_start", out=hres_t[:, d, :], in_=src_dram[d * 128:(d + 1) * 128, t0:t0 + TH]),
                       reads=[hT_b[(d, half)]], writes=hres_b[d], dsem=ds_h[d % 2])
            stored = [False]

            def store_hres():
                if stored[0]:
                    return
                stored[0] = True
                for d in range(KD):
                    s.emit("sp", I("dma_start", out=hT_dram[d * 128:(d + 1) * 128, t0:t0 + TH], in_=hres_t[:, d, :]),
                           reads=hres_b[d], writes=[hT_b[(d, half)]], dsem=ds_st)

            for si, sub in enumerate(sublayers):
                if sub[0] == "normout":
                    _, layer, dst = sub
                    if si == len(sublayers) - 1:
                        store_hres()
                    gi = W["gain_idx"][("mix_norm", layer)]
                    rmsnorm_T(cx, st, hres_b, hres_t, gains_t[:, gi, :], xn_t, xn_b, rstd_t, rstd_b, sq_t, sq_b, ones_t, ones_b, [6, 7])
                    for d in range(KD):
                        s.emit("sp", I("dma_start", out=dst[d * 128:(d + 1) * 128, t0:t0 + TH], in_=xn_t[:, d, :]),
                               reads=xn_b[d], writes=[W["scr_b"]], dsem=ds_st)
                elif sub[0] == "proj":
                    _, srcT, w_ap, k0 = sub
                    w_v = w_ap.rearrange("(kc p) c -> p kc c", p=128)
                    s.emit("pool", I("dma_start", out=wgate_t[:, :, :], in_=w_v[:, k0:k0 + KD, :]), writes=[wgate_b], dsem=ds_misc)
                    for d in range(KD):
                        s.emit("sp", I("dma_start", out=xn_t[:, d, :], in_=srcT[(k0 + d) * 128:(k0 + d + 1) * 128, t0:t0 + TH]),
                               reads=[W["scr_b"]], writes=xn_b[d], dsem=ds_h[d % 2])
                    for d in range(KD):
                        for tt in range(NTT):
                            tsl = slice(tt * 512, (tt + 1) * 512)
                            pa = cx.psum[4 + (d * NTT + tt) % 2]
                            for k in range(KD):
                                s.emit("pe", I("matmul", pa.ap[:, :], lhsT=wgate_t[:, k, d * 128:(d + 1) * 128], rhs=xn_t[:, k, tsl], start=(k == 0), stop=(k == KD - 1)),
                                       reads=[wgate_b, xn_b[k][tt]], writes=[pa])
                            s.emit("dve", I("tensor_tensor", out=hres_t[:, d, tsl], in0=hres_t[:, d, tsl], in1=pa.ap[:, :], op=ALU.add),
                                   reads=[pa, hres_b[d][tt]], writes=[hres_b[d][tt]])
                elif sub[0] == "proj_ssd":
                    _, srcT, w_ap, ng_t = sub
                    w_v = w_ap.rearrange("(kc p) c -> p kc c", p=128)
                    xn16 = xn_t[:, :, :].rearrange("p a (b t) -> p (a b) t", b=2)
                    src_v = srcT.rearrange("(kc p) t -> p kc t", p=128)
                    for qtr in range(2):
                        q0 = t0 + qtr * 1024
                        allx = [xn_b[kc // 2][(kc % 2) * 2 + t2] for kc in range(16) for t2 in range(2)]
                        for kh in range(2):
                            s.emit("sp", I("dma_start", out=xn16[:, kh * 8:(kh + 1) * 8, :], in_=src_v[:, kh * 8:(kh + 1) * 8, q0:q0 + 1024]),
                                   reads=[W["scr_b"]], writes=allx, dsem=ds_h[kh])
                        for t2 in range(2):
                            tt = qtr * 2 + t2
                            lsl = slice(t2 * 512, (t2 + 1) * 512)
                            tsl = slice(tt * 512, (tt + 1) * 512)
                            pss = cx.psum[6 + t2]
                            for kc in range(16):
                                xb = xn_b[kc // 2][(kc % 2) * 2 + t2]
                                sq = sq_b[kc % NSQ]
                                sqt = sq_t[kc % NSQ]
                                s.emit("act", I("activation", out=sqt[:, :], in_=xn16[:, kc, lsl], func=AF.Square), reads=[xb], writes=[sq])
                                s.emit("pe", I("matmul", pss.ap[:, :], lhsT=W["ones2k_t"][:, :], rhs=sqt[:, :], start=(kc == 0), stop=(kc == 15)), reads=[sq, ones_b], writes=[pss])
                            s.emit("act", I("activation", out=rstd_t[:, tsl], in_=pss.ap[:, :], func=AF.Ln, bias=cx.eps_t[:, 0:1], scale=1.0), reads=[pss], writes=[rstd_b[tt]])
                            s.emit("act", I("activation", out=rstd_t[:, tsl], in_=rstd_t[:, tsl], func=AF.Exp, scale=-0.5), reads=[rstd_b[tt]], writes=[rstd_b[tt]])
                            for kc in range(16):
                                xb = xn_b[kc // 2][(kc % 2) * 2 + t2]
                                s.emit("dve", I("scalar_tensor_tensor", out=xn16[:, kc, lsl], in0=xn16[:, kc, lsl], scalar=ng_t[:, kc:kc + 1], in1=rstd_t[:, tsl], op0=ALU.mult, op1=ALU.mult),
                                       reads=[xb, rstd_b[tt], W["ng_b"]], writes=[xb])
                        for kh in range(2):
                            s.emit("pool", I("dma_start", out=wgate_t[:, :, :], in_=w_v[:, kh * 8:(kh + 1) * 8, :]), writes=[wgate_b], dsem=ds_misc)
                            for d in range(KD):
                                for t2 in range(2):
                                    tt = qtr * 2 + t2
                                    lsl = slice(t2 * 512, (t2 + 1) * 512)
                                    tsl = slice(tt * 512, (tt + 1) * 512)
                                    pa = cx.psum[4 + (d * 2 + t2) % 2]
                                    for k in range(KD):
                                        kc = kh * 8 + k
                                        xb = xn_b[kc // 2][(kc % 2) * 2 + t2]
                                        s.emit("pe", I("matmul", pa.ap[:, :], lhsT=wgate_t[:, k, d * 128:(d + 1) * 128], rhs=xn16[:, kc, lsl], start=(k == 0), stop=(k == KD - 1)),
                                               reads=[wgate_b, xb], writes=[pa])
                                    s.emit("dve", I("tensor_tensor", out=hres_t[:, d, tsl], in0=hres_t[:, d, tsl], in1=pa.ap[:, :], op=ALU.add),
                                           reads=[pa, hres_b[d][tt]], writes=[hres_b[d][tt]])
                elif sub[0] == "ffn":
                    _, layer, which = sub
                    w_gu = W[f"ffn{which}_w_gu"][layer]
                    w_dn = W[f"ffn{which}_w_down"][layer]
                    gi = W["gain_idx"][(f"ffn{which}_norm", layer)]
                    rmsnorm_T(cx, st, hres_b, hres_t, gains_t[:, gi, :], xn_t, xn_b, rstd_t, rstd_b, sq_t, sq_b, ones_t, ones_b, [6, 7])
                    ngrp = NFF // G
                    w_gu_v = w_gu.rearrange("(kc p) c -> p kc c", p=128)
                    w_dn_v = w_dn.rearrange("(c p) d -> p c d", p=128)

                    def phaseA(gi_, units):
                        bi = gi_ % 2
                        c0 = gi_ * G * 128
                        s.emit("pool", I("dma_start", out=wgu_t[bi][:, :, 0, :], in_=w_gu_v[:, :, c0:c0 + G * 128]),
                               writes=[wgu_b[bi]], dsem=ds_wgu[bi])
                        s.emit("pool", I("dma_start", out=wgu_t[bi][:, :, 1, :], in_=w_gu_v[:, :, DFF + c0:DFF + c0 + G * 128]),
                               writes=[wgu_b[bi]], dsem=ds_wgu[bi])
                        s.emit("pool", I("dma_start", out=wd_t[bi][:, :, :], in_=w_dn_v[:, gi_ * G:(gi_ + 1) * G, :]),
                               writes=[wd_b[bi]], dsem=ds_wd[bi])
                        for g in range(G):
                            for tt in range(NTT):
                                units.append((gi_, g, tt))

                    def unitA(gi_, g, tt, bq=()):
                        bi = gi_ % 2
                        if True:
                            if True:
                                tsl = slice(tt * 512, (tt + 1) * 512)
                                pg = cx.psum[(g * NTT + tt) % 2]
                                pu = cx.psum[2 + (g * NTT + tt) % 2]
                                for k in range(KD):
                                    s.emit("pe", I("matmul", pg.ap[:, :], lhsT=wgu_t[bi][:, k, 0, g * 128:(g + 1) * 128], rhs=xn_t[:, k, tsl], start=(k == 0), stop=(k == KD - 1)),
                                           reads=[wgu_b[bi], xn_b[k][tt]], writes=[pg])
                                    if k % 4 == 3 and bq:
                                        unitB(*bq.pop(0))
                                for k in range(KD):
                                    s.emit("pe", I("matmul", pu.ap[:, :], lhsT=wgu_t[bi][:, k, 1, g * 128:(g + 1) * 128], rhs=xn_t[:, k, tsl], start=(k == 0), stop=(k == KD - 1)),
                                           reads=[wgu_b[bi], xn_b[k][tt]], writes=[pu])
                                    if k % 4 == 3 and bq:
                                        unitB(*bq.pop(0))
                                sgi = (g * NTT + tt) % 2
                                s.emit("act", I("activation", out=sg_t[sgi][:, :], in_=pg.ap[:, :], func=AF.Silu),
                                       reads=[pg], writes=[sg_b[sgi]])
                                s.emit("dve", I("tensor_tensor", out=act_t[bi][:, g, tsl], in0=sg_t[sgi][:, :], in1=pu.ap[:, :], op=ALU.mult),
                                       reads=[pu, sg_b[sgi]], writes=[act_b[bi][g][tt]])

                    def unitB(gi_, d, tt):
                        bi = gi_ % 2
                        if True:
                            if True:
                                tsl = slice(tt * 512, (tt + 1) * 512)
                                pa = cx.psum[4 + (d * NTT + tt) % 4]
                                for g in range(G):
                                    s.emit("pe", I("matmul", pa.ap[:, :], lhsT=wd_t[bi][:, g, d * 128:(d + 1) * 128], rhs=act_t[bi][:, g, tsl], start=(g == 0), stop=(g == G - 1)),
                                           reads=[wd_b[bi], act_b[bi][g][tt]], writes=[pa])
                                s.emit("dve", I("scalar_tensor_tensor", out=hres_t[:, d, tsl], in0=pa.ap[:, :], scalar=0.5, in1=hres_t[:, d, tsl], op0=ALU.mult, op1=ALU.add),
                                       reads=[pa, hres_b[d][tt]], writes=[hres_b[d][tt]])

                    ua = []
                    phaseA(0, ua)
                    for u in ua:
                        unitA(*u)
                    for gi_ in range(1, ngrp + 1):
                        ua = []
                        if gi_ < ngrp:
                            phaseA(gi_, ua)
                        ub = [(gi_ - 1, d, tt) for d in range(KD) for tt in range(NTT)]
                        for u in ua:
                            unitA(*u, bq=ub)
                        while ub:
                            unitB(*ub.pop(0))
                else:
                    _, layer = sub
                    gi = W["gain_idx"][("ple_norm", layer)]
                    gpi = W["gain_idx"][("ple_post_norm", layer)]
                    rmsnorm_T(cx, st, hres_b, hres_t, gains_t[:, gi, :], xn_t, xn_b, rstd_t, rstd_b, sq_t, sq_b, ones_t, ones_b, [6, 7])
                    wg_v = W["ple_w_gate"][layer].rearrange("(kc p) c -> p kc c", p=128)
                    wp_v = W["ple_w_proj"][layer].rearrange("(kc p) c -> p kc c", p=128)
                    pT_v = W["pT"][layer].rearrange("(kc p) t -> p kc t", p=128)
                    s.emit("pool", I("dma_start", out=wgate_t[:, :, :], in_=wg_v), writes=[wgate_b], dsem=ds_misc)
                    s.emit("pool", I("dma_start", out=wproj_t[:, :, :], in_=wp_v), writes=[wproj_b], dsem=ds_misc)
                    s.emit("pool", I("dma_start", out=pT_t[:, :, :], in_=pT_v[:, :, t0:t0 + TH]), writes=[pT_b], dsem=ds_misc)
                    for tt in range(NTT):
                        tsl = slice(tt * 512, (tt + 1) * 512)
                        pss = cx.psum[6 + tt % 2]
                        for d in range(KD):
                            pp = cx.psum[d % 2]
                            for k in range(2):
                                s.emit("pe", I("matmul", pp.ap[:, :], lhsT=wproj_t[:, k, d * 128:(d + 1) * 128], rhs=pT_t[:, k, tsl], start=(k == 0), stop=(k == 1)),
                                       reads=[wproj_b, pT_b], writes=[pp])
                            s.emit("act", I("activation", out=proj_t[:, d, :], in_=pp.ap[:, :], func=AF.Copy),
                                   reads=[pp], writes=[proj_b[d]])
                            sq = sq_b[d % NSQ]
                            sqt = sq_t[d % NSQ]
                            s.emit("act", I("activation", out=sqt[:, :], in_=proj_t[:, d, :], func=AF.Square),
                                   reads=[proj_b[d]], writes=[sq])
                            s.emit("pe", I("matmul", pss.ap[:, :], lhsT=ones_t[:, :], rhs=sqt[:, :], start=(d == 0), stop=(d == KD - 1)),
                                   reads=[sq, ones_b], writes=[pss])
                        s.emit("act", I("activation", out=rstd_t[:, tsl], in_=pss.ap[:, :], func=AF.Ln, bias=cx.eps_t[:, 0:1], scale=1.0),
                               reads=[pss], writes=[rstd_b[tt]])
                        s.emit("act", I("activation", out=rstd_t[:, tsl], in_=rstd_t[:, tsl], func=AF.Exp, scale=-0.5),
                               reads=[rstd_b[tt]], writes=[rstd_b[tt]])
                        for d in range(KD):
                            pgt = cx.psum[2 + d % 2]
                            for k in range(KD):
                                s.emit("pe", I("matmul", pgt.ap[:, :], lhsT=wgate_t[:, k, d * 128:(d + 1) * 128], rhs=xn_t[:, k, tsl], start=(k == 0), stop=(k == KD - 1)),
                                       reads=[wgate_b, xn_b[k][tt]], writes=[pgt])
                            gb = d % 2
                            s.emit("act", I("activation", out=gate_t[gb][:, :], in_=pgt.ap[:, :], func=AF.Sigmoid),
                                   reads=[pgt], writes=[gate_b[gb]])
                            s.emit("dve", I("scalar_tensor_tensor", out=tmp_t[gb][:, :], in0=proj_t[:, d, :], scalar=gains_t[:, gpi, d:d + 1], in1=rstd_t[:, tsl], op0=ALU.mult, op1=ALU.mult),
                                   reads=[proj_b[d], rstd_b[tt]], writes=[tmp_b[gb]])
                            s.emit("dve", I("tensor_tensor", out=tmp_t[gb][:, :], in0=tmp_t[gb][:, :], in1=gate_t[gb][:, :], op=ALU.mult),
                                   reads=[tmp_b[gb], gate_b[gb]], writes=[tmp_b[gb]])
                            s.emit("dve", I("tensor_tensor", out=hres_t[:, d, tsl], in0=hres_t[:, d, tsl], in1=tmp_t[gb][:, :], op=ALU.add),
                                   reads=[tmp_b[gb], hres_b[d][tt]], writes=[hres_b[d][tt]])
            store_hres()
        s.barrier()


GAIN_NAMES = ["ffn1_norm", "mix_norm", "ffn2_norm", "ple_norm", "ple_post_norm"]
WEIGHT_SHAPES = (("ffn1_w_gu", [D, 2 * DFF], DEPTH), ("ffn1_w_down", [DFF, D], DEPTH), ("ffn2_w_gu", [D, 2 * DFF], DEPTH),
                 ("ffn2_w_down", [DFF, D], DEPTH), ("ple_w_gate", [D, D], DEPTH), ("ple_w_proj", [DPLE, D], DEPTH),
                 ("na_w_qkv", [D, 3 * D], 2), ("na_w_out", [D, D], 2), ("na_bias_g", [16, 128, 14, 256], 2),
                 ("ssd_w_in", [D, 8256], 2), ("ssd_w_out", [2048, D], 2), ("ssd_cw", [128, 48, 5], 2), ("ssd_cb", [128, 48], 2),
                 ("ssd_dtb", [64, 1], 2), ("ssd_alog", [64, 1], 2), ("ssd_dexp", [128, 16], 2), ("ssd_ng", [128, 16], 2))


def build_nc(plan, test_in=(), test_out=()):
    nc = bass.Bass("TRN2", target_bir_lowering=False)
    with contextlib.ExitStack() as stack:
        cx = Ctx(nc, stack)
        s = cx.s
        W = {}

        def scr(name, shape, dt):
            kind = "ExternalInput" if name in test_in else ("ExternalOutput" if name in test_out else "Internal")
            return nc.dram_tensor(name, list(shape), dt, kind=kind).ap()

        xT = nc.dram_tensor("xT", [D, SEQ], F32, kind="ExternalInput").ap()
        outT = nc.dram_tensor("outT", [D, SEQ], F32, kind="ExternalOutput").ap()
        W["pT"] = [nc.dram_tensor(f"pT{i}", [DPLE, SEQ], F32, kind="ExternalInput").ap() for i in range(DEPTH)]
        gains = nc.dram_tensor("gains", [128, len(GAIN_NAMES) * DEPTH, KD], F32, kind="ExternalInput").ap()
        ident_d = nc.dram_tensor("ident", [128, 128], F32, kind="ExternalInput").ap()
        nag_d = nc.dram_tensor("na_gain", [128, 2, 2], F32, kind="ExternalInput").ap()
        for nm, shp, n in WEIGHT_SHAPES:
            W[nm] = [nc.dram_tensor(f"{nm}{i}", shp, F32, kind="ExternalInput").ap() for i in range(n)]
        W["gain_idx"] = {(nm, l): gi * DEPTH + l for gi, nm in enumerate(GAIN_NAMES) for l in range(DEPTH)}
        W["xnT"] = scr("xnT", [D, SEQ], BF16)
        W["mixT"] = scr("mixT", [2 * D, SEQ], BF16)
        ar = Arena(SBUF_BASE)
        ones_t = cx.sb(ar, [128, 128], BF16, "ones")
        NG = len(GAIN_NAMES) * DEPTH
        gains_t = cx.sb(ar, [128, NG, KD], F32, "gains")
        W["ones_t"], W["ones_b"] = ones_t, Buf(None)
        W["gains_t"], W["gains_b"] = gains_t, Buf(None)
        ds_c = s.new_dsem("dconst")
        cb = W["ones_b"]
        s.emit("dve", I("memset", ones_t[:, :], 1.0 / D), writes=[cb])
        cx.eps_t = cx.sb(ar, [128, 1], F32, "eps")
        s.emit("dve", I("memset", cx.eps_t[:, :], EPS), writes=[cb])
        W["eps64_t"] = cx.sb(ar, [128, 1], F32, "eps64")
        s.emit("dve", I("memset", W["eps64_t"][:, :], 64 * EPS), writes=[cb])
        W["ident_t"], W["ident_b"] = cx.sb(ar, [128, 128], BF16, "ident"), cb
        W["bd1_t"] = cx.sb(ar, [128, 128], BF16, "bd1")
        W["bd64_t"] = cx.sb(ar, [128, 128], BF16, "bd64")
        W["bd_b"] = cb
        for t, v in ((W["bd1_t"], 1.0), (W["bd64_t"], 1.0 / 64)):
            s.emit("dve", I("memset", t[:, :], 0.0), writes=[cb])
            s.emit("dve", I("memset", t[0:64, 0:64], v), writes=[cb])
            s.emit("dve", I("memset", t[64:128, 64:128], v), writes=[cb])
        W["nag_t"], W["nag_b"] = cx.sb(ar, [128, 2, 2], F32, "nag"), cb
        s.emit("sp", I("dma_start", out=gains_t[:, :, :], in_=gains), writes=[cb], dsem=ds_c)
        s.emit("sp", I("dma_start", out=W["nag_t"][:, :, :], in_=nag_d), writes=[cb], dsem=ds_c)
        ds_c2 = s.new_dsem("dconst2")
        s.emit("pool", I("dma_start", out=W["ident_t"][:, :], in_=ident_d), writes=[cb], dsem=ds_c2)
        identf_d = nc.dram_tensor("identf", [128, 128], F32, kind="ExternalInput").ap()
        ssdc_d = nc.dram_tensor("ssd_consts", [128, 1536], F32, kind="ExternalInput").ap()
        W["identf_t"] = cx.sb(ar, [128, 128], F32, "identf")
        W["ssdc_t"] = cx.sb(ar, [128, 1536], BF16, "ssdc")
        W["ones1_t"] = cx.sb(ar, [128, 128], BF16, "ones1")
        W["one_t"] = cx.sb(ar, [128, 1], F32, "one")
        s.emit("dve", I("memset", W["ones1_t"][:, :], 1.0), writes=[cb])
        s.emit("dve", I("memset", W["one_t"][:, :], 1.0), writes=[cb])
        s.emit("sp", I("dma_start", out=W["identf_t"][:, :], in_=identf_d), writes=[cb], dsem=ds_c)
        s.emit("pool", I("dma_start", out=W["ssdc_t"][:, :], in_=ssdc_d), writes=[cb], dsem=ds_c2)
        W["ones2k_t"] = cx.sb(ar, [128, 128], BF16, "ones2k")
        s.emit("dve", I("memset", W["ones2k_t"][:, :], 1.0 / 2048), writes=[cb])
        W["ng_t"] = [cx.sb(ar, [128, 16], F32, "ssdng") for _ in range(2)]
        W["ng_b"] = cb
        for l in range(2):
            s.emit("sp", I("dma_start", out=W["ng_t"][l][:, :], in_=W["ssd_ng"][l]), writes=[cb], dsem=ds_c)
        W["zsT"] = scr("zsT", [2048, SEQ], BF16)
        W["xcT"] = scr("xcT", [6144, SEQ], BF16)
        W["scr2_b"] = Buf(None)
        cx.const_top = ar.top
        hT_b = {(d, half): Buf(None) for d in range(KD) for half in range(2)}
        W["hT_b"] = hT_b
        W["scr_b"] = Buf(None)
        s.barrier()
        first = True
        for ph in plan:
            if ph[0] == "chain":
                _, halves, subs = ph
                subs2 = []
                for sub in subs:
                    if sub[0] == "normout":
                        subs2.append(("normout", sub[1], W["xnT"]))
                    elif sub[0] == "proj_na":
                        subs2.append(("proj", W["mixT"], W["na_w_out"][sub[1]], 0))
                    elif sub[0] == "proj_ssd":
                        subs2.append(("proj_ssd", W["mixT"], W["ssd_w_out"][sub[1]], W["ng_t"][sub[1]]))
                    else:
                        subs2.append(sub)
                build_chain_phase(cx, xT if first else outT, outT, halves, subs2, W)
                first = False
            elif ph[0] == "na":
                build_na_phase(cx, W["xnT"], W["mixT"], ph[1], W)
            elif ph[0] == "ssd":
                build_ssd_phase(cx, W["xnT"], W["mixT"], W["zsT"], W["xcT"], ph[1], W)
        s.barrier()
        s.replay()
    return nc


NA_NE = 14


def na_tables():
    idx = np.zeros((128, NA_NE, 256), dtype=np.int64)
    PAD = 15 * 31
    ents = [(4, 4 - 4 + 2 * j) for j in range(6)] + [(0, 2 * j) for j in range(4)] + [(60, 56 + 2 * j) for j in range(4)]
    for e, (rb, a0) in enumerate(ents):
        for jrow in range(2):
            a = a0 + jrow
            for qr in range(4):
                r = rb + qr
                r0 = min(max(r - 4, 0), 56)
                vrow = (r0 <= a <= r0 + 7)
                rr = a - r + 7
                for c in range(64):
                    wc0 = min(max(c - 8, 0), 48)
                    for kc in range(64):
                        ok = vrow and (wc0 <= kc < wc0 + 16)
                        cr = kc - c + 15
                        idx[jrow * 64 + kc, e, qr * 64 + c] = (rr * 31 + cr) if ok else PAD
    return idx


def build_na_phase(cx, xnT, oT, j, W):
    s = cx.s
    st = Arena(cx.const_top)
    ps_t = cx.ps_t
    ps7_bf = ps_t[:, 7 * 512:8 * 512].bitcast(BF16)
    xn_t = cx.sb(st, [128, KD, SEQ], BF16, "na_xn")
    w_t = [cx.sb(st, [128, KD, 3, 128], BF16, "na_w") for _ in range(2)]
    qT_t = [cx.sb(st, [128, SEQ], BF16, "na_qT") for _ in range(2)]
    kT_t = [cx.sb(st, [128, SEQ], BF16, "na_kT") for _ in range(2)]
    vx_t = [cx.sb(st, [128, 32, 2, 66], BF16, "na_vx") for _ in range(2)]
    bias_t = [cx.sb(st, [128, NA_NE, 256], BF16, "na_bias") for _ in range(2)]
    P_t = [cx.sb(st, [128, 6 * 256], BF16, "na_P") for _ in range(2)]
    otok_t = cx.sb(st, [128, 32, 128], BF16, "na_otok")
    oT_t = [cx.sb(st, [128, SEQ], BF16, "na_oT") for _ in range(2)]
    qsb_t = [cx.sb(st, [128, 512], F32, "na_qsb") for _ in range(2)]
    sq_t = [cx.sb(st, [128, 512], BF16, "na_sq") for _ in range(2)]
    rs_t = [cx.sb(st, [128, 512], F32, "na_rs") for _ in range(2)]
    rec_t = [cx.sb(st, [128, 2], F32, "na_rec") for _ in range(2)]

    xn_b = [[Buf(None) for _ in range(8)] for _ in range(KD)]
    w_b = [Buf(None) for _ in range(2)]
    qT_b = [[Buf(None) for _ in range(8)] for _ in range(2)]
    kT_b = [[Buf(None) for _ in range(8)] for _ in range(2)]
    vx_b = [[Buf(None) for _ in range(8)] for _ in range(2)]
    vx1_b = [Buf(None) for _ in range(2)]
    bias_b = [Buf(None) for _ in range(2)]
    P_b = [Buf(None) for _ in range(2)]
    otok_b = [Buf(None) for _ in range(8)]
    oT_b = [Buf(None) for _ in range(2)]
    qsb_b = [Buf(None) for _ in range(2)]
    sq_b = [Buf(None) for _ in range(2)]
    rs_b = [Buf(None) for _ in range(2)]
    rec_b = [Buf(None) for _ in range(2)]
    O_b = [cx.psum[6], cx.psum[7]]
    ident_t, ident_b = W["ident_t"], W["ident_b"]
    bd1_t, bd64_t, bd_b = W["bd1_t"], W["bd64_t"], W["bd_b"]
    nag_t, nag_b = W["nag_t"], W["nag_b"]
    eps64_t = W["eps64_t"]

    ds_x = s.new_dsem("na_dx")
    ds_w = [s.new_dsem(f"na_dw{i}") for i in range(2)]
    ds_b = [s.new_dsem(f"na_db{i}") for i in range(2)]
    ds_o = [s.new_dsem(f"na_do{i}") for i in range(2)]

    xn_v = xnT.rearrange("(kc p) t -> p kc t", p=128)
    for k in range(KD):
        s.emit("sp", I("dma_start", out=xn_t[:, k, :], in_=xnT[k * 128:(k + 1) * 128, :]),
               reads=[W["scr_b"]], writes=xn_b[k], dsem=ds_x)
    for bi in range(2):
        s.emit("pool", I("memset", vx_t[bi][:, :, :, 64:66], 1.0), writes=[vx1_b[bi]])
    w_v = W["na_w_qkv"][j].rearrange("(kc p) c -> p kc c", p=128)
    bias_g = W["na_bias_g"][j]

    for c in range(KD):
        bi = c % 2
        for wi in range(3):
            s.emit("pool", I("dma_start", out=w_t[bi][:, :, wi, :], in_=w_v[:, :, wi * D + c * 128: wi * D + (c + 1) * 128]),
                   writes=[w_b[bi]], dsem=ds_w[bi])
        units = [(tt, wi) for tt in range(8) for wi in range(2)]
        bank_rr = [0]

        def proj(u):
            tt, wi = units[u]
            pb = cx.psum[bank_rr[0] % 6]
            bank_rr[0] += 1
            tsl = slice(tt * 512, (tt + 1) * 512)
            for k in range(KD):
                s.emit("pe", I("matmul", pb.ap[:, :], lhsT=w_t[bi][:, k, wi, :], rhs=xn_t[:, k, tsl], start=(k == 0), stop=(k == KD - 1)),
                       reads=[w_b[bi], xn_b[k][tt]], writes=[pb])
            x = u % 2
            s.emit("act", I("activation", out=sq_t[x][:, :], in_=pb.ap[:, :], func=AF.Square), reads=[pb], writes=[sq_b[x]])
            s.emit("act", I("activation", out=qsb_t[x][:, :], in_=pb.ap[:, :], func=AF.Copy), reads=[pb], writes=[qsb_b[x]])

        def norm(u):
            tt, wi = units[u]
            x = u % 2
            tsl = slice(tt * 512, (tt + 1) * 512)
            p7 = cx.psum[7]
            bd = bd1_t if wi == 0 else bd64_t
            ept = eps64_t if wi == 0 else cx.eps_t
            s.emit("pe", I("matmul", p7.ap[:, :], lhsT=bd[:, :], rhs=sq_t[x][:, :], start=True, stop=True), reads=[bd_b, sq_b[x]], writes=[p7])
            s.emit("act", I("activation", out=rs_t[x][:, :], in_=p7.ap[:, :], func=AF.Ln, bias=ept[:, 0:1], scale=1.0), reads=[p7], writes=[rs_b[x]])
            s.emit("act", I("activation", out=rs_t[x][:, :], in_=rs_t[x][:, :], func=AF.Exp, scale=-0.5), reads=[rs_b[x]], writes=[rs_b[x]])
            dst_t = qT_t if wi == 0 else kT_t
            dst_b = qT_b if wi == 0 else kT_b
            s.emit("dve", I("scalar_tensor_tensor", out=dst_t[bi][:, tsl], in0=qsb_t[x][:, :], scalar=nag_t[:, j, wi:wi + 1], in1=rs_t[x][:, :], op0=ALU.mult, op1=ALU.mult),
                   reads=[qsb_b[x], rs_b[x], nag_b], writes=[dst_b[bi][tt]])

        proj(0)
        for u in range(1, len(units)):
            proj(u)
            norm(u - 1)
        norm(len(units) - 1)
        for tg in range(8):
            pb = cx.psum[bank_rr[0] % 6]
            bank_rr[0] += 1
            for t4 in range(4):
                tile = tg * 4 + t4
                for k in range(KD):
                    s.emit("pe", I("matmul", pb.ap[:, t4 * 128:(t4 + 1) * 128], lhsT=xn_t[:, k, tile * 128:(tile + 1) * 128], rhs=w_t[bi][:, k, 2, :], start=(k == 0), stop=(k == KD - 1)),
                           reads=[w_b[bi], xn_b[k][tile // 4]], writes=[pb])
            s.emit("act", I("activation", out=vx_t[bi][:, tg * 4:(tg + 1) * 4, :, 0:64], in_=pb.ap[:, :].rearrange("p (t h d) -> p t h d", t=4, h=2), func=AF.Copy),
                   reads=[pb], writes=[vx_b[bi][tg]])

        iters = []
        for hh in range(2):
            for b in range(16):
                iters.append((hh, b))

        def geom(b):
            rb = 4 * b
            if b == 0:
                return rb, [(6 + jj, 2 * jj) for jj in range(4)]
            if b == 15:
                return rb, [(10 + jj, 56 + 2 * jj) for jj in range(4)]
            return rb, [(jj, rb - 4 + 2 * jj) for jj in range(6)]

        def S_stage(it):
            hh, b = iters[it]
            g = it % 2
            h = 2 * c + hh
            hb = h % 2
            if b == 0:
                s.emit("pool", I("dma_start", out=bias_t[hb][:, :, :], in_=bias_g[h]), writes=[bias_b[hb]], dsem=ds_b[hb])
            rb, tiles = geom(b)
            q0 = rb * 64
            for par2 in range(2):
                sel = [(jj, te) for jj, te in enumerate(tiles) if jj % 2 == par2]
                for jj, (ent, a0) in sel:
                    bank = cx.psum[3 * g + jj // 2]
                    reg = ps_t[:, 3 * g * 512 + jj * 256: 3 * g * 512 + (jj + 1) * 256]
                    k0 = a0 * 64
                    s.emit("pe", I("matmul", reg, lhsT=kT_t[bi][hh * 64:(hh + 1) * 64, k0:k0 + 128], rhs=qT_t[bi][hh * 64:(hh + 1) * 64, q0:q0 + 256], start=True, stop=False),
                           reads=[kT_b[bi][k0 // 512], kT_b[bi][(k0 + 127) // 512], qT_b[bi][q0 // 512]], writes=[bank])
                for jj, (ent, a0) in sel:
                    bank = cx.psum[3 * g + jj // 2]
                    reg = ps_t[:, 3 * g * 512 + jj * 256: 3 * g * 512 + (jj + 1) * 256]
                    s.emit("pe", I("matmul", reg, lhsT=ident_t[:, :], rhs=bias_t[hb][:, ent, :], start=False, stop=True),
                           reads=[bias_b[hb], ident_b], writes=[bank])
            nt = len(tiles)
            banks = [cx.psum[3 * g + x] for x in range((nt + 1) // 2)]
            s.emit("act", I("activation", out=P_t[g][:, 0:nt * 256], in_=ps_t[:, 3 * g * 512: 3 * g * 512 + nt * 256], func=AF.Exp),
                   reads=banks, writes=[P_b[g]])

        def PV_stage(it):
            hh, b = iters[it]
            g = it % 2
            rb, tiles = geom(b)
            nt = len(tiles)
            obase = (6 + g) * 512
            for t in range(2):
                oreg = ps_t[:, obase + t * 66: obase + t * 66 + 65]
                for jj, (ent, a0) in enumerate(tiles):
                    vt = a0 // 2
                    s.emit("pe", I("matmul", oreg, lhsT=P_t[g][:, jj * 256 + t * 128: jj * 256 + (t + 1) * 128], rhs=vx_t[bi][:, vt, hh, 0:65], start=(jj == 0), stop=(jj == nt - 1)),
                           reads=[P_b[g], vx_b[bi][vt // 4], vx1_b[bi]], writes=[O_b[g]])
            s.emit("dve", I("reciprocal", out=rec_t[g][:, :], in_=ps_t[:, obase:obase + 132].rearrange("p (t d) -> p t d", t=2)[:, :, 64]),
                   reads=[O_b[g]], writes=[rec_b[g]])
            for t in range(2):
                tile = rb // 2 + t
                s.emit("dve", I("tensor_scalar", out=otok_t[:, tile, hh * 64:(hh + 1) * 64], in0=ps_t[:, obase + t * 66: obase + t * 66 + 64], scalar1=rec_t[g][:, t:t + 1], scalar2=None, op0=ALU.mult),
                       reads=[O_b[g], rec_b[g]], writes=[otok_b[tile // 4]])

        S_stage(0)
        for it in range(1, len(iters)):
            S_stage(it)
            PV_stage(it - 1)
        PV_stage(len(iters) - 1)

        for tg in range(8):
            p7 = cx.psum[7]
            for t4 in range(4):
                tile = tg * 4 + t4
                s.emit("pe", I("transpose", out=ps7_bf[:, t4 * 128:(t4 + 1) * 128], in_=otok_t[:, tile, :], identity=ident_t[:, :]),
                       reads=[otok_b[tg], ident_b], writes=[p7])
            s.emit("act", I("activation", out=oT_t[bi][:, tg * 512:(tg + 1) * 512], in_=ps7_bf[:, 0:512], func=AF.Copy),
                   reads=[p7], writes=[oT_b[bi]])
        s.emit("sp", I("dma_start", out=oT[c * 128:(c + 1) * 128, :], in_=oT_t[bi][:, :]),
               reads=[oT_b[bi]], writes=[W["scr_b"]], dsem=ds_o[bi])
    s.barrier()


_NA_IDX = None


def host_bias_g(rpb):
    global _NA_IDX
    if _NA_IDX is None:
        _NA_IDX = na_tables()
    flat = np.concatenate([rpb.reshape(16, -1).astype(np.float32), np.full((16, 1), -30000.0, np.float32)], axis=1)
    return np.ascontiguousarray(flat[:, _NA_IDX])


def host_na_gain(qs, ks):
    out = np.zeros((128, 2, 2), np.float32)
    for l in range(2):
        out[:, l, 0] = np.tile(qs[l], 2)
        out[:, l, 1] = np.tile(ks[l], 2)
    return out


def dummy_inputs():
    ins = {"xT": np.zeros((D, SEQ), np.float32), "gains": np.ones((128, 20, KD), np.float32),
           "ident": np.eye(128, dtype=np.float32), "na_gain": np.ones((128, 2, 2), np.float32),
           "identf": np.eye(128, dtype=np.float32), "ssd_consts": host_ssd_consts()}
    for i in range(DEPTH):
        ins[f"pT{i}"] = np.zeros((DPLE, SEQ), np.float32)
    for nm, shp, n in WEIGHT_SHAPES:
        for i in range(n):
            ins[f"{nm}{i}"] = np.zeros(shp, np.float32)
    return ins


DI = 2048
NQ = 6
SSD_IN = 8256


def build_ssd_phase(cx, xnT, uT, zsT, xcT, j, W):
    s = cx.s
    ps_t = cx.ps_t
    base = Arena(cx.const_top)
    sc_t = cx.sb(base, [128, 32, NQ, 64], F32, "ssd_sc")
    sc_b = Buf(None)
    identf_t = W["identf_t"]
    ident_t, ident_b = W["ident_t"], W["ident_b"]
    cb = ident_b
    w_in = W["ssd_w_in"][j]
    w_v = w_in.rearrange("(kc p) c -> p kc c", p=128)
    st = Arena(base.top)
    xn_t = cx.sb(st, [128, KD, SEQ], BF16, "sa_xn")
    xn_b = [[Buf(None) for _ in range(8)] for _ in range(KD)]
    w_t = [cx.sb(st, [128, KD, 128], BF16, "sa_w") for _ in range(2)]
    w_b = [Buf(None) for _ in range(2)]
    cw_t = cx.sb(st, [128, 48, 5], F32, "sa_cw")
    cbias_t = cx.sb(st, [128, 48], F32, "sa_cb")
    dtb_t = cx.sb(st, [64, 1], F32, "sa_dtb")
    a_t = cx.sb(st, [64, 1], F32, "sa_a")
    prm_b = Buf(None)
    ds_x = s.new_dsem("sa_dx")
    ds_w = [s.new_dsem(f"sa_dw{i}") for i in range(2)]
    ds_p = s.new_dsem("sa_dp")
    ds_o = [s.new_dsem(f"sa_do{i}") for i in range(2)]
    for k in range(KD):
        s.emit("sp", I("dma_start", out=xn_t[:, k, :], in_=xnT[k * 128:(k + 1) * 128, :]),
               reads=[W["scr_b"]], writes=xn_b[k], dsem=ds_x)
    s.emit("sp", I("dma_start", out=cw_t[:, :, :], in_=W["ssd_cw"][j]), writes=[prm_b], dsem=ds_p)
    s.emit("sp", I("dma_start", out=cbias_t[:, :], in_=W["ssd_cb"][j]), writes=[prm_b], dsem=ds_p)
    s.emit("sp", I("dma_start", out=dtb_t[:, :], in_=W["ssd_dtb"][j]), writes=[prm_b], dsem=ds_p)
    s.emit("sp", I("dma_start", out=a_t[:, :], in_=W["ssd_alog"][j]), writes=[prm_b], dsem=ds_p)
    s.emit("act", I("activation", out=a_t[:, :], in_=a_t[:, :], func=AF.Exp), reads=[prm_b], writes=[prm_b])
    s.emit("dve", I("tensor_scalar", out=a_t[:, :], in0=a_t[:, :], scalar1=-1.0, scalar2=None, op0=ALU.mult), reads=[prm_b], writes=[prm_b])

    s.emit("pool", I("dma_start", out=w_t[0][:, :, 0:64], in_=w_v[:, :, 8192:8256]), writes=[w_b[0]], dsem=ds_w[0])
    sa = Arena(st.top)
    HT = 1024
    names = ["dt", "dtar", "cum", "X", "tmp", "dte", "E", "cd", "dtd", "negX", "mask", "e1"]
    F = {n: cx.sb(sa, [64, HT], F32, "sf_" + n) for n in names}
    dtab_t = cx.sb(sa, [64, HT], BF16, "sf_dtab")
    fb = {n: Buf(None) for n in names + ["dtab"]}
    s.emit("pool", I("memset", F["mask"][:, :], 1.0), writes=[fb["mask"]])
    s.emit("pool", I("memset", F["mask"][:, :].rearrange("p (c q) -> p c q", q=128)[:, :, 0:1], 0.0), writes=[fb["mask"]])
    for half in range(SEQ // HT):
        for t4 in range(HT // 512):
            tt = half * (HT // 512) + t4
            pb = cx.psum[tt % 4]
            tsl = slice(tt * 512, (tt + 1) * 512)
            lsl = slice(t4 * 512, (t4 + 1) * 512)
            for k in range(KD):
                s.emit("pe", I("matmul", pb.ap[0:64, :], lhsT=w_t[0][:, k, 0:64], rhs=xn_t[:, k, tsl], start=(k == 0), stop=(k == KD - 1)),
                       reads=[w_b[0], xn_b[k][tt]], writes=[pb])
            s.emit("act", I("activation", out=F["e1"][:, lsl], in_=pb.ap[0:64, :], func=AF.Exp, bias=dtb_t[:, 0:1], scale=1.0),
                   reads=[pb, prm_b], writes=[fb["e1"]])
        s.emit("act", I("activation", out=F["dt"][:, :], in_=F["e1"][:, :], func=AF.Ln, bias=W["one_t"][0:64, 0:1], scale=1.0),
               reads=[fb["e1"]], writes=[fb["dt"]])
        s.emit("dve", I("tensor_scalar", out=dtab_t[:, :], in0=F["dt"][:, :], scalar1=a_t[:, 0:1], scalar2=None, op0=ALU.mult),
               reads=[fb["dt"], prm_b], writes=[fb["dtab"]])
        s.emit("dve", I("tensor_copy", out=F["dtar"][:, :], in_=dtab_t[:, :]), reads=[fb["dtab"]], writes=[fb["dtar"]])
        s.emit("dve", I("tensor_tensor_scan", out=F["cum"][:, :], data0=F["mask"][:, :], data1=F["dtar"][:, :], initial=0.0, op0=ALU.mult, op1=ALU.add),
               reads=[fb["mask"], fb["dtar"]], writes=[fb["cum"]])
        cumv = F["cum"][:, :].rearrange("p (c q) -> p c q", q=128)
        totbc = cumv[:, :, 127:128].to_broadcast([64, HT // 128, 128])
        v3 = lambda n, lo=0, hi=64: F[n][lo:hi, :].rearrange("p (c q) -> p c q", q=128)
        s.emit("dve", I("tensor_copy", out=F["X"][0:32, :], in_=F["cum"][0:32, :]), reads=[fb["cum"]], writes=[fb["X"]])
        totbc_b = cumv[32:64, :, 127:128].to_broadcast([32, HT // 128, 128])
        s.emit("dve", I("tensor_tensor", out=v3("tmp", 32, 64), in0=totbc_b, in1=v3("cum", 32, 64), op=ALU.subtract), reads=[fb["cum"]], writes=[fb["tmp"]])
        s.emit("dve", I("tensor_tensor", out=F["X"][32:64, :], in0=F["tmp"][32:64, :], in1=F["dtar"][32:64, :], op=ALU.add), reads=[fb["tmp"], fb["dtar"]], writes=[fb["X"]])
        s.emit("dve", I("tensor_scalar", out=F["negX"][:, :], in0=F["X"][:, :], scalar1=-1.0, scalar2=None, op0=ALU.mult), reads=[fb["X"]], writes=[fb["negX"]])
        s.emit("act", I("activation", out=F["E"][:, :], in_=F["X"][:, :], func=AF.Exp), reads=[fb["X"]], writes=[fb["E"]])
        s.emit("dve", I("tensor_tensor", out=v3("tmp"), in0=totbc, in1=v3("X"), op=ALU.subtract), reads=[fb["cum"], fb["X"]], writes=[fb["tmp"]])
        s.emit("act", I("activation", out=F["dte"][:, :], in_=F["tmp"][:, :], func=AF.Exp), reads=[fb["tmp"]], writes=[fb["dte"]])
        s.emit("act", I("activation", out=v3("cd"), in_=totbc, func=AF.Exp), reads=[fb["cum"]], writes=[fb["cd"]])
        s.emit("dve", I("tensor_tensor", out=F["dtd"][:, :], in0=F["dt"][:, :], in1=F["dte"][:, :], op=ALU.mult), reads=[fb["dt"], fb["dte"]], writes=[fb["dtd"]])
        qn = ["dt", "dtd", "dtar", "negX", "E", "cd"]
        for cl in range(HT // 128):
            c = half * (HT // 128) + cl
            pb = cx.psum[4 + cl % 2]
            for qi, n in enumerate(qn):
                s.emit("pe", I("transpose", out=pb.ap[:, qi * 64:(qi + 1) * 64], in_=F[n][:, cl * 128:(cl + 1) * 128], identity=identf_t[0:64, 0:64]),
                       reads=[fb[n], cb], writes=[pb])
            s.emit("act", I("activation", out=sc_t[:, c, :, :], in_=pb.ap[:, 0:NQ * 64].rearrange("p (q h) -> p q h", q=NQ), func=AF.Copy),
                   reads=[pb], writes=[sc_b])
    s.barrier()

    sa = Arena(st.top)
    xe_t = [cx.sb(sa, [128, SEQ + 4], F32, "sa_xe") for _ in range(2)]
    acc_t = [cx.sb(sa, [128, SEQ], F32, "sa_acc") for _ in range(2)]
    ob_t = [cx.sb(sa, [128, SEQ], BF16, "sa_ob") for _ in range(2)]
    xe_b = [[Buf(None) for _ in range(8)] for _ in range(2)]
    xepad_b = [Buf(None) for _ in range(2)]
    acc_b = [Buf(None) for _ in range(2)]
    ob_b = [Buf(None) for _ in range(2)]
    for i in range(2):
        s.emit("pool", I("memset", xe_t[i][:, 0:2], 0.0), writes=[xepad_b[i]])
        s.emit("pool", I("memset", xe_t[i][:, SEQ + 2:SEQ + 4], 0.0), writes=[xepad_b[i]])
    for m in range(64):
        bi = m % 2
        s.emit("pool", I("dma_start", out=w_t[bi][:, :, :], in_=w_v[:, :, m * 128:(m + 1) * 128]), writes=[w_b[bi]], dsem=ds_w[bi])
        for tt in range(8):
            pb = cx.psum[(m * 8 + tt) % 6]
            tsl = slice(tt * 512, (tt + 1) * 512)
            for k in range(KD):
                s.emit("pe", I("matmul", pb.ap[:, :], lhsT=w_t[bi][:, k, :], rhs=xn_t[:, k, tsl], start=(k == 0), stop=(k == KD - 1)),
                       reads=[w_b[bi], xn_b[k][tt]], writes=[pb])
            if m < 16:
                s.emit("act", I("activation", out=ob_t[bi][:, tsl], in_=pb.ap[:, :], func=AF.Silu), reads=[pb], writes=[ob_b[bi]])
            else:
                s.emit("act", I("activation", out=xe_t[bi][:, 2 + tt * 512: 2 + (tt + 1) * 512], in_=pb.ap[:, :], func=AF.Copy), reads=[pb], writes=[xe_b[bi][tt]])
        if m < 16:
            s.emit("sp", I("dma_start", out=zsT[m * 128:(m + 1) * 128, :], in_=ob_t[bi][:, :]), reads=[ob_b[bi]], writes=[W["scr2_b"]], dsem=ds_o[bi])
        else:
            cm = m - 16
            rd = xe_b[bi] + [xepad_b[bi], prm_b]
            s.emit("dve", I("tensor_scalar", out=acc_t[bi][:, :], in0=xe_t[bi][:, 0:SEQ], scalar1=cw_t[:, cm, 0:1], scalar2=cbias_t[:, cm:cm + 1], op0=ALU.mult, op1=ALU.add),
                   reads=rd, writes=[acc_b[bi]])
            for kk in (1, 2, 3, 4):
                s.emit("dve", I("scalar_tensor_tensor", out=acc_t[bi][:, :], in0=xe_t[bi][:, kk:kk + SEQ], scalar=cw_t[:, cm, kk:kk + 1], in1=acc_t[bi][:, :], op0=ALU.mult, op1=ALU.add),
                       reads=rd + [acc_b[bi]], writes=[acc_b[bi]])
            s.emit("act", I("activation", out=ob_t[bi][:, :], in_=acc_t[bi][:, :], func=AF.Silu), reads=[acc_b[bi]], writes=[ob_b[bi]])
            s.emit("sp", I("dma_start", out=xcT[cm * 128:(cm + 1) * 128, :], in_=ob_t[bi][:, :]), reads=[ob_b[bi]], writes=[W["scr2_b"]], dsem=ds_o[bi])
    s.barrier()
    build_ssd_scan(cx, base, sc_t, uT, zsT, xcT, j, W)


def build_ssd_scan(cx, base, sc_t, uT, zsT, xcT, j, W):
    s = cx.s
    ps_t = cx.ps_t
    st = Arena(base.top)
    ident_t, cb = W["ident_t"], W["ident_b"]
    identf_t = W["identf_t"]
    ones1_t = W["ones1_t"]
    U_t = [W["ssdc_t"][:, 0:128], W["ssdc_t"][:, 128:256]]
    M_t = [W["ssdc_t"][:, 256:768], W["ssdc_t"][:, 768:1280]]
    Ls_t = [W["ssdc_t"][:, 1280:1408], W["ssdc_t"][:, 1408:1536]]
    tmpS = [cx.sb(st, [128, 256], F32, "sb_tmpS") for _ in range(2)]
    tmpS_b = [Buf(None) for _ in range(2)]
    BT = [cx.sb(st, [128, SEQ], BF16, "sb_BT") for _ in range(2)]
    CT = [cx.sb(st, [128, SEQ], BF16, "sb_CT") for _ in range(2)]
    Btok = [cx.sb(st, [128, 32, 128], BF16, "sb_Btok") for _ in range(2)]
    xs_tok = cx.sb(st, [128, 32, 256], BF16, "sb_xstok")
    y_tok = cx.sb(st, [128, 32, 256], F32, "sb_ytok")
    S32 = [cx.sb(st, [128, 256], F32, "sb_S32") for _ in range(2)]
    Sbf = [cx.sb(st, [128, 256], BF16, "sb_Sbf") for _ in range(2)]
    rhs1 = [cx.sb(st, [128, 512], BF16, "sb_rhs1") for _ in range(2)]
    LT = [cx.sb(st, [128, 512], F32, "sb_LT") for _ in range(2)]
    STt = [cx.sb(st, [128, 512], BF16, "sb_ST") for _ in range(2)]
    xdt2 = [cx.sb(st, [128, 2, 4, 64], BF16, "sb_xdt2") for _ in range(2)]
    xdt = [t[:, 0, :, :].rearrange("p h e -> p (h e)") for t in xdt2]
    xdtd = [t[:, 1, :, :].rearrange("p h e -> p (h e)") for t in xdt2]
    yo = [cx.sb(st, [128, 256], F32, "sb_yo") for _ in range(2)]
    xq = [cx.sb(st, [128, 2, 512], BF16, "sb_xq") for _ in range(2)]
    zq = [cx.sb(st, [128, 2, 512], BF16, "sb_zq") for _ in range(2)]
    tq = [cx.sb(st, [128, 512], F32, "sb_tq") for _ in range(2)]
    uT_t = cx.sb(st, [128, 2, SEQ], BF16, "sb_uT")
    dexp_t = cx.sb(st, [128, 16], F32, "sb_dexp")

    BT_b = [Buf(None) for _ in range(2)]
    CT_b = [Buf(None) for _ in range(2)]
    Btok_b = [[Buf(None) for _ in range(8)] for _ in range(2)]
    xs_b = [Buf(None) for _ in range(8)]
    y_b = [Buf(None) for _ in range(32)]
    S32_b = [Buf(None) for _ in range(2)]
    Sbf_b = [Buf(None) for _ in range(2)]
    rhs1_b = [Buf(None) for _ in range(2)]
    LT_b = [Buf(None) for _ in range(2)]
    ST_b = [Buf(None) for _ in range(2)]
    xdt_b = [Buf(None) for _ in range(2)]
    xdtd_b = [Buf(None) for _ in range(2)]
    yo_b = [Buf(None) for _ in range(2)]
    xq_b = [Buf(None) for _ in range(2)]
    zq_b = [Buf(None) for _ in range(2)]
    tq_b = [Buf(None) for _ in range(2)]
    uT_b = [Buf(None) for _ in range(2)]
    dexp_b = Buf(None)
    sc_b = Buf(None)
    ds_g = [s.new_dsem(f"sb_dg{i}") for i in range(4)]
    ds_q = [s.new_dsem(f"sb_dq{i}") for i in range(2)]
    ds_u = s.new_dsem("sb_du")
    ds_d = s.new_dsem("sb_dd")
    s.emit("sp", I("dma_start", out=dexp_t[:, :], in_=W["ssd_dexp"][j]), writes=[dexp_b], dsem=ds_d)
    bf = lambda bank: ps_t[:, bank * 512:(bank + 1) * 512].bitcast(BF16)

    def load_bc(g):
        for d in range(2):
            r0 = 2048 + d * 2048 + g * 128
            s.emit("sp", I("dma_start", out=BT[d][:, :], in_=xcT[r0:r0 + 128, :]), reads=[W["scr2_b"]], writes=[BT_b[d]], dsem=ds_g[d * 2])
            s.emit("sp", I("dma_start", out=CT[d][:, :], in_=xcT[r0 + 1024:r0 + 1024 + 128, :]), reads=[W["scr2_b"]], writes=[CT_b[d]], dsem=ds_g[d * 2 + 1])

    load_bc(0)
    for g in range(8):
        for d in range(2):
            for c4 in range(8):
                bank = 6 + c4 % 2
                for t in range(4):
                    c = c4 * 4 + t
                    s.emit("pe", I("transpose", out=bf(bank)[:, t * 128:(t + 1) * 128], in_=BT[d][:, c * 128:(c + 1) * 128], identity=ident_t[:, :]),
                           reads=[BT_b[d], cb], writes=[cx.psum[bank]])
                s.emit("act", I("activation", out=Btok[d][:, c4 * 4:(c4 + 1) * 4, :], in_=bf(bank)[:, 0:512].rearrange("p (t n) -> p t n", t=4), func=AF.Copy),
                       reads=[cx.psum[bank]], writes=[Btok_b[d][c4]])
        for quad in range(8):
            qb = quad % 2
            for cc in range(2):
                r0 = g * 256 + cc * 128
                s.emit("sp", I("dma_start", out=xq[qb][:, cc, :], in_=xcT[r0:r0 + 128, quad * 512:(quad + 1) * 512]), reads=[W["scr2_b"]], writes=[xq_b[qb]], dsem=ds_q[qb])
            bank = 6 + quad % 2
            for t in range(4):
                for cc in range(2):
                    s.emit("pe", I("transpose", out=bf(bank)[:, (t * 2 + cc) * 128:(t * 2 + cc + 1) * 128], in_=xq[qb][:, cc, t * 128:(t + 1) * 128], identity=ident_t[:, :]),
                           reads=[xq_b[qb], cb], writes=[cx.psum[bank]])
            s.emit("act", I("activation", out=xs_tok[:, quad * 4:(quad + 1) * 4, :], in_=bf(bank)[:, 0:1024].rearrange("p (t n) -> p t n", t=4), func=AF.Copy),
                   reads=[cx.psum[bank]], writes=[xs_b[quad]])
        for d in range(2):
            s.emit("dve", I("memset", S32[d][:, :], 0.0), writes=[S32_b[d]])
            s.emit("dve", I("memset", Sbf[d][:, :], 0.0), writes=[Sbf_b[d]])

        steps = [(i, d) for i in range(32) for d in range(2)]

        def pre(n):
            i, d = steps[n]
            c = i if d == 0 else 31 - i
            par = n % 2
            csl = slice(c * 128, (c + 1) * 128)
            h0 = d * 32 + g * 4
            v4 = lambda t: t[:, :].rearrange("p (h q) -> p h q", h=4)
            s.emit("pool", I("tensor_tensor", out=v4(rhs1[par]), in0=U_t[d].unsqueeze(1).to_broadcast([128, 4, 128]),
                             in1=sc_t[:, c, 2, h0:h0 + 4].unsqueeze(2).to_broadcast([128, 4, 128]), op=ALU.mult),
                   reads=[sc_b, cb], writes=[rhs1_b[par]])
            seg = cx.psum[par]
            s.emit("pe", I("matmul", seg.ap[:, :], lhsT=Ls_t[d], rhs=rhs1[par][:, :], start=True, stop=False), reads=[rhs1_b[par], cb], writes=[seg])
            s.emit("pe", I("matmul", seg.ap[:, :], lhsT=ident_t[:, :], rhs=M_t[d], start=False, stop=True), reads=[cb], writes=[seg])
            s.emit("act", I("activation", out=LT[par][:, :], in_=seg.ap[:, :], func=AF.Exp), reads=[seg], writes=[LT_b[par]])
            cbk = cx.psum[2 + par]
            s.emit("pe", I("matmul", cbk.ap[:, 0:128], lhsT=BT[d][:, csl], rhs=CT[d][:, csl], start=True, stop=True), reads=[BT_b[d], CT_b[d]], writes=[cbk])
            s.emit("dve", I("tensor_tensor", out=v4(STt[par]), in0=v4(LT[par]), in1=cbk.ap[:, 0:128].unsqueeze(1).to_broadcast([128, 4, 128]), op=ALU.mult),
                   reads=[LT_b[par], cbk], writes=[ST_b[par]])
            xv = xs_tok[:, c, :].rearrange("p (h e) -> p h e", h=4)
            v64 = lambda t: t[:, :].rearrange("p (h e) -> p h e", h=4)
            s.emit("pool", I("tensor_tensor", out=xdt2[par][:, :, :, :], in0=xv.unsqueeze(1).to_broadcast([128, 2, 4, 64]),
                             in1=sc_t[:, c, 0:2, h0:h0 + 4].unsqueeze(3).to_broadcast([128, 2, 4, 64]), op=ALU.mult),
                   reads=[xs_b[c // 4], sc_b], writes=[xdt_b[par], xdtd_b[par]])

        def post(n):
            i, d = steps[n]
            c = i if d == 0 else 31 - i
            par = n % 2
            csl = slice(c * 128, (c + 1) * 128)
            h0 = d * 32 + g * 4
            first = i < 16
            yb = cx.psum[4 + par]
            sb_ = cx.psum[6 + par]
            for h in range(4):
                s.emit("pe", I("matmul", yb.ap[:, h * 64:(h + 1) * 64], lhsT=STt[par][:, h * 128:(h + 1) * 128], rhs=xdt[par][:, h * 64:(h + 1) * 64], start=True, stop=True),
                       reads=[ST_b[par], xdt_b[par]], writes=[yb])
            s.emit("pe", I("matmul", yb.ap[:, 256:512], lhsT=CT[d][:, csl], rhs=Sbf[d][:, :], start=True, stop=True), reads=[CT_b[d], Sbf_b[d]], writes=[yb])
            s.emit("pe", I("matmul", sb_.ap[:, 0:256], lhsT=Btok[d][:, c, :], rhs=xdtd[par][:, :], start=True, stop=True), reads=[Btok_b[d][c // 4], xdtd_b[par]], writes=[sb_])
            v64 = lambda t: t.rearrange("p (h e) -> p h e", h=4)
            s.emit("dve", I("tensor_tensor", out=v64(yo[par][:, :]), in0=v64(yb.ap[:, 256:512]), in1=sc_t[:, c, 4, h0:h0 + 4].unsqueeze(2).to_broadcast([128, 4, 64]), op=ALU.mult),
                   reads=[yb, sc_b], writes=[yo_b[par]])
            if first:
                s.emit("dve", I("tensor_tensor", out=y_tok[:, c, :], in0=yo[par][:, :], in1=yb.ap[:, 0:256], op=ALU.add), reads=[yo_b[par], yb], writes=[y_b[c]])
            else:
                s.emit("dve", I("tensor_tensor", out=yo[par][:, :], in0=yo[par][:, :], in1=yb.ap[:, 0:256], op=ALU.add), reads=[yo_b[par], yb], writes=[yo_b[par]])
                s.emit("dve", I("tensor_tensor", out=y_tok[:, c, :], in0=y_tok[:, c, :], in1=yo[par][:, :], op=ALU.add), reads=[yo_b[par], y_b[c]], writes=[y_b[c]])
            for h in range(4):
                hs = slice(h * 64, (h + 1) * 64)
                s.emit("act", I("activation", out=tmpS[par][:, hs], in_=S32[d][:, hs], func=AF.Copy, scale=sc_t[:, c, 5, h0 + h:h0 + h + 1]),
                       reads=[S32_b[d], sc_b], writes=[tmpS_b[par]])
            s.emit("dve", I("tensor_tensor", out=S32[d][:, :], in0=tmpS[par][:, :], in1=sb_.ap[:, 0:256], op=ALU.add),
                   reads=[tmpS_b[par], sb_], writes=[S32_b[d]])
            pending_sbf.append(d)

        pending_sbf = []

        def flush_sbf():
            while pending_sbf:
                d_ = pending_sbf.pop(0)
                s.emit("act", I("activation", out=Sbf[d_][:, :], in_=S32[d_][:, :], func=AF.Copy), reads=[S32_b[d_]], writes=[Sbf_b[d_]])

        pre(0)
        for n in range(1, len(steps)):
            pre(n)
            flush_sbf()
            post(n - 1)
        flush_sbf()
        post(len(steps) - 1)
        pending_sbf.clear()
        if g + 1 < 8:
            load_bc(g + 1)

        xfin = [xs_tok[:, 0:16, :].rearrange("p a b -> p (a b)"), xs_tok[:, 16:32, :].rearrange("p a b -> p (a b)")]
        zfin = [Btok[0][:, :, :].rearrange("p a b -> p (a b)"), Btok[1][:, :, :].rearrange("p a b -> p (a b)")]
        xfin_b = [xs_b[0:4], xs_b[4:8]]
        zfin_b = [Btok_b[0], Btok_b[1]]
        for cc in range(2):
            r0 = g * 256 + cc * 128
            s.emit("sp", I("dma_start", out=xfin[cc], in_=xcT[r0:r0 + 128, :]), reads=[W["scr2_b"]], writes=xfin_b[cc], dsem=ds_q[0])
            s.emit("sp", I("dma_start", out=zfin[cc], in_=zsT[r0:r0 + 128, :]), reads=[W["scr2_b"]], writes=zfin_b[cc], dsem=ds_q[1])
        for quad in range(8):
            qb = quad % 2
            for cc in range(2):
                bank = cx.psum[(quad * 2 + cc) % 4]
                for t in range(4):
                    c = quad * 4 + t
                    s.emit("pe", I("transpose", out=bank.ap[:, t * 128:(t + 1) * 128], in_=y_tok[:, c, cc * 128:(cc + 1) * 128], identity=identf_t[:, :]),
                           reads=[y_b[c], cb], writes=[bank])
                tb = (quad * 2 + cc) % 2
                qsl = slice(quad * 512, (quad + 1) * 512)
                s.emit("dve", I("scalar_tensor_tensor", out=tq[tb][:, :], in0=xfin[cc][:, qsl], scalar=dexp_t[:, g * 2 + cc:g * 2 + cc + 1], in1=bank.ap[:, :], op0=ALU.mult, op1=ALU.add),
                       reads=xfin_b[cc] + [bank, dexp_b], writes=[tq_b[tb]])
                s.emit("pool", I("tensor_tensor", out=uT_t[:, cc, qsl], in0=tq[tb][:, :], in1=zfin[cc][:, qsl], op=ALU.mult),
                       reads=[tq_b[tb]] + zfin_b[cc], writes=[uT_b[cc]])
        for cc in range(2):
            r0 = g * 256 + cc * 128
            s.emit("sp", I("dma_start", out=uT[r0:r0 + 128, :], in_=uT_t[:, cc, :]), reads=[uT_b[cc]], writes=[W["scr_b"]], dsem=ds_u)
    s.barrier()


def host_ssd_consts():
    j = np.arange(128)[:, None]
    q = np.arange(128)[None, :]
    Uf = (j <= q).astype(np.float32)
    Ub = (j >= q).astype(np.float32)
    Mf = np.where(q >= j, 0.0, -30000.0).astype(np.float32)
    Mb = np.where(j >= q, 0.0, -30000.0).astype(np.float32)
    Lf = (j > q).astype(np.float32)
    Lb = (j < q).astype(np.float32)
    return np.ascontiguousarray(np.concatenate([Uf, Ub, np.tile(Mf, (1, 4)), np.tile(Mb, (1, 4)), Lf, Lb], axis=1))


def host_ssd_params(conv_w, conv_b, dt_bias, a_log, d_skip, norm_g):
    out = {}
    out["ssd_cw"] = np.ascontiguousarray(conv_w.reshape(5, 48, 128).transpose(2, 1, 0)).astype(np.float32)
    out["ssd_cb"] = np.ascontiguousarray(conv_b.reshape(48, 128).T).astype(np.float32)
    out["ssd_dtb"] = np.ascontiguousarray(dt_bias.reshape(64, 1)).astype(np.float32)
    out["ssd_alog"] = np.ascontiguousarray(a_log.reshape(64, 1)).astype(np.float32)
    out["ssd_dexp"] = np.ascontiguousarray(np.repeat(d_skip, 64).reshape(16, 128).T).astype(np.float32)
    out["ssd_ng"] = np.ascontiguousarray(norm_g.reshape(16, 128).T).astype(np.float32)
    return out


def full_plan():
    plan = [("chain", [0, 1], [("ffn", 0, 1), ("normout", 0)])]
    for i in range(DEPTH):
        jm = i // 2
        if i % 2 == 0:
            plan.append(("ssd", jm))
            head = [("proj_ssd", jm)]
        else:
            plan.append(("na", jm))
            head = [("proj_na", jm)]
        subs = head + [("ffn", i, 2), ("ple", i)]
        if i + 1 < DEPTH:
            subs += [("ffn", i + 1, 1), ("normout", i + 1)]
        plan.append(("chain", [0, 1], subs))
    return plan


_NC_CACHE = {}


def kernel(x, p, ffn1_norm, ffn1_w_gu, ffn1_w_down, mix_norm, ffn2_norm, ffn2_w_gu, ffn2_w_down,
           ple_norm, ple_w_gate, ple_w_proj, ple_post_norm,
           ssd_w_in, ssd_conv_w, ssd_conv_b, ssd_dt_bias, ssd_a_log, ssd_d, ssd_norm, ssd_w_out,
           na_w_qkv, na_q_norm, na_k_norm, na_rpb, na_w_out):
    f32 = lambda a: np.ascontiguousarray(np.asarray(a, dtype=np.float32))
    x = np.asarray(x, dtype=np.float32)
    p = np.asarray(p, dtype=np.float32)
    B = x.shape[0]
    shared = {}
    gl = {"ffn1_norm": ffn1_norm, "mix_norm": mix_norm, "ffn2_norm": ffn2_norm, "ple_norm": ple_norm, "ple_post_norm": ple_post_norm}
    gains = np.stack([np.asarray(gl[nm], np.float32)[l] for nm in GAIN_NAMES for l in range(DEPTH)])
    shared["gains"] = np.ascontiguousarray(gains.reshape(len(GAIN_NAMES) * DEPTH, KD, 128).transpose(2, 0, 1))
    shared["ident"] = np.eye(128, dtype=np.float32)
    shared["identf"] = np.eye(128, dtype=np.float32)
    shared["ssd_consts"] = host_ssd_consts()
    shared["na_gain"] = host_na_gain(np.asarray(na_q_norm, np.float32), np.asarray(na_k_norm, np.float32))
    wl = {"ffn1_w_gu": ffn1_w_gu, "ffn1_w_down": ffn1_w_down, "ffn2_w_gu": ffn2_w_gu, "ffn2_w_down": ffn2_w_down,
          "ple_w_gate": ple_w_gate, "ple_w_proj": ple_w_proj, "na_w_qkv": na_w_qkv, "na_w_out": na_w_out,
          "ssd_w_in": ssd_w_in, "ssd_w_out": ssd_w_out}
    for nm, arr in wl.items():
        arr = np.asarray(arr, np.float32)
        for i in range(arr.shape[0]):
            shared[f"{nm}{i}"] = f32(arr[i])
    rpb = np.asarray(na_rpb, np.float32)
    for l in range(2):
        shared[f"na_bias_g{l}"] = host_bias_g(rpb[l])
        sp = host_ssd_params(np.asarray(ssd_conv_w, np.float32)[l], np.asarray(ssd_conv_b, np.float32)[l],
                             np.asarray(ssd_dt_bias, np.float32)[l], np.asarray(ssd_a_log, np.float32)[l],
                             np.asarray(ssd_d, np.float32)[l], np.asarray(ssd_norm, np.float32)[l])
        for k, v in sp.items():
            shared[f"{k}{l}"] = v
    in_maps = []
    for b in range(B):
        m = dict(shared)
        m["xT"] = np.ascontiguousarray(x[b].T)
        for i in range(DEPTH):
            m[f"pT{i}"] = np.ascontiguousarray(p[i, b].T)
        in_maps.append(m)
    if "nc" not in _NC_CACHE:
        _NC_CACHE["nc"] = build_nc(full_plan())
    nc = _NC_CACHE["nc"]
    res = run_bass_kernel_spmd(nc, in_maps, core_ids=list(range(B)))
    out = np.stack([np.ascontiguousarray(np.asarray(r["outT"], dtype=np.float32).T) for r in res.results])
    return out
```
